# Optimizing a Trainium2 kernel written in Bass

```python
import jax, jax.numpy as jnp
from jax import lax
import numpy as np

D_MODEL = 1024
BATCH = 32
SEQ = 256
DEPTH = 2
DEC_BATCH = 2
DEC_SEQ = 1024
PAST_LEN = 512

GRID_W = 64
N_EVEN = (DEPTH + 1) // 2
N_ODD = DEPTH // 2
D_FF = 4 * D_MODEL
EPS = 1e-6
D_CONF = D_MODEL // 2
CONF_WIDTH = 31
D_SHORT = D_MODEL // 2
SHORT_WIDTH = 3
EVEN_SPLITS = (D_CONF, D_CONF, D_SHORT, D_SHORT, D_SHORT)
HEAD_DIM = 64
N_Q_HEADS = (D_MODEL // 2) // HEAD_DIM
N_KV_HEADS = 2
Q_PER_KV = N_Q_HEADS // N_KV_HEADS
D_ATT = N_Q_HEADS * HEAD_DIM
D_KV = N_KV_HEADS * HEAD_DIM
WINDOW = 128
BLOCK = 128
ROPE_BASE = 10000.0
N_M_HEADS = (D_MODEL // 2) // HEAD_DIM
D_MLSTM = N_M_HEADS * HEAD_DIM
N_GATES = 4
CHUNK = 128
FORGET_BIAS = 3.0
ODD_SPLITS = (D_ATT, D_KV, D_KV, D_MLSTM, D_MLSTM, D_MLSTM, D_MLSTM, N_GATES * N_M_HEADS)
D_IN_EVEN = sum(EVEN_SPLITS)
D_IN_ODD = sum(ODD_SPLITS)
ATT_SCALE = HEAD_DIM ** -0.5

kernel_name = 'hybrid_diffusion_prefix_step'


def split_cols(x, sizes):
    return jnp.split(x, [int(s) for s in np.cumsum(sizes)[:-1]], axis=-1)


def rmsnorm(x, g):
    xf = x.astype(jnp.float32)
    y = xf * lax.rsqrt(jnp.mean(xf * xf, axis=-1, keepdims=True) + EPS)
    return (y * g.astype(jnp.float32)).astype(x.dtype)


def layernorm(x, g, b):
    xf = x.astype(jnp.float32)
    mu = jnp.mean(xf, axis=-1, keepdims=True)
    var = jnp.mean(jnp.square(xf - mu), axis=-1, keepdims=True)
    y = (xf - mu) * lax.rsqrt(var + EPS) * g.astype(jnp.float32) + b.astype(jnp.float32)
    return y.astype(x.dtype)


def depthwise_conv(x, w):
    k = w.shape[0]
    return lax.conv_general_dilated(x, w[:, None, :].astype(x.dtype), (1,), [(k // 2, k // 2)],
                                    dimension_numbers=('NWC', 'WIO', 'NWC'),
                                    feature_group_count=x.shape[-1])


def modulation(cond, mod_w, mod_b):
    return [m[:, None, :] for m in jnp.split(jax.nn.silu(cond) @ mod_w + mod_b, 6, axis=-1)]


def modulate_in(x, g, shift, scale):
    return rmsnorm(x, g) * (1 + scale) + shift


def gated_out(x, h, g, gate):
    return x + gate * rmsnorm(h, g)


def sq_relu_mlp(h, w1, w2):
    return jnp.square(jax.nn.relu(h @ w1)) @ w2


def even_mixer(h, w_in, conv_a_w, conv_a_b, ln_g, ln_b, conv_b_w, w_out):
    a_val, a_gate, b_gate, c_gate, b_x = split_cols(h @ w_in, EVEN_SPLITS)
    a = a_val * jax.nn.sigmoid(a_gate)
    a = depthwise_conv(a, conv_a_w) + conv_a_b
    a = jax.nn.silu(layernorm(a, ln_g, ln_b))
    s = b_gate * depthwise_conv(c_gate * b_x, conv_b_w)
    return jnp.concatenate([a, s], axis=-1) @ w_out


def axial_rope_tables(t):
    rows = t // GRID_W
    row = jnp.broadcast_to(jnp.arange(rows)[:, None], (rows, GRID_W)).reshape(t).astype(jnp.float32)
    col = jnp.broadcast_to(jnp.arange(GRID_W)[None, :], (rows, GRID_W)).reshape(t).astype(jnp.float32)
    n_freq = HEAD_DIM // 4
    inv_freq = ROPE_BASE ** (-jnp.arange(n_freq, dtype=jnp.float32) / n_freq)
    ang = jnp.concatenate([row[:, None] * inv_freq, col[:, None] * inv_freq], axis=-1)
    return jnp.cos(ang), jnp.sin(ang)


def apply_rope(x, cos, sin):
    half = HEAD_DIM // 2
    xf = x.astype(jnp.float32)
    x1, x2 = xf[..., :half], xf[..., half:]
    c, s = cos[:, None, :], sin[:, None, :]
    return jnp.concatenate([x1 * c - x2 * s, x1 * s + x2 * c], axis=-1).astype(x.dtype)


def sink_column(sink, shape):
    s = sink.reshape(N_KV_HEADS, Q_PER_KV).astype(jnp.float32)[None, :, :, None, None]
    return jnp.broadcast_to(s, shape[:-1] + (1,))


def context_attention(q, k, v, sink):
    b, l = q.shape[:2]
    nb = l // BLOCK
    qb = q.reshape(b, nb, BLOCK, N_KV_HEADS, Q_PER_KV, HEAD_DIM).transpose(1, 0, 2, 3, 4, 5)

    def one_block(qblk):
        s = jnp.einsum('bqhgd,bhkd->bhgqk', qblk, k).astype(jnp.float32) * ATT_SCALE
        p = jax.nn.softmax(jnp.concatenate([s, sink_column(sink, s.shape)], axis=-1), axis=-1)
        return jnp.einsum('bhgqk,bhkd->bqhgd', p[..., :-1].astype(v.dtype), v)

    out = lax.map(one_block, qb)
    return out.transpose(1, 0, 2, 3, 4, 5).reshape(b, l, D_ATT)


def latent_attention(q, k, v, k_ctx, v_ctx, sink):
    b, t = q.shape[:2]
    nb = t // BLOCK
    lc = k_ctx.shape[2]
    qb = q.reshape(b, nb, BLOCK, N_KV_HEADS, Q_PER_KV, HEAD_DIM).transpose(1, 0, 2, 3, 4, 5)
    pad = ((0, 0), (BLOCK, BLOCK), (0, 0), (0, 0))

    def band(a):
        ap = jnp.pad(a, pad).reshape(b, nb + 2, BLOCK, N_KV_HEADS, HEAD_DIM)
        return jnp.concatenate([ap[:, :-2], ap[:, 1:-1], ap[:, 2:]], axis=2).transpose(1, 0, 2, 3, 4)

    qi = jnp.arange(BLOCK)
    kj = jnp.arange(3 * BLOCK)
    band_ok = jnp.abs(kj[None, :] - BLOCK - qi[:, None]) <= WINDOW
    key_pos = (jnp.arange(nb)[:, None] - 1) * BLOCK + kj[None, :]
    mask = band_ok[None] & ((key_pos >= 0) & (key_pos < t))[:, None, :]

    def one_block(args):
        qblk, kblk, vblk, mblk = args
        s_ctx = jnp.einsum('bqhgd,bhkd->bhgqk', qblk, k_ctx).astype(jnp.float32) * ATT_SCALE
        s_lat = jnp.einsum('bqhgd,bkhd->bhgqk', qblk, kblk).astype(jnp.float32) * ATT_SCALE
        s_lat = jnp.where(mblk, s_lat, -jnp.inf)
        logits = jnp.concatenate([s_ctx, s_lat, sink_column(sink, s_lat.shape)], axis=-1)
        p = jax.nn.softmax(logits, axis=-1).astype(v.dtype)
        return (jnp.einsum('bhgqk,bhkd->bqhgd', p[..., :lc], v_ctx)
                + jnp.einsum('bhgqk,bkhd->bqhgd', p[..., lc:lc + 3 * BLOCK], vblk))

    out = lax.map(one_block, (qb, band(k), band(v), mask))
    return out.transpose(1, 0, 2, 3, 4, 5).reshape(b, t, D_ATT)


def mlstm_scan(q, k, v, ig, lf, c0, n0, m0):
    bsz, nh, t, dh = q.shape
    nc = t // CHUNK
    causal = jnp.tril(jnp.ones((CHUNK, CHUNK), dtype=bool))

    def chunks(a):
        return jnp.moveaxis(a.reshape((bsz, nh, nc, CHUNK) + a.shape[3:]), 2, 0)

    def step(carry, xs):
        c_prev, n_prev, m_prev = carry
        qc, kc, vc, igc, lfc = xs
        bcum = jnp.cumsum(lfc, axis=-1)
        log_d = jnp.where(causal, bcum[..., :, None] - bcum[..., None, :] + igc[..., None, :], -jnp.inf)
        log_inter = bcum + m_prev[..., None]
        m_t = jnp.maximum(log_inter, jnp.max(log_d, axis=-1))
        dmat = jnp.exp(log_d - m_t[..., None])
        w_inter = jnp.exp(log_inter - m_t)
        s = jnp.einsum('bhtd,bhsd->bhts', qc, kc) * dmat
        numer = (jnp.einsum('bhts,bhse->bhte', s, vc)
                 + w_inter[..., None] * jnp.einsum('bhtd,bhde->bhte', qc, c_prev))
        denom = jnp.sum(s, axis=-1) + w_inter * jnp.einsum('bhtd,bhd->bht', qc, n_prev)
        h = numer / jnp.maximum(jnp.abs(denom), jnp.exp(-m_t))[..., None]
        b_last = bcum[..., -1]
        log_e = b_last[..., None] - bcum + igc
        m_new = jnp.maximum(b_last + m_prev, jnp.max(log_e, axis=-1))
        e = jnp.exp(log_e - m_new[..., None])
        decay = jnp.exp(b_last + m_prev - m_new)
        c_new = decay[..., None, None] * c_prev + jnp.einsum('bhs,bhsd,bhse->bhde', e, kc, vc)
        n_new = decay[..., None] * n_prev + jnp.einsum('bhs,bhsd->bhd', e, kc)
        return (c_new, n_new, m_new), h

    init = (c0.astype(jnp.float32), n0.astype(jnp.float32), m0.astype(jnp.float32))
    (c, n, m), h = lax.scan(step, init, tuple(chunks(a) for a in (q, k, v, ig, lf)))
    return jnp.moveaxis(h, 0, 2).reshape(bsz, nh, t, dh), (c, n, m)


def mlstm_bidir(qm, km, vm, g, gate_b, init_fwd, init_bwd):
    g = g + gate_b.astype(jnp.float32)[None, :, :, None]
    ig_f, lf_f = g[:, 0], jax.nn.log_sigmoid(g[:, 1])
    ig_b, lf_b = g[:, 2], jax.nn.log_sigmoid(g[:, 3])
    h_f, st_f = mlstm_scan(qm, km, vm, ig_f, lf_f, *init_fwd)
    flip = lambda a: jnp.flip(a, axis=2)
    h_b, st_b = mlstm_scan(flip(qm), flip(km), flip(vm), flip(ig_b), flip(lf_b), *init_bwd)
    return h_f + flip(h_b), st_f, st_b


def mlstm_readout(h, om, hnorm_g, dtype):
    bsz, nh, t, dh = h.shape
    h = h.transpose(0, 2, 1, 3)
    h = h * lax.rsqrt(jnp.mean(h * h, axis=-1, keepdims=True) + EPS) * hnorm_g.astype(jnp.float32).reshape(nh, dh)
    return (h.reshape(bsz, t, D_MLSTM) * jax.nn.sigmoid(om.astype(jnp.float32))).astype(dtype)


def odd_project(h, w_in):
    bsz, t = h.shape[:2]
    qa, ka, va, qm, km, vm, om, g = split_cols(h @ w_in, ODD_SPLITS)
    heads = lambda a: a.reshape(bsz, t, N_M_HEADS, HEAD_DIM).transpose(0, 2, 1, 3).astype(jnp.float32)
    g = g.reshape(bsz, t, N_GATES, N_M_HEADS).transpose(0, 2, 3, 1).astype(jnp.float32)
    return (qa.reshape(bsz, t, N_Q_HEADS, HEAD_DIM), ka.reshape(bsz, t, N_KV_HEADS, HEAD_DIM),
            va.reshape(bsz, t, N_KV_HEADS, HEAD_DIM), heads(qm), heads(km) * HEAD_DIM ** -0.5, heads(vm), om, g)


def odd_mixer_context(h, w_in, sink, gate_b, hnorm_g, w_out):
    bsz = h.shape[0]
    qa, ka, va, qm, km, vm, om, g = odd_project(h, w_in)
    k_ctx = ka.transpose(0, 2, 1, 3)
    v_ctx = va.transpose(0, 2, 1, 3)
    att = context_attention(qa, k_ctx, v_ctx, sink)
    zero = (jnp.zeros((bsz, N_M_HEADS, HEAD_DIM, HEAD_DIM), jnp.float32),
            jnp.zeros((bsz, N_M_HEADS, HEAD_DIM), jnp.float32),
            jnp.zeros((bsz, N_M_HEADS), jnp.float32))
    hm, st_f, st_b = mlstm_bidir(qm, km, vm, g, gate_b, zero, zero)
    y = jnp.concatenate([att, mlstm_readout(hm, om, hnorm_g, h.dtype)], axis=-1) @ w_out
    return y, (k_ctx, v_ctx, st_f[0], st_f[1], st_f[2], st_b[0], st_b[1], st_b[2])


def odd_mixer_latent(h, k_ctx, v_ctx, st_f, st_b, w_in, sink, gate_b, hnorm_g, w_out):
    qa, ka, va, qm, km, vm, om, g = odd_project(h, w_in)
    cos, sin = axial_rope_tables(h.shape[1])
    att = latent_attention(apply_rope(qa, cos, sin), apply_rope(ka, cos, sin), va,
                           k_ctx.astype(h.dtype), v_ctx.astype(h.dtype), sink)
    hm, _, _ = mlstm_bidir(qm, km, vm, g, gate_b, st_f, st_b)
    return jnp.concatenate([att, mlstm_readout(hm, om, hnorm_g, h.dtype)], axis=-1) @ w_out


def setup_inputs(seed: int = 0) -> dict:
    key = jax.random.key(seed)
    ks = list(jax.random.split(key, 40))
    cnt = [0]

    def nrm(shape, s=1.0):
        cnt[0] += 1
        return s * jax.random.normal(ks[cnt[0] - 1], shape, jnp.float32)

    return {
        'x_prompt': nrm((BATCH, SEQ, D_MODEL)),
        'x_sample': nrm((DEC_BATCH, DEC_SEQ, D_MODEL)),
        'c': nrm((DEC_BATCH, D_MODEL)),
        'cache_k': nrm((DEC_BATCH, N_ODD, N_KV_HEADS, PAST_LEN, HEAD_DIM)),
        'cache_v': nrm((DEC_BATCH, N_ODD, N_KV_HEADS, PAST_LEN, HEAD_DIM)),
        'state_c_fwd': nrm((DEC_BATCH, N_ODD, N_M_HEADS, HEAD_DIM, HEAD_DIM), 0.5),
        'state_n_fwd': nrm((DEC_BATCH, N_ODD, N_M_HEADS, HEAD_DIM), 0.5),
        'state_m_fwd': nrm((DEC_BATCH, N_ODD, N_M_HEADS)),
        'state_c_bwd': nrm((DEC_BATCH, N_ODD, N_M_HEADS, HEAD_DIM, HEAD_DIM), 0.5),
        'state_n_bwd': nrm((DEC_BATCH, N_ODD, N_M_HEADS, HEAD_DIM), 0.5),
        'state_m_bwd': nrm((DEC_BATCH, N_ODD, N_M_HEADS)),
        'c_ctx': nrm((D_MODEL,)),
        'mod_w': nrm((DEPTH, D_MODEL, 6 * D_MODEL), 0.5 * D_MODEL ** -0.5),
        'mod_b': nrm((DEPTH, 6 * D_MODEL), 0.02),
        'norm_g': 1.0 + nrm((DEPTH, 4, D_MODEL), 0.05),
        'mlp_w1': nrm((DEPTH, D_MODEL, D_FF), D_MODEL ** -0.5),
        'mlp_w2': nrm((DEPTH, D_FF, D_MODEL), D_FF ** -0.5),
        'even_in_w': nrm((N_EVEN, D_MODEL, D_IN_EVEN), D_MODEL ** -0.5),
        'conv_a_w': nrm((N_EVEN, CONF_WIDTH, D_CONF), CONF_WIDTH ** -0.5),
        'conv_a_b': nrm((N_EVEN, D_CONF), 0.02),
        'ln_a_g': 1.0 + nrm((N_EVEN, D_CONF), 0.05),
        'ln_a_b': nrm((N_EVEN, D_CONF), 0.02),
        'conv_b_w': nrm((N_EVEN, SHORT_WIDTH, D_SHORT), SHORT_WIDTH ** -0.5),
        'even_out_w': nrm((N_EVEN, D_CONF + D_SHORT, D_MODEL), (D_CONF + D_SHORT) ** -0.5),
        'odd_in_w': nrm((N_ODD, D_MODEL, D_IN_ODD), D_MODEL ** -0.5),
        'attn_sink': nrm((N_ODD, N_Q_HEADS)),
        'gate_b': nrm((N_ODD, N_GATES, N_M_HEADS), 0.1)
                  + jnp.array([0.0, FORGET_BIAS, 0.0, FORGET_BIAS], jnp.float32)[None, :, None],
        'hnorm_g': 1.0 + nrm((N_ODD, D_MLSTM), 0.05),
        'odd_out_w': nrm((N_ODD, D_ATT + D_MLSTM, D_MODEL), (D_ATT + D_MLSTM) ** -0.5),
    }


def reference(x_prompt, x_sample, c, cache_k, cache_v, state_c_fwd, state_n_fwd, state_m_fwd,
              state_c_bwd, state_n_bwd, state_m_bwd, c_ctx, mod_w, mod_b, norm_g, mlp_w1, mlp_w2,
              even_in_w, conv_a_w, conv_a_b, ln_a_g, ln_a_b, conv_b_w, even_out_w,
              odd_in_w, attn_sink, gate_b, hnorm_g, odd_out_w):
    yp, ys = x_prompt, x_sample
    new = [[] for _ in range(8)]
    for layer in range(DEPTH):
        j = layer // 2
        g4 = norm_g[layer]
        mp_ = modulation(c_ctx[None, :], mod_w[layer], mod_b[layer])
        ms_ = modulation(c, mod_w[layer], mod_b[layer])
        hp = modulate_in(yp, g4[0], mp_[0], mp_[1])
        hs = modulate_in(ys, g4[0], ms_[0], ms_[1])
        if layer % 2 == 0:
            ep = (even_in_w[j], conv_a_w[j], conv_a_b[j], ln_a_g[j], ln_a_b[j], conv_b_w[j], even_out_w[j])
            op_p = even_mixer(hp, *ep)
            op_s = even_mixer(hs, *ep)
        else:
            od = (odd_in_w[j], attn_sink[j], gate_b[j], hnorm_g[j], odd_out_w[j])
            op_p, st = odd_mixer_context(hp, *od)
            for lst, a in zip(new, st):
                lst.append(a)
            op_s = odd_mixer_latent(hs, cache_k[:, j], cache_v[:, j],
                                    (state_c_fwd[:, j], state_n_fwd[:, j], state_m_fwd[:, j]),
                                    (state_c_bwd[:, j], state_n_bwd[:, j], state_m_bwd[:, j]), *od)
        yp = gated_out(yp, op_p, g4[1], mp_[2])
        ys = gated_out(ys, op_s, g4[1], ms_[2])
        hp = modulate_in(yp, g4[2], mp_[3], mp_[4])
        hs = modulate_in(ys, g4[2], ms_[3], ms_[4])
        yp = gated_out(yp, sq_relu_mlp(hp, mlp_w1[layer], mlp_w2[layer]), g4[3], mp_[5])
        ys = gated_out(ys, sq_relu_mlp(hs, mlp_w1[layer], mlp_w2[layer]), g4[3], ms_[5])
    s = [jnp.stack(lst, axis=1) for lst in new]
    return (yp, ys, s[0], s[1], s[2], s[3], s[4], s[5], s[6], s[7])
```

```python
import os
import numpy as np
import concourse.bass as bass
import concourse.mybir as mybir
from concourse.bass_utils import run_bass_kernel_spmd

F32 = mybir.dt.float32
BF16 = mybir.dt.bfloat16
AF = mybir.ActivationFunctionType
ALU = mybir.AluOpType
AX = mybir.AxisListType

D = 1024
T = 1024
EPS = 1e-6
SAME_ENG_SYNC = True
STAGE = 99
SUB = 99
NCORES = 8


class _Stop(Exception):
    pass


def _ck(k):
    if SUB <= k:
        raise _Stop()


class Op:
    __slots__ = ("eng", "fn", "deps", "dma", "dcount", "dwaits", "tick", "inc", "idx")


class Sched:
    ENGS = ("pe", "act", "dve", "pool", "sp")

    def __init__(self):
        self.ops = []
        self.lastw = {}
        self.readers = {}
        self.dcount = {}
        self.pool_dmas = []
        self.psi = 0
        self.bar_deps = set()
        self.bar_dw = {}
        self.rot = {}
        self.marks = []
        self.multiw = {}

    def add(self, eng, fn, r=(), w=(), dma=None):
        op = Op()
        op.eng, op.fn, op.dma = eng, fn, dma
        op.idx = len(self.ops)
        deps = set()
        for k in r:
            if k in self.lastw:
                deps.add(self.lastw[k])
            if k in self.multiw:
                deps.update(self.multiw[k])
            if isinstance(k, tuple) and k[0] == "ps":
                for ridx in self.readers.get(k, ()):
                    if self.ops[ridx].eng != eng:
                        deps.add(ridx)
        for k in w:
            if k in self.lastw:
                deps.add(self.lastw[k])
            if k in self.multiw:
                deps.update(self.multiw.pop(k))
            deps.update(self.readers.get(k, ()))
        if dma is not None and eng == "pool":
            if len(self.pool_dmas) >= 4:
                deps.add(self.pool_dmas[-4])
            self.pool_dmas.append(op.idx)
        deps.update(self.bar_deps)
        deps.discard(op.idx)
        op.deps = deps
        op.dwaits = dict(self.bar_dw)
        for d in deps:
            P = self.ops[d]
            if P.dma is not None:
                op.dwaits[P.dma] = self.dcount[P.dma]
        if dma is not None:
            self.dcount[dma] = self.dcount.get(dma, 0) + 16
            op.dcount = self.dcount[dma]
        op.tick = 0
        op.inc = False
        for k in r:
            self.readers.setdefault(k, []).append(op.idx)
        for k in w:
            self.lastw[k] = op.idx
            self.readers[k] = []
        self.ops.append(op)
        return op

    def mark(self, name):
        self.marks.append((name, sum(1 for o in self.ops if o.eng == 'pe')))

    def ps(self):
        i = self.psi
        self.psi = (self.psi + 1) % 7
        return i

    def ps_from(self, lst):
        k = tuple(lst)
        j = self.rot.get(k, 0)
        self.rot[k] = (j + 1) % len(lst)
        return lst[j]

    def alias(self, fine_keys, coarse):
        idxs = [self.lastw[k] for k in fine_keys if k in self.lastw]
        for k in fine_keys:
            idxs.extend(self.readers.get(k, ()))
        self.multiw.setdefault(coarse, set()).update(idxs)
        self.readers.setdefault(coarse, [])

    def barrier(self):
        last = {}
        for op in self.ops:
            if op.dma is None:
                last[op.eng] = op.idx
        self.bar_deps = set(last.values())
        self.bar_dw = dict(self.dcount)

    def emit(self, nc, final_waits):
        ops = self.ops
        for op in ops:
            for d in op.deps:
                P = ops[d]
                if P.dma is not None:
                    continue
                if P.eng == op.eng and (P.eng == "pe" or not SAME_ENG_SYNC):
                    continue
                P.inc = True
        cnt = {e: 0 for e in self.ENGS}
        for op in ops:
            if op.dma is None and op.inc:
                cnt[op.eng] += 1
                op.tick = cnt[op.eng]
        nops = {e: sum(1 for o in ops if o.eng == e) for e in self.ENGS}
        import contextlib
        with contextlib.ExitStack() as es:
            esem = {e: es.enter_context(nc.semaphore("e_" + e)) for e in self.ENGS}
            dsem = {k: es.enter_context(nc.semaphore("d_" + str(k))) for k in self.dcount}
            block = es.enter_context(nc.Block())

            def run(eng_name):
                def body(e):
                    waited = {}
                    for op in ops:
                        if op.eng != eng_name:
                            continue
                        waits = {}
                        for d in op.deps:
                            P = ops[d]
                            if P.dma is not None:
                                continue
                            if P.eng == op.eng and (P.eng == "pe" or not SAME_ENG_SYNC):
                                continue
                            key = ("e", P.eng)
                            waits[key] = max(waits.get(key, 0), P.tick)
                        for k, v in op.dwaits.items():
                            waits[("d", k)] = max(waits.get(("d", k), 0), v)
                        for key, v in waits.items():
                            if waited.get(key, 0) < v:
                                sem = esem[key[1]] if key[0] == "e" else dsem[key[1]]
                                e.wait_ge(sem, v)
                                waited[key] = v
                        inst = op.fn(e)
                        if op.dma is not None:
                            inst.then_inc(dsem[op.dma], 16)
                        elif op.inc:
                            inst.then_inc(esem[op.eng], 1)
                    if eng_name == "sp":
                        for k in final_waits:
                            if k in self.dcount:
                                e.wait_ge(dsem[k], self.dcount[k])
                return body

            block.tensor(run("pe"))
            block.scalar(run("act"))
            block.vector(run("dve"))
            block.gpsimd(run("pool"))
            block.sync(run("sp"))


def build_program():
    nc = bass.Bass("TRN2", target_bir_lowering=False)
    S = Sched()

    def din(name, shape, dt=F32):
        return nc.dram_tensor(name, list(shape), dt, kind="ExternalInput").ap()

    def dout(name, shape):
        return nc.dram_tensor(name, list(shape), F32, kind="ExternalOutput").ap()

    xp_d = din("xp", [1024, 1024])
    xs_d = din("xs", [1024, 1024])
    condT_d = din("condT", [128, 8, 2])
    modw_d = din("mod_w", [2, 1024, 6144])
    modbT_d = din("modbT", [128, 2, 48])
    normgT_d = din("normgT", [128, 2, 4, 8])
    w1_d = din("mlp_w1", [2, 1024, 4096])
    w2_d = din("mlp_w2", [2, 4096, 1024])
    ein_d = din("even_in_w", [1024, 2560])
    eout_d = din("even_out_w", [1024, 1024])
    oin_d = din("odd_in_w", [1024, 2848])
    oout_d = din("odd_out_w", [1024, 1024])
    convaT_d = din("convaT", [128, 4, 31])
    convbT_d = din("convbT", [128, 4, 3])
    evec_d = din("evec", [128, 3, 4])
    rowv_d = din("rowv", [1, 552])
    ck_d = din("cache_k", [2, 512, 64])
    cv_d = din("cache_v", [2, 512, 64])
    scf_d = din("st_c_f", [8, 64, 64])
    scb_d = din("st_c_b", [8, 64, 64])
    snm_d = din("st_nm", [128, 4, 4])
    cosT_d = din("cosT", [128, 1024])
    sinT_d = din("sinT", [128, 1024])
    sel_d = din("selv", [128, 4])
    mown_d = din("maskown", [128, 2048])
    cmat_d = din("cmat", [128, 4, 128])

    yp_d = dout("yp", [1024, 1024])
    ys_d = dout("ys", [256, 1024])
    nk_d = dout("nk", [4, 2, 256, 64])
    nv_d = dout("nv", [4, 2, 256, 64])
    ncf_d = dout("ncf", [4, 8, 64, 64])
    nnf_d = dout("nnf", [4, 8, 64])
    nmf_d = dout("nmf", [4, 8])
    ncb_d = dout("ncb", [4, 8, 64, 64])
    nnb_d = dout("nnb", [4, 8, 64])
    nmb_d = dout("nmb", [4, 8])

    import contextlib
    es = contextlib.ExitStack()

    def sb(name, shape, dt=F32):
        return es.enter_context(nc.sbuf_tensor(name, list(shape), dt))

    X = sb("X", [128, 8, T])
    H = sb("H", [128, 8, T], BF16)
    Y = sb("Y", [128, 8, T])
    BIG = sb("BIG", [128, 32768], BF16)
    WR = [sb("WR%d" % i, [128, 4096], BF16) for i in range(3)]
    XIN = [sb("XIN%d" % i, [128, 1024]) for i in range(2)]
    MR = [XIN[i][:, :].rearrange("p (k c) -> p k c", c=128) for i in range(2)]
    RS = [sb("RS%d" % i, [128, 512]) for i in range(2)]
    RS2 = sb("RS2", [128, 512])
    cmat = sb("cmat_sb", [128, 4, 128])
    ident = cmat[:, 0, :]
    triU = cmat[:, 1, :]
    triL = cmat[:, 2, :]
    cbf = sb("cbf", [128, 4, 128], BF16)
    ones_f = sb("ones_f", [128, 128])
    mask4 = sb("mask4", [128, 2, 4, 128], BF16)
    epsc = sb("epsc", [128, 1])
    condT = sb("condT_sb", [128, 8, 2])
    modb = sb("modb", [128, 2, 48])
    normg = sb("normg", [128, 2, 4, 8])
    modsb = sb("modsb", [128, 2, 48, 2])
    dvec = sb("dvec", [128, 2, 6, 8, 2])
    convaT = sb("convaT_sb", [128, 4, 31])
    convbT = sb("convbT_sb", [128, 4, 3])
    evec = sb("evec_sb", [128, 3, 4])
    rowb = sb("rowb", [128, 552])
    snm = sb("snm_sb", [128, 4, 4])
    selv = sb("selv_sb", [128, 4])
    SM = sb("SM", [128, 8, 8, 16])
    Cst = sb("Cst", [128, 2, 4, 65])
    Cz = sb("Cz", [128, 2, 8, 66], BF16)
    Ctmp = sb("Ctmp", [128, 4, 65])
    MXB = sb("MXB", [128, 8, 16])
    mrec = sb("mrec", [128, 4, 16])
    CoutS = sb("CoutS", [128, 3, 4, 65])
    DgT = sb("DgT", [128, 128])
    sml = sb("sml", [128, 8, 8])
    d3 = BIG[:, 30240:31776].rearrange("p (k m) -> p k m", m=128)

    PS = [es.enter_context(nc.psum_tensor("ps%d" % i, [128, 512], F32)) for i in range(8)]

    def pk(i):
        return ("ps", i)

    def MM(out, lhsT, rhs, start, stop, r, w, tp=None):
        if False:
            return S.add("pe", lambda e: e.matmul(out, lhsT=lhsT, rhs=rhs, start=start, stop=stop, tile_position=tp), r, w)
        return S.add("pe", lambda e: e.matmul(out, lhsT=lhsT, rhs=rhs, start=start, stop=stop), r, w)

    def TR(out, in_, idn, r, w):
        return S.add("pe", lambda e: e.transpose(out=out, in_=in_, identity=idn), r, w)

    def ACT(out, in_, func, r, w, bias=None, scale=None, eng="act"):
        kw = {}
        if bias is not None:
            kw["bias"] = bias
        if scale is not None:
            kw["scale"] = scale
        return S.add("act", lambda e: e.activation(out=out, in_=in_, func=func, **kw), r, w)

    def TT(eng, out, in0, in1, op, r, w):
        return S.add(eng, lambda e: e.tensor_tensor(out=out, in0=in0, in1=in1, op=op), r, w)

    def TS(eng, out, in0, s1, s2, op0, op1, r, w):
        if s2 is None:
            return S.add(eng, lambda e: e.tensor_scalar(out=out, in0=in0, scalar1=s1, scalar2=None, op0=op0), r, w)
        return S.add(eng, lambda e: e.tensor_scalar(out=out, in0=in0, scalar1=s1, scalar2=s2, op0=op0, op1=op1), r, w)

    def STT(eng, out, in0, sc, in1, op0, op1, r, w):
        return S.add(eng, lambda e: e.scalar_tensor_tensor(out=out, in0=in0, scalar=sc, in1=in1, op0=op0, op1=op1), r, w)

    def CP(eng, out, in_, r, w):
        if eng == "act":
            return S.add("act", lambda e: e.activation(out=out, in_=in_, func=AF.Copy), r, w)
        return S.add(eng, lambda e: e.tensor_copy(out=out, in_=in_), r, w)

    def MS(eng, ap, val, w):
        return S.add(eng, lambda e: e.memset(ap, val), (), w)

    def RCP(out, in_, r, w):
        return S.add("dve", lambda e: e.reciprocal(out=out, in_=in_), r, w)

    def DMA(eng, out, in_, dkey, r, w, slow=False):
        if slow:
            return S.add(eng, lambda e: e.dma_start(out=out, in_=in_, allow_slow_non_contiguous=True), r, w, dma=dkey)
        return S.add(eng, lambda e: e.dma_start(out=out, in_=in_), r, w, dma=dkey)

    alt = [0]

    def ev_eng():
        alt[0] ^= 1
        return "act" if alt[0] else "dve"

    DMA("sp", cmat[:], cmat_d, "const", (), ["cmat"])
    DMA("sp", condT[:], condT_d, "const", (), ["condT"])
    DMA("sp", modb[:], modbT_d, "const", (), ["modb"])
    DMA("sp", normg[:], normgT_d, "const", (), ["normg"])
    DMA("sp", convaT[:], convaT_d, "const", (), ["convaT"])
    DMA("sp", convbT[:], convbT_d, "const", (), ["convbT"])
    DMA("sp", evec[:], evec_d, "const", (), ["evec"])
    DMA("sp", rowb[0:1, :], rowv_d, "const", (), ["rowv"])
    DMA("sp", snm[:], snm_d, "const", (), ["snm"])
    DMA("sp", selv[:], sel_d, "const", (), ["selv"])
    MS("dve", ones_f[:], 1.0, ["ones_f"])
    MS("dve", Cz[:], 0.0, [("Cz", 0), ("Cz", 1)])
    MS("dve", epsc[:], EPS, ["epsc"])
    MS("dve", cbf[:, 3, :], 1.0, ["cbf"])
    CP("dve", cbf[:, 0:3, :], cmat[:, 0:3, :], ["cmat"], ["cbf"])
    for k in range(4):
        CP("dve", mask4[:, 0, k, :], cmat[:, 1, :], ["cmat"], ["mask4"])
        CP("dve", mask4[:, 1, k, :], cmat[:, 2, :], ["cmat"], ["mask4"])
    ones_b = cbf[:, 3, :]
    ident_b = cbf[:, 0, :]
    scTb = sb("scTb", [128, 8, 2], BF16)
    ACT(scTb[:], condT[:], AF.Silu, ["condT"], ["scT"])
    for c0 in (0, 276):
        MM(PS[7][:, 0:276], ones_f[0:1, :], rowb[0:1, c0:c0 + 276], True, True, ["ones_f", "rowv"], [pk(7)])
        CP("dve", rowb[:, c0:c0 + 276], PS[7][:, 0:276], [pk(7)], ["rowb", "rowv"])
    ACT(rowb[:, 0:8], rowb[:, 0:8], AF.Exp, ["rowb"], ["rowb"])
    esink = rowb[:, 0:8]
    gateb = rowb[:, 8:40]
    hng = rowb[:, 40:552]

    def mod_steps(l):
        for cb in range(12):
            view, wkey = load_slab(modw_d[l], 8, [(0, cb * 512, 512)])
            b = S.ps()
            for kc in range(8):
                MM(PS[b][0:2, :], scTb[:, kc, :], view[:, kc, :], kc == 0, kc == 7, [wkey, "scT"], [pk(b)])
            mrow = RS2
            CP("dve", mrow[0:2, :], PS[b][0:2, :], [pk(b)], ["RS2"])
            for j in range(4):
                oc = cb * 4 + j
                MM(PS[7][:, oc * 2:oc * 2 + 2], mrow[0:2, j * 128:(j + 1) * 128], ident[0:2, 0:2], True, True,
                   ["RS2", "cmat"], [pk(7)])
            if cb % 2 == 1:
                grp = cb // 2
                TT("dve", modsb[:, l, grp * 8:(grp + 1) * 8, :],
                   PS[7][:, grp * 16:(grp + 1) * 16].rearrange("p (a b) -> p a b", b=2),
                   modb[:, l, grp * 8:(grp + 1) * 8].unsqueeze(2).to_broadcast([128, 8, 2]), ALU.add,
                   [pk(7), "modb"], [("modsb", l, grp)])
                src = modsb[:, l, grp * 8:(grp + 1) * 8, :]
                if grp in (0, 3):
                    j = 1 if grp == 0 else 4
                    CP("dve", dvec[:, l, j, :, :], src, [("modsb", l, grp)], [("dvec", l, j)])
                elif grp in (1, 4):
                    j = 0 if grp == 1 else 3
                    gi = 0 if grp == 1 else 2
                    TS("dve", dvec[:, l, j, :, :], src, 1.0, None, ALU.add, None, [("modsb", l, grp)], [("dvec", l, j)])
                    TT("dve", dvec[:, l, j, :, :], dvec[:, l, j, :, :],
                       normg[:, l, gi, :].unsqueeze(2).to_broadcast([128, 8, 2]), ALU.mult, [("dvec", l, j), "normg"],
                       [("dvec", l, j)])
                else:
                    j = 2 if grp == 2 else 5
                    gi = 1 if grp == 2 else 3
                    TT("dve", dvec[:, l, j, :, :], src, normg[:, l, gi, :].unsqueeze(2).to_broadcast([128, 8, 2]), ALU.mult,
                       [("modsb", l, grp), "normg"], [("dvec", l, j)])
            yield

    def run_all(gen):
        for _ in gen:
            pass

    bg = [None]
    half_is0 = [True]
    bgcnt = [0]
    bg_every = [1]

    def bg_step(n=1):
        if bg[0] is None:
            return
        bgcnt[0] += 1
        if bgcnt[0] % bg_every[0] != 0:
            return
        bg_force(n)

    bgdone = [0]

    def bg_force(n=1):
        for _ in range(n):
            if bg[0] is None:
                return
            try:
                next(bg[0])
                bgdone[0] += 1
            except StopIteration:
                bg[0] = None
                return

    def bg_ensure(n_done):
        while bg[0] is not None and bgdone[0] < n_done:
            bg_force(1)

    def bg_flush():
        if bg[0] is not None:
            run_all(bg[0])
            bg[0] = None

    wri = [0]

    def load_slab(Wd, KC, pieces):
        ncol = max(p[0] + p[2] for p in pieces)
        i = wri[0]
        wri[0] = (wri[0] + 1) % 3
        view = WR[i][:, 0:KC * ncol].rearrange("p (k c) -> p k c", c=ncol)
        for (off, c0, wd) in pieces:
            kstep = max(1, min(KC, 2048 // wd))
            for k0 in range(0, KC, kstep):
                DMA("pool", view[:, k0:k0 + kstep, off:off + wd],
                    Wd[k0 * 128:(k0 + kstep) * 128, c0:c0 + wd].rearrange("(kc p) c -> p kc c", p=128),
                    "wr%d" % i, (), [("WR", i)])
        return view, ("WR", i)

    WIN = [2, 512]

    def wsl(tb):
        return slice(tb * WIN[1], (tb + 1) * WIN[1])

    def kX(tb):
        return [("X", t) for t in range(4 * tb, 4 * tb + 4)]

    def kH(tb):
        if WIN[1] == 256:
            return [("H", tb)]
        return [("H", 2 * tb), ("H", 2 * tb + 1)]

    def kY(tb):
        if WIN[1] == 256:
            return [("Y", tb)]
        return [("Y", 2 * tb), ("Y", 2 * tb + 1)]

    def load_x(xd):
        for t in range(8):
            xin = XIN[t % 2]
            DMA("sp", xin[:], xd[t * 128:(t + 1) * 128, :], "xin%d" % (t % 2), (), [("XIN", t % 2)])
            for hf in range(2):
                b = S.ps()
                for j in range(4):
                    kc = hf * 4 + j
                    TR(PS[b][:, j * 128:(j + 1) * 128], xin[:, kc * 128:(kc + 1) * 128], ident,
                       [("XIN", t % 2), "cmat"], [pk(b)])
                CP(ev_eng(), X[:, hf * 4:hf * 4 + 4, t * 128:(t + 1) * 128],
                   PS[b][:].rearrange("p (a b) -> p a b", b=128), [pk(b)], [("X", t)])

    def store_x(yd, ntiles=8):
        for t in range(ntiles):
            xin = XIN[t % 2]
            for hf in range(2):
                b = S.ps()
                for j in range(4):
                    kc = hf * 4 + j
                    TR(PS[b][:, j * 128:(j + 1) * 128], X[:, kc, t * 128:(t + 1) * 128], ident,
                       [("X", t), "cmat"], [pk(b)])
                CP(ev_eng(), xin[:, hf * 512:(hf + 1) * 512], PS[b][:], [pk(b)], [("XIN", t % 2)])
            DMA("sp", yd[t * 128:(t + 1) * 128, :], xin[:], "out", [("XIN", t % 2)], [])

    def rstd_from_ps(b, rs, scale, rkey):
        ACT(rs[:], PS[b][:], AF.Sqrt, [pk(b), "epsc"], [rkey], bias=epsc[:, 0:1], scale=scale)
        RCP(rs[:], rs[:], [rkey], [rkey])

    def rsq(q):
        return RS[q // 2][:, (q % 2) * 256:(q % 2 + 1) * 256], ("RSq", q)

    QS = [slice(q * 256, (q + 1) * 256) for q in range(4)]

    def stats_all(srcbuf, srckeys_fn, presq=False):
        if not presq:
            for q in range(4):
                ACT(H[:, :, QS[q]], srcbuf[:, :, QS[q]], AF.Square, srckeys_fn(q), [("H", q)])
        banks = []
        for q in range(4):
            b = S.ps()
            banks.append(b)
            for kc in range(8):
                MM(PS[b][:, 0:256], ones_b, H[:, kc, QS[q]], kc == 0, kc == 7, [("H", q), "cbf"], [pk(b)])
        for q in range(4):
            rs, rk = rsq(q)
            ACT(rs, PS[banks[q]][:, 0:256], AF.Ln, [pk(banks[q]), "epsc"], [rk], bias=epsc[:, 0:1], scale=1.0 / D)
        for q in range(4):
            rs, rk = rsq(q)
            ACT(rs, rs, AF.Exp, [rk], [rk], scale=-0.5)

    def xkeys(q):
        return [("X", 2 * q), ("X", 2 * q + 1)]

    def modnorm(l, j, cond):
        gs = dvec[:, l, 3 * j, :, cond:cond + 1]
        sh = dvec[:, l, 3 * j + 1, :, cond:cond + 1]
        dk = [("dvec", l, 3 * j), ("dvec", l, 3 * j + 1)]
        stats_all(X, xkeys)
        for q in range(4):
            rs, rk = rsq(q)
            TT("dve", Y[:, :, QS[q]], X[:, :, QS[q]], rs.unsqueeze(1).to_broadcast([128, 8, 256]), ALU.mult,
               xkeys(q) + [rk], [("Y", q)])
        for q in range(4):
            for kc in range(8):
                ACT(H[:, kc, QS[q]], Y[:, kc, QS[q]], AF.Identity, [("Y", q)] + dk, [("H", q)],
                    bias=sh[:, kc, :], scale=gs[:, kc, :])

    def gated_out(l, j, cond, prescaled=False):
        gg = dvec[:, l, 3 * j + 2, :, cond:cond + 1]
        stats_all(Y, lambda q: [("Y", q)], presq=prescaled)
        for q in range(4):
            rs, rk = rsq(q)
            TT("dve", Y[:, :, QS[q]], Y[:, :, QS[q]], rs.unsqueeze(1).to_broadcast([128, 8, 256]), ALU.mult,
               [("Y", q), rk], [("Y", q)])
            if not prescaled:
                TT("dve", Y[:, :, QS[q]], Y[:, :, QS[q]], gg.to_broadcast([128, 8, 256]), ALU.mult,
                   [("Y", q), ("dvec", l, 3 * j + 2)], [("Y", q)])
        for q in range(4):
            TT("dve", X[:, 0:6, QS[q]], X[:, 0:6, QS[q]], Y[:, 0:6, QS[q]], ALU.add, [("Y", q)] + xkeys(q), [("Xa", q)])
            TT("pool", X[:, 6:8, QS[q]], X[:, 6:8, QS[q]], Y[:, 6:8, QS[q]], ALU.add, [("Y", q)] + xkeys(q), [("Xb", q)])
        for q in range(4):
            for t_ in (2 * q, 2 * q + 1):
                S.alias([("Xa", q), ("Xb", q)], ("X", t_))

    def junction(go, mn, cond, qs=(0, 1, 2, 3)):
        def st(q):
            b = S.ps()
            for kc in range(8):
                MM(PS[b][:, 0:256], ones_b, H[:, kc, QS[q]], kc == 0, kc == 7, [("H", q), "cbf"], [pk(b)])
            rs, rk = rsq(q)
            ACT(rs, PS[b][:, 0:256], AF.Ln, [pk(b), "epsc"], [rk], bias=epsc[:, 0:1], scale=1.0 / D)
            ACT(rs, rs, AF.Exp, [rk], [rk], scale=-0.5)

        def A(q):
            l, j, pres = go
            if not pres:
                ACT(H[:, :, QS[q]], Y[:, :, QS[q]], AF.Square, [("Y", q)], [("H", q)])
            st(q)

        def B(q):
            l, j, pres = go
            gg = dvec[:, l, 3 * j + 2, :, cond:cond + 1]
            rs, rk = rsq(q)
            TT("dve", Y[:, :, QS[q]], Y[:, :, QS[q]], rs.unsqueeze(1).to_broadcast([128, 8, 256]), ALU.mult,
               [("Y", q), rk], [("Y", q)])
            if not pres:
                TT("dve", Y[:, :, QS[q]], Y[:, :, QS[q]], gg.to_broadcast([128, 8, 256]), ALU.mult,
                   [("Y", q), ("dvec", l, 3 * j + 2)], [("Y", q)])
            TT("dve", X[:, :, QS[q]], X[:, :, QS[q]], Y[:, :, QS[q]], ALU.add, [("Y", q)] + xkeys(q), xkeys(q))

        def C(q):
            ACT(H[:, :, QS[q]], X[:, :, QS[q]], AF.Square, xkeys(q), [("H", q)])
            st(q)

        def Dq(q):
            rs, rk = rsq(q)
            TT("dve", Y[:, :, QS[q]], X[:, :, QS[q]], rs.unsqueeze(1).to_broadcast([128, 8, 256]), ALU.mult,
               xkeys(q) + [rk], [("Y", q)])

        def E(q):
            l, j = mn
            gs = dvec[:, l, 3 * j, :, cond:cond + 1]
            sh = dvec[:, l, 3 * j + 1, :, cond:cond + 1]
            dk = [("dvec", l, 3 * j), ("dvec", l, 3 * j + 1)]
            for kc in range(8):
                ACT(H[:, kc, QS[q]], Y[:, kc, QS[q]], AF.Identity, [("Y", q)] + dk, [("H", q)],
                    bias=sh[:, kc, :], scale=gs[:, kc, :])

        order = [(A, 0), (A, 1), (A, 2), (A, 3), (B, 0), (C, 0), (B, 1), (C, 1), (B, 2), (C, 2), (B, 3), (C, 3),
                 (Dq, 0), (E, 0), (Dq, 1), (E, 1), (Dq, 2), (E, 2), (Dq, 3), (E, 3)]
        for fn, q in order:
            if q not in qs:
                continue
            if fn in (A, B) and go is None:
                continue
            if fn in (C, Dq, E) and mn is None:
                continue
            fn(q)

    def epi_scaled(l, j, cond):
        gg = dvec[:, l, 3 * j + 2, :, cond:cond + 1]

        def epi_(ci, tb, b):
            sl = wsl(tb)
            W_ = WIN[1]
            ACT(Y[:, ci, sl], PS[b][:, 0:W_], AF.Copy, [pk(b), ("dvec", l, 3 * j + 2)], kY(tb), scale=gg[:, ci, :])
            ACT(H[:, ci, sl], PS[b][:, 0:W_], AF.Square, [pk(b)], kH(tb))
        return epi_

    def proj_fm(Wd, KC, chunks, src, src_keys, epi, group=2, lead=0):
        def load_group(g0):
            grp = chunks[g0:g0 + group]
            pieces = []
            for gi, ch in enumerate(grp):
                off = gi * 128
                for (c0, wd) in ch:
                    pieces.append((off, c0, wd))
                    off += wd
            merged = []
            for p in pieces:
                if merged and merged[-1][0] + merged[-1][2] == p[0] and merged[-1][1] + merged[-1][2] == p[1]:
                    merged[-1] = (merged[-1][0], merged[-1][1], merged[-1][2] + p[2])
                else:
                    merged.append(p)
            view, wkey = load_slab(Wd, KC, merged)
            return grp, view, wkey

        def run(g0, grp, view, wkey, tb):
            for gi in range(len(grp)):
                b = S.ps()
                for kc in range(KC):
                    MM(PS[b][:, 0:WIN[1]], view[:, kc, gi * 128:(gi + 1) * 128], src(kc, tb), kc == 0, kc == KC - 1,
                       [wkey] + src_keys(tb), [pk(b)])
                epi(g0 + gi, tb, b)

        starts = list(range(0, len(chunks), group))
        nlead = lead if (WIN[0] == 2 and len(starts) >= lead) else 0
        if nlead:
            loaded = [(g0,) + load_group(g0) for g0 in starts[:nlead]]
            for tb in range(2):
                for (g0, grp, view, wkey) in loaded:
                    run(g0, grp, view, wkey, tb)
            for _ in range(nlead):
                bg_step()
        for g0 in starts[nlead:]:
            grp, view, wkey = load_group(g0)
            for tb in range(WIN[0]):
                run(g0, grp, view, wkey, tb)
            bg_step()

    def srcH(kc, tb):
        return H[:, kc, wsl(tb)]

    hid = BIG[:, :].rearrange("p (a b) -> p a b", b=T)

    def mlp(l, cond):

        def epiA(ci, tb, b):
            sl = wsl(tb)
            W_ = WIN[1]
            if W_ == 256:
                ti = ci % 2
                tmp = (RS[1], RS2)[ti]
                tk = (("RS", 1), "RS2")[ti]
            else:
                ti = (ci * 2 + tb) % 3
                tmp = (RS[0], RS[1], RS2)[ti]
                tk = (("RS", 0), ("RS", 1), "RS2")[ti]
            ACT(tmp[:, 0:W_], PS[b][:, 0:W_], AF.Relu, [pk(b)], [tk])
            TT("dve", hid[:, ci, sl], tmp[:, 0:W_], tmp[:, 0:W_], ALU.mult, [tk], [("hid", ci, tb)])

        proj_fm(w1_d[l], 8, [[(c * 128, 128)] for c in range(32)], srcH, kH, epiA, group=4, lead=3)
        S.mark('mlpA')

        def srcHid(kc, tb):
            return hid[:, kc, wsl(tb)]

        def keysHid(tb):
            return [("hid", c, tb) for c in range(32)]

        if WIN[1] == 256:
            epi2 = epi_scaled(l, 1, cond)
            banks = list(range(8))
            for sidx in range(8):
                view, wkey = load_slab(w2_d[l][sidx * 512:(sidx + 1) * 512, :], 4, [(0, 0, 1024)])
                for oc in range(8):
                    for kc in range(4):
                        hc = sidx * 4 + kc
                        MM(PS[banks[oc]][:, 0:256], view[:, kc, oc * 128:(oc + 1) * 128],
                           hid[:, hc, 0:256], sidx == 0 and kc == 0, sidx == 7 and kc == 3,
                           [wkey, ("hid", hc, 0)], [pk(banks[oc])])
            W_ = 256
            gg_ = dvec[:, l, 5, :, cond:cond + 1]
            for oc in range(8):
                src_ = PS[banks[oc]][:, 0:256]
                ACT(Y[:, oc, 0:256], src_, AF.Copy, [pk(banks[oc]), ("dvec", l, 5)], kY(0), scale=gg_[:, oc, :])
                ACT(H[:, oc, 0:256], src_, AF.Square, [pk(banks[oc])], kH(0))
        else:
            proj_fm(w2_d[l], 32, [[(c * 128, 128)] for c in range(8)], srcHid, keysHid, epi_scaled(l, 1, cond), group=1)
        S.mark('mlpB')
        pass

    def even_layer(cond, nseq):
        L = T // nseq
        LP = L + 30
        LC = L + 2
        apad = BIG[:, 0:4576].rearrange("p (c x) -> p c x", c=4)
        cxpad = BIG[:, 4576:8736].rearrange("p (c x) -> p c x", c=4)
        bgt = BIG[:, 8736:12832].rearrange("p (c x) -> p c x", c=4)
        ac = BIG[:, 12832:21024].bitcast(F32).rearrange("p (c x) -> p c x", c=4)
        U = BIG[:, 21024:29216].rearrange("p (c x) -> p c x", c=8)
        sgt = BIG[:, 29216:30240].bitcast(F32)
        diag = Y[:, :, :].rearrange("p a b -> p (a b)").bitcast(BF16)[:, 0:15872].rearrange("p (k m) -> p k m", m=128)

        MS("dve", diag[:, 0, 0:2], 0.0, ["Yclaim"] + [("Y", q) for q in range(4)])

        def diag_gen():
            for c in range(4):
                for k in range(31):
                    TS("dve", diag[:, c * 31 + k, :], ident_b, convaT[:, c, k:k + 1], None, ALU.mult, None,
                       ["cbf", "convaT", "Yclaim"], [("diag", c, k)])
                    yield
        dgen = [diag_gen()]

        def diag_step(n):
            for _ in range(n):
                if dgen[0] is None:
                    return
                try:
                    next(dgen[0])
                except StopIteration:
                    dgen[0] = None
        MS("dve", apad[:], 0.0, ["apad"])
        MS("dve", cxpad[:], 0.0, ["cxpad"])

        def padview(buf, c, tb, LPx, padl):
            if nseq == 1:
                return buf[:, c, padl + tb * 512: padl + (tb + 1) * 512]
            s0 = tb * 2
            return buf[:, c, s0 * LPx:(s0 + 2) * LPx].rearrange("p (s x) -> p s x", s=2)[:, :, padl:padl + L]

        def psview(b):
            if nseq == 1:
                return PS[b][:]
            return PS[b][:].rearrange("p (s x) -> p s x", s=2)

        hold = {}

        def epi(ci, tb, b):
            diag_step(4)
            grp, c = ci // 2 // 4, None
            if ci < 8:
                c = ci // 2
                if ci % 2 == 0:
                    hold[(tb, "v")] = b
                else:
                    bv = hold[(tb, "v")]
                    ACT(sgt[:], PS[b][:], AF.Sigmoid, [pk(b)], ["sgt"])
                    dst = padview(apad, c, tb, LP, 15)
                    sv = sgt[:] if nseq == 1 else sgt[:].rearrange("p (s x) -> p s x", s=2)
                    TT("dve", dst, psview(bv), sv, ALU.mult, [pk(bv), "sgt"], ["apad"])
            elif ci < 16:
                c = (ci - 8) // 2
                if ci % 2 == 0:
                    hold[(tb, "v")] = b
                else:
                    bv = hold[(tb, "v")]
                    ACT(sgt[:], PS[b][:], AF.Copy, [pk(b)], ["sgt"])
                    dst = padview(cxpad, c, tb, LC, 1)
                    sv = sgt[:] if nseq == 1 else sgt[:].rearrange("p (s x) -> p s x", s=2)
                    TT("dve", dst, psview(bv), sv, ALU.mult, [pk(bv), "sgt"], ["cxpad"])
            else:
                c = ci - 16
                CP(ev_eng(), bgt[:, c, tb * 512:(tb + 1) * 512], PS[b][:], [pk(b)], ["bgt"])

        chunks = []
        for c in range(4):
            chunks += [[(c * 128, 128)], [(512 + c * 128, 128)]]
        for c in range(4):
            chunks += [[(1536 + c * 128, 128)], [(2048 + c * 128, 128)]]
        for c in range(4):
            chunks += [[(1024 + c * 128, 128)]]
        proj_fm(ein_d, 8, chunks, srcH, kH, epi, lead=3)
        diag_step(200)
        for c in range(4):
            S.alias([("diag", c, k) for k in range(31)], ("diag", c))
        S.mark('e_inproj')

        for c in range(4):
            for k in range(3):
                TS("dve", d3[:, c * 3 + k, :], ident_b, convbT[:, c, k:k + 1], None, ALU.mult, None,
                   ["cbf", "convbT"], ["d3"])

        def win(buf, c, tb, LPx, k):
            if nseq == 1:
                return buf[:, c, k + tb * 512: k + (tb + 1) * 512]
            s0 = tb * 2
            return buf[:, c, s0 * LPx:(s0 + 2) * LPx].rearrange("p (s x) -> p s x", s=2)[:, :, k:k + L]

        for cp in range(2):
            for cc in range(2):
                c = cp * 2 + cc
                for tb in range(2):
                    sl = slice(tb * 512, (tb + 1) * 512)
                    b = S.ps()
                    for k in range(31):
                        MM(psview(b), diag[:, c * 31 + k, :], win(apad, c, tb, LP, k), k == 0, k == 30,
                           [("diag", c), "apad"], [pk(b)])
                    ACT(ac[:, c, sl], PS[b][:], AF.Identity, [pk(b), "evec"], [("ac", tb)], bias=evec[:, 0, c:c + 1])
                    b2 = S.ps()
                    for k in range(3):
                        MM(psview(b2), d3[:, c * 3 + k, :], win(cxpad, c, tb, LC, k), k == 0, k == 2,
                           ["d3", "cxpad"], [pk(b2)])
                    TT("dve", U[:, 4 + c, sl], PS[b2][:], bgt[:, c, sl], ALU.mult, [pk(b2), "bgt"], [("U", tb)])
            bg_force(2)
        S.mark('e_conv')
        for tb in range(2):
            sl = slice(tb * 512, (tb + 1) * 512)
            sq = H[:, 0:4, sl]
            ACT(sq, ac[:, :, sl], AF.Square, [("ac", tb)], kH(tb))
            b1 = S.ps()
            for c in range(4):
                MM(PS[b1][:], ones_f[:], ac[:, c, sl], c == 0, c == 3, [("ac", tb), "ones_f"], [pk(b1)])
            b2 = S.ps()
            for c in range(4):
                MM(PS[b2][:], ones_b, H[:, c, sl], c == 0, c == 3, kH(tb) + ["cbf"], [pk(b2)])
            mean = RS[tb]
            TS("dve", mean[:], PS[b1][:], 1.0 / 512, None, ALU.mult, None, [pk(b1)], [("RS", tb)])
            TT("dve", RS2[:], mean[:], mean[:], ALU.mult, [("RS", tb)], ["RS2"])
            STT("dve", RS2[:], PS[b2][:], 1.0 / 512, RS2[:], ALU.mult, ALU.subtract, [pk(b2), "RS2"], ["RS2"])
            ACT(RS2[:], RS2[:], AF.Sqrt, ["RS2", "epsc"], ["RS2"], bias=epsc[:, 0:1], scale=1.0)
            RCP(RS2[:], RS2[:], ["RS2"], ["RS2"])
            TT("dve", ac[:, :, sl], ac[:, :, sl], mean[:].unsqueeze(1).to_broadcast([128, 4, 512]), ALU.subtract,
               [("ac", tb), ("RS", tb)], [("ac", tb)])
            TT("dve", ac[:, :, sl], ac[:, :, sl], RS2[:].unsqueeze(1).to_broadcast([128, 4, 512]), ALU.mult,
               [("ac", tb), "RS2"], [("ac", tb)])
            for c in range(4):
                ACT(U[:, c, sl], ac[:, c, sl], AF.Silu, [("ac", tb), "evec"], [("U", tb)],
                    bias=evec[:, 2, c:c + 1], scale=evec[:, 1, c:c + 1])

        def srcU(kc, tb):
            return U[:, kc, tb * 512:(tb + 1) * 512]

        for q in range(4):
            S.alias([("diag", c) for c in range(4)], ("Y", q))
        if half_is0[0]:
            bg_ensure(6)
            bg_every[0] = 2
        proj_fm(eout_d, 8, [[(c * 128, 128)] for c in range(8)], srcU, lambda tb: [("U", tb)], epi_scaled(0, 0, cond))
        S.mark('e_outproj')
        pass


    ATT_SCALE = 64 ** -0.5

    def odd_layer(half, cond):
        nseq = 4 if half == 0 else 1
        tps = 8 // nseq
        sample = (half == 1)
        S.barrier()
        QaT = BIG[:, 0:4096].rearrange("p (c x) -> p c x", c=4)
        KaT2 = BIG[:, 4096:6144].rearrange("p (c x) -> p c x", c=2)
        Va1 = BIG[:, 6144:7200].rearrange("p (t g e) -> p t g e", t=8, g=2)
        QmT = BIG[:, 7200:11296].rearrange("p (c x) -> p c x", c=4)
        KmT = BIG[:, 11296:15392].rearrange("p (c x) -> p c x", c=4)
        Kmtok = BIG[:, 15392:19488].rearrange("p (t x) -> p t x", t=8)
        Vm1 = BIG[:, 19488:23712].rearrange("p (t h e) -> p t h e", t=8, h=8)
        SG = BIG[:, 23712:27808].rearrange("p (t x) -> p t x", t=8)
        G = BIG[:, 27808:28320].bitcast(F32).rearrange("p (t x) -> p t x", t=8)
        KcT = BIG[:, 28320:29344].rearrange("p (g x) -> p g x", g=2)
        Vc1 = BIG[:, 29344:29872].rearrange("p (k g e) -> p k g e", k=4, g=2)
        Vpp = [BIG[:, 29872 + i * 528:29872 + (i + 1) * 528].rearrange("p (h e) -> p h e", h=8) for i in range(2)]
        PmT = [BIG[:, 30928 + i * 512:30928 + (i + 1) * 512].rearrange("p (h x) -> p h x", h=4) for i in range(2)]
        Yf = Y[:, :, :].rearrange("p a b -> p (a b)")
        ropeA = Yf[:, 0:512]
        ropeB = Yf[:, 512:1024]
        KVst = Yf[:, 1024:3072].rearrange("p (t x) -> p t x", t=8)
        cosT = Yf[:, 3072:4096]
        sinT = Yf[:, 4096:5120]
        concat = Y
        Hf = H[:, :, :].rearrange("p a b -> p (a b)")
        PTc = [XIN[i][:, :].bitcast(BF16).rearrange("p (k x) -> p k x", k=4) for i in range(2)]
        _ptb = [RS[0][:, :].bitcast(BF16), RS[1][:, :].bitcast(BF16), RS2[:, :].bitcast(BF16), BIG[:, 31952:32720]]
        PTbj = [_ptb[j // 2][:, (j % 2) * 384:(j % 2 + 1) * 384].rearrange("p (r x) -> p r x", r=3) for j in range(8)]

        if sample:
            DMA("sp", cosT, cosT_d, "const2", (), ["cosT"])
            DMA("sp", sinT, sinT_d, "const2", (), ["sinT"])
        MS("dve", Va1[:, :, :, 64:65], 1.0, ["Va1"])
        MS("dve", Vm1[:, :, :, 64:65], 1.0, ["Vm1"])

        chunks = []
        kinds = []
        for c in range(4):
            chunks.append([(c * 128, 128)]); kinds.append(("qa", c))
        for g in range(2):
            chunks.append([(512 + g * 64, 64), (512 + g * 64, 64)]); kinds.append(("ka", g))
        if not sample:
            pass
        for c in range(4):
            chunks.append([(768 + c * 128, 128)]); kinds.append(("qm", c))
        for c in range(4):
            chunks.append([(1280 + c * 128, 128)]); kinds.append(("km", c))
        hold = {}

        def epi(ci, tb, b):
            kind, c = kinds[ci]
            sl = slice(tb * 512, (tb + 1) * 512)
            if kind in ("qa", "ka"):
                dst = QaT[:, c, sl] if kind == "qa" else KaT2[:, c, sl]
                dkey = ("QaT", tb) if kind == "qa" else ("KaT2", tb)
                if sample:
                    CP("act", ropeA, PS[b][:], [pk(b)], ["ropeA"])
                    bsw = S.ps()
                    MM(PS[bsw][:], cmat[:, 3, :], ropeA, True, True, ["ropeA", "cmat"], [pk(bsw)])
                    TT("dve", ropeB, PS[bsw][:], sinT[:, sl], ALU.mult, [pk(bsw), "sinT"], ["ropeB"])
                    TT("dve", ropeA, ropeA, cosT[:, sl], ALU.mult, ["ropeA", "cosT"], ["ropeA"])
                    TT("dve", dst, ropeA, ropeB, ALU.add, ["ropeA", "ropeB"], [dkey])
                else:
                    CP(ev_eng(), dst, PS[b][:], [pk(b)], [dkey])
            elif kind == "qm":
                CP(ev_eng(), QmT[:, c, sl], PS[b][:], [pk(b)], [("QmT", tb)])
            else:
                ACT(KmT[:, c, sl], PS[b][:], AF.Copy, [pk(b)], [("KmT", tb)], scale=0.125)

        proj_fm(oin_d, 8, chunks, srcH, kH, epi, lead=3)
        S.mark('o_inproj_fm')
        _ck(1)

        def proj_tm(c0, ncols, epi_t):
            view, wkey = load_slab(oin_d, 8, [(0, c0, ncols)])
            for t in range(8):
                b = S.ps()
                for kc in range(8):
                    MM(PS[b][:, 0:ncols], H[:, kc, t * 128:(t + 1) * 128], view[:, kc, :], kc == 0, kc == 7,
                       [wkey] + kH(t // 4), [pk(b)])
                epi_t(t, b)

        def epi_kv(t, b):
            import os
            if not sample:
                kvm = '0'
                if kvm != '2':
                    CP("act", KVst[:, t, :], PS[b][:, 0:256], [pk(b)], [("KVst", t)])
                s_, qt = t // 2, t % 2
                for g_ in (range(2) if kvm != '1' else ()):
                    DMA("sp", nk_d[s_, g_, qt * 128:(qt + 1) * 128, :], KVst[:, t, g_ * 64:(g_ + 1) * 64], "out", [("KVst", t)], [])
                    DMA("sp", nv_d[s_, g_, qt * 128:(qt + 1) * 128, :], KVst[:, t, 128 + g_ * 64:128 + (g_ + 1) * 64], "out", [("KVst", t)], [])
            CP("dve", Va1[:, t, :, 0:64], PS[b][:, 128:256].rearrange("p (g d) -> p g d", g=2), [pk(b)], ["Va1"])

        proj_tm(512, 256, epi_kv)
        _ck(1.1)

        def epi_km(t, b):
            ACT(Kmtok[:, t, :], PS[b][:], AF.Copy, [pk(b)], ["Kmtok"], scale=0.125)

        proj_tm(1280, 512, epi_km)
        _ck(1.2)

        def epi_vm(t, b):
            CP("dve", Vm1[:, t, :, 0:64], PS[b][:].rearrange("p (h d) -> p h d", h=8), [pk(b)], ["Vm1"])

        proj_tm(1792, 512, epi_vm)
        _ck(1.3)

        def epi_om(t, b):
            ACT(SG[:, t, :], PS[b][:], AF.Sigmoid, [pk(b)], ["SG"])
            TT("dve", SG[:, t, :], SG[:, t, :], hng, ALU.mult, ["SG", "rowb"], ["SG"])

        proj_tm(2304, 512, epi_om)
        _ck(1.4)

        def epi_g(t, b):
            TT("dve", G[:, t, :], PS[b][:, 0:32], gateb, ALU.add, [pk(b), "rowb"], ["G"])

        proj_tm(2816, 32, epi_g)
        S.mark('o_inproj_tm')
        _ck(2)

        if sample:
            kst = XIN[0][:, :].rearrange("p (k g r d) -> p k g r d", k=4, g=2, r=2)
            vst = XIN[1][:, 0:512].rearrange("p (k g d) -> p k g d", k=4, g=2)
            for r_ in range(2):
                for g in range(2):
                    DMA("sp", kst[:, :, g, r_, :], ck_d[g].rearrange("(k p) d -> p k d", p=128), "xin0", (), [("XIN", 0)])
            for g in range(2):
                DMA("sp", vst[:, :, g, :], cv_d[g].rearrange("(k p) d -> p k d", p=128), "xin1", (), [("XIN", 1)])
            for g in range(2):
                b = S.ps()
                for kb in range(4):
                    TR(PS[b][:, kb * 128:(kb + 1) * 128], kst[:, kb, g].rearrange("p r d -> p (r d)"), ident,
                       [("XIN", 0), "cmat"], [pk(b)])
                CP("dve", KcT[:, g, :], PS[b][:], [pk(b)], ["KcT"])
            MS("dve", Vc1[:, :, :, 64:65], 1.0, ["Vc1"])
            CP("dve", Vc1[:, :, :, 0:64], vst, [("XIN", 1)], ["Vc1"])

        _ck(3)
        LFn = SM[:, :, 0, :]
        BCn = SM[:, :, 1, :]
        LA = SM[:, :, 2, :]
        Aex = SM[:, :, 3, :]
        Bex = SM[:, :, 4, :]
        EBT = SM[:, :, 5, :]
        BTn = SM[:, :, 6, :]
        ACT(SM[:, :, 0, 0:8], G[:, :, 8:16], AF.Exp, ["G"], ["SM0"], scale=-1.0)
        ACT(SM[:, :, 0, 8:16], G[:, :, 24:32], AF.Exp, ["G"], ["SM0"], scale=-1.0)
        ACT(LFn, LFn, AF.Ln, ["SM0", "ones_f"], ["SM0"], bias=ones_f[:, 0:1])
        _ck(3.1)
        bg_ = S.ps()
        gv = PS[bg_][:, 0:256].rearrange("p (t x) -> p t x", t=8)
        for t in range(8):
            MM(gv[:, t, 0:8], triU, SM[:, t, 0, 0:8], True, True, ["cmat", "SM0"], [pk(bg_)])
            MM(gv[:, t, 8:16], triL, SM[:, t, 0, 8:16], True, True, ["cmat", "SM0"], [pk(bg_)])
            MM(gv[:, t, 16:32], ones_f[:], SM[:, t, 0, 0:16], True, True, ["ones_f", "SM0"], [pk(bg_)])
        _ck(3.2)
        CP("dve", BCn, gv[:, :, 0:16], [pk(bg_)], ["SM1"])
        CP("dve", BTn, gv[:, :, 16:32], [pk(bg_)], ["SM6"])
        ACT(EBT, gv[:, :, 16:32], AF.Exp, [pk(bg_)], ["SM5"], scale=-1.0)
        _ck(3.3)
        TT("dve", SM[:, :, 2, 0:8], G[:, :, 0:8], SM[:, :, 1, 0:8], ALU.add, ["G", "SM1"], ["SM2"])
        TT("dve", SM[:, :, 2, 8:16], G[:, :, 16:24], SM[:, :, 1, 8:16], ALU.add, ["G", "SM1"], ["SM2"])
        _ck(3.4)
        ACT(Aex, LA, AF.Exp, ["SM2"], ["SM3"])
        _ck(3.5)
        ACT(Bex, BCn, AF.Exp, ["SM1"], ["SM4"], scale=-1.0)
        ACT(SM[:, :, 7, :], BCn, AF.Exp, ["SM1"], ["SM7"])

        _ck(4)
        if not sample:
            bts = [S.ps(), S.ps()]
            for t in range(8):
                TR(PS[bts[t // 4]][0:16, (t % 4) * 128:(t % 4 + 1) * 128], SM[:, t, 2, :], ident, ["SM2", "cmat"], [pk(bts[t // 4])])
            mxT = sml[0:16, 0, :]

            def red(hf):
                src = PS[bts[hf]][0:16, :].rearrange("p (t x) -> p t x", t=4)
                dst = sml[0:16, 0, hf * 4:(hf + 1) * 4]
                S.add("dve", lambda e: e.tensor_reduce(out=dst, in_=src, axis=AX.X, op=ALU.max), [pk(bts[hf])], ["sml"])
            red(0)
            red(1)
            Dg = DgT[0:16, :].rearrange("p (t j) -> p t j", t=8)
            TT("dve", Dg, mxT.unsqueeze(2).to_broadcast([16, 8, 16]),
               cmat[0:16, 0, 0:16].unsqueeze(1).to_broadcast([16, 8, 16]), ALU.mult, ["sml", "cmat"], ["Dg"])
            bm_ = S.ps()
            MM(PS[bm_][:, 0:128], ones_f[0:16, :], Dg.rearrange("p t j -> p (t j)"), True, True, ["Dg", "ones_f"], [pk(bm_)])
            CP("dve", MXB[:, :, :].rearrange("p a b -> p (a b)"), PS[bm_][:, 0:128], [pk(bm_)], ["MXB"])

        _ck(5)
        S.barrier()

        def mlstm_gen():
            ACCd = [[0, 1], [4, 5]]
            ROTd = [[2, 3], [6, 7]]
            PmTd = [[XIN[d_][:, :].bitcast(BF16)[:, hg_ * 512:(hg_ + 1) * 512].rearrange("p (h x) -> p h x", h=4)
                     for hg_ in range(2)] for d_ in range(2)]
            Ctmp_d = [Ctmp, RS2[:, 0:260].rearrange("p (c e) -> p c e", c=4)]
            Ctmp2_d = [CoutS[:, 0], RS[0][:, 0:260].rearrange("p (c e) -> p c e", c=4)]
            KmTz = [Hf[:, 0:4096].rearrange("p (c x) -> p c x", c=4), Hf[:, 4096:8192].rearrange("p (c x) -> p c x", c=4)]
            MS("dve", Hf[64:128, 0:4096], 0.0, ["KmTz"])
            MS("dve", Hf[0:64, 4096:8192], 0.0, ["KmTz"])
            CP("dve", KmTz[0][0:64], KmT[0:64], [("KmT", 0), ("KmT", 1)], ["KmTz"])
            CP("act", KmTz[1][64:128], KmT[64:128], [("KmT", 0), ("KmT", 1)], ["KmTz"])
            written = set()
            bufi = [0]
            mfin = mrec[:, 0, :]
            em = mrec[:, 1, :]
            mt = mrec[:, 2, :]
            Cout = CoutS

            for s_ in range(nseq):
                t0 = s_ * tps
                if sample:
                    DMA("sp", Cst[:, 0, :, 0:64], scf_d.rearrange("(c p) d e -> (p d) c e", p=2), "const2", (), [("Cst", 0)])
                    DMA("sp", Cst[:, 1, :, 0:64], scb_d.rearrange("(c p) d e -> (p d) c e", p=2), "const2", (), [("Cst", 1)])
                    ACT(sml[:, 1, :].rearrange("p (d c) -> p d c", d=2), snm[:, :, 2:4].rearrange("p c d -> p d c"), AF.Exp,
                        ["snm"], ["sml1"])
                    emi = sml[:, 1, :].rearrange("p (d c) -> p d c", d=2)
                    for d_ in range(2):
                        TT("dve", Cst[:, d_, :, 0:64], Cst[:, d_, :, 0:64], emi[:, d_, :].unsqueeze(2).to_broadcast([128, 4, 64]),
                           ALU.mult, [("Cst", d_), "sml1"], [("Cst", d_)])
                        TT("dve", Cst[:, d_, :, 64], snm[:, :, d_], emi[:, d_, :], ALU.mult, ["snm", "sml1"], [("Cst", d_)])
                else:
                    for d_ in range(2):
                        MS("dve", Cst[:, d_], 0.0, [("Cst", d_)])
                for d_ in range(2):
                    CP("act", Cz[0:64, d_, 0:8:2, 0:65], Cst[0:64, d_], [("Cst", d_)], [("Cz", d_)])
                    CP("act", Cz[64:128, d_, 1:8:2, 0:65], Cst[64:128, d_], [("Cst", d_)], [("Cz", d_)])

                def unit(i, d_, t0=t0, s_=s_):
                    if True:
                        t = t0 + (i if d_ == 0 else tps - 1 - i)
                        tsl = slice(t * 128, (t + 1) * 128)
                        bi = d_
                        vpp = Vpp[bi]
                        ACC = ACCd[d_]
                        ROT = ROTd[d_]
                        TT("pool", vpp[:, :, 0:65], Vm1[:, t, :, 0:65], SM[:, t, 3, d_ * 8:(d_ + 1) * 8].unsqueeze(2).to_broadcast([128, 8, 65]),
                           ALU.mult, ["Vm1", "SM3"], [("Vpp", bi)])
                        yield
                        for hg in range(2):
                            bS = S.ps_from(ROT)
                            for hh in range(4):
                                h = hg * 4 + hh
                                c, p0 = h // 2, (h % 2) * 64
                                MM(PS[bS][:, hh * 128:(hh + 1) * 128], KmTz[h % 2][:, c, tsl], QmT[:, c, tsl],
                                   True, True, ["KmTz", ("QmT", t // 4)], [pk(bS)])
                            yield
                            pm = PmTd[d_][hg]
                            TT("dve", pm[:, :, :].rearrange("p h x -> p (h x)"), PS[bS][:],
                               mask4[:, d_].rearrange("p h x -> p (h x)"), ALU.mult, [pk(bS), "mask4"], [("PmT", d_, hg)])
                            yield
                            bA = ACC[hg]
                            for hh in range(4):
                                h = hg * 4 + hh
                                c, p0 = h // 2, (h % 2) * 64
                                MM(PS[bA][:, hh * 65:(hh + 1) * 65], pm[:, hh, :], vpp[:, h, 0:65], True, False,
                                   [("PmT", d_, hg), ("Vpp", bi)], [pk(bA)])
                                MM(PS[bA][:, hh * 65:(hh + 1) * 65], QmT[:, c, tsl], Cz[:, d_, h, 0:65],
                                   False, True, [("QmT", t // 4), ("Cz", d_)], [pk(bA)])
                            yield
                            acc = PS[bA][:, 0:260].rearrange("p (h e) -> p h e", h=4)
                            bsl = SM[:, t, 4, d_ * 8 + hg * 4:d_ * 8 + hg * 4 + 4]
                            den = sml[:, 2 + 2 * d_ + hg, 0:4]
                            rr = sml[:, 2 + 2 * d_ + hg, 4:8]
                            binv = SM[:, t, 7, d_ * 8 + hg * 4:d_ * 8 + hg * 4 + 4]
                            ACT(den, acc[:, :, 64], AF.Abs, [pk(bA)], [("den", d_, hg)])
                            TT("dve", den, den, binv, ALU.max, [("den", d_, hg), "SM7"], [("den", d_, hg)])
                            RCP(rr, den, [("den", d_, hg)], [("den", d_, hg)])
                            dst = concat[:, t, 512 + hg * 256:512 + (hg + 1) * 256].rearrange("p (h e) -> p h e", h=4)
                            if (t, hg) not in written:
                                written.add((t, hg))
                                TT("dve", dst, acc[:, :, 0:64], rr.unsqueeze(2).to_broadcast([128, 4, 64]), ALU.mult,
                                   [pk(bA), ("den", d_, hg)], [("hm", t)])
                            else:
                                tmp = Ctmp_d[d_][:, :, 0:64]
                                TT("dve", tmp, acc[:, :, 0:64], rr.unsqueeze(2).to_broadcast([128, 4, 64]), ALU.mult,
                                   [pk(bA), ("den", d_, hg)], [("Ctmp", d_)])
                                TT("pool", dst, dst, tmp, ALU.add, [("Ctmp", d_), ("hm", t)], [("hm", t)])
                            yield
                        last = (i == tps - 1)
                        if not (last and sample):
                            bE = S.ps_from(ROT)
                            bO = S.ps_from(ROT)
                            for c in range(4):
                                MM(PS[bE][:, c * 65:(c + 1) * 65], Kmtok[:, t, c * 128:(c + 1) * 128], vpp[:, 2 * c, 0:65], True, True,
                                   ["Kmtok", ("Vpp", bi)], [pk(bE)])
                                MM(PS[bO][:, c * 65:(c + 1) * 65], Kmtok[:, t, c * 128:(c + 1) * 128], vpp[:, 2 * c + 1, 0:65], True, True,
                                   ["Kmtok", ("Vpp", bi)], [pk(bO)])
                            yield
                            for par, bb in ((0, bE), (1, bO)):
                                pr = slice(par * 64, (par + 1) * 64)
                                dC = PS[bb][pr, 0:260].rearrange("p (c e) -> p c e", c=4)
                                ctk = ("Ctmp2", d_, par)
                                TT("dve", Ctmp2_d[d_][pr], dC, Cst[pr, d_], ALU.add, [pk(bb), ("Cst", d_)], [ctk])
                                ebt = SM[pr, t, 5, d_ * 8 + par:d_ * 8 + 8:2]
                                TT("pool", Cst[pr, d_], Ctmp2_d[d_][pr], ebt.unsqueeze(2).to_broadcast([64, 4, 65]), ALU.mult,
                                   [ctk, "SM5"], [("Cst", d_)])
                            CP("act", Cz[0:64, d_, 0:8:2, 0:65], Cst[0:64, d_], [("Cst", d_)], [("Cz", d_)])
                            CP("act", Cz[64:128, d_, 1:8:2, 0:65], Cst[64:128, d_], [("Cst", d_)], [("Cz", d_)])
                        yield
                        if last and not sample:
                            order = list(range(t0, t0 + tps)) if d_ == 0 else list(range(t0 + tps - 1, t0 - 1, -1))
                            cs = slice(d_ * 8, (d_ + 1) * 8)
                            mk = ("mfin", d_)
                            for n_, tt in enumerate(order):
                                if n_ == 0:
                                    TS("dve", mt[:, cs], MXB[:, tt, cs], 0.0, None, ALU.max, None, ["MXB"], [mk])
                                else:
                                    TT("dve", mt[:, cs], mfin[:, cs], MXB[:, tt, cs], ALU.max, ["MXB", mk], [mk])
                                TT("dve", mfin[:, cs], mt[:, cs], SM[:, tt, 6, cs], ALU.subtract, [mk, "SM6"], [mk])
                            ACT(em[:, cs], mfin[:, cs], AF.Exp, [mk], [mk], scale=-1.0)
                            md = nmf_d if d_ == 0 else nmb_d
                            DMA("sp", md[s_:s_ + 1, :], mfin[0:1, cs], "out", [mk], [])
                            for par in range(2):
                                pr = slice(par * 64, (par + 1) * 64)
                                emb = em[pr, d_ * 8 + par:d_ * 8 + 8:2]
                                TT("dve", Cout[pr, 1 + d_], Cst[pr, d_], emb.unsqueeze(2).to_broadcast([64, 4, 65]), ALU.mult,
                                   [("Cst", d_), mk], [("Cout", d_)])
                            cd = ncf_d if d_ == 0 else ncb_d
                            nd = nnf_d if d_ == 0 else nnb_d
                            DMA("sp", cd[s_].rearrange("(c p) d e -> (p d) c e", p=2), Cout[:, 1 + d_, :, 0:64], "out",
                                [("Cout", d_)], [])
                            DMA("sp", nd[s_].rearrange("(c p) d -> (p d) c", p=2), Cout[:, 1 + d_, :, 64], "out",
                                [("Cout", d_)], [], slow=True)

                def chain(d_):
                    for i_ in range(tps):
                        yield from unit(i_, d_)
                cs_ = [chain(0), chain(1)]
                while cs_:
                    for c_ in list(cs_):
                        try:
                            next(c_)
                        except StopIteration:
                            cs_.remove(c_)
            yield

        def attn_gen():
            ACC = [0, 1]
            ROT = [2, 3, 4, 5, 6, 7]
            S.mark('o_mlstm')
            i1 = wri[0]
            wri[0] = (wri[0] + 1) % 3
            kzk = ("WR", i1)
            KaTz = [WR[i1][:, p_ * 2048:(p_ + 1) * 2048].rearrange("p (g x) -> p g x", g=2) for p_ in range(2)]
            MS("dve", KaTz[0][64:128], 0.0, [kzk])
            MS("dve", KaTz[1][0:64], 0.0, [kzk])
            CP("dve", KaTz[0][0:64], KaT2[0:64], [("KaT2", 0), ("KaT2", 1)], [kzk])
            CP("act", KaTz[1][64:128], KaT2[64:128], [("KaT2", 0), ("KaT2", 1)], [kzk])
            if sample:
                i2 = wri[0]
                wri[0] = (wri[0] + 1) % 3
                kck = ("WR", i2)
                KcTz = [WR[i2][:, p_ * 1024:(p_ + 1) * 1024].rearrange("p (g x) -> p g x", g=2) for p_ in range(2)]
                MS("dve", KcTz[0][64:128], 0.0, [kck])
                MS("dve", KcTz[1][0:64], 0.0, [kck])
                CP("dve", KcTz[0][0:64], KcT[0:64], ["KcT"], [kck])
                CP("act", KcTz[1][64:128], KcT[64:128], ["KcT"], [kck])
            for s_ in range(nseq):
                if not sample:
                    for hg in range(2):
                        for hh in range(4):
                            h = hg * 4 + hh
                            g, c, p0 = h // 4, h // 2, (h % 2) * 64
                            bS = S.ps_from(ROT)
                            for kb in range(2):
                                MM(PS[bS][:, kb * 256:(kb + 1) * 256], KaTz[h % 2][:, g, (2 * s_ + kb) * 128:(2 * s_ + kb + 1) * 128],
                                   QaT[:, c, s_ * 256:(s_ + 1) * 256], True, True,
                                   [kzk, ("QaT", s_ // 2)], [pk(bS)])
                            pt = PTc[h % 2][:, 0:2, 0:256]
                            ACT(pt, PS[bS][:].rearrange("p (k x) -> p k x", k=2), AF.Exp, [pk(bS)], [("PTc", h % 2)], scale=ATT_SCALE)
                            for qt in range(2):
                                for kb in range(2):
                                    MM(PS[ACC[qt]][:, hh * 65:(hh + 1) * 65], pt[:, kb, qt * 128:(qt + 1) * 128],
                                       Va1[:, 2 * s_ + kb, g, 0:65], kb == 0, kb == 1, [("PTc", h % 2), "Va1"], [pk(ACC[qt])])
                        for qt in range(2):
                            acc = PS[ACC[qt]][:, 0:260].rearrange("p (h e) -> p h e", h=4)
                            den = sml[:, 4 + qt, 0:4]
                            TT("dve", den, acc[:, :, 64], esink[:, hg * 4:(hg + 1) * 4], ALU.add, [pk(ACC[qt]), "rowb"], [("aden", qt)])
                            RCP(den, den, [("aden", qt)], [("aden", qt)])
                            dst = concat[:, 2 * s_ + qt, hg * 256:(hg + 1) * 256].rearrange("p (h e) -> p h e", h=4)
                            TT("dve", dst, acc[:, :, 0:64], den.unsqueeze(2).to_broadcast([128, 4, 64]), ALU.mult,
                               [pk(ACC[qt]), ("aden", qt)], [("att", 2 * s_ + qt)])
                        yield
                else:
                    Qown = RS2[:, :].bitcast(BF16).rearrange("p (c x) -> p c x", c=4)
                    TS("dve", Qown, QaT[:, :, QS[0]], selv[:, 0:1], None, ALU.mult, None, [("QaT", 0), "selv"], ["Qown"])
                    for q in range(1, 4):
                        STT("dve", Qown, QaT[:, :, QS[q]], selv[:, q:q + 1], Qown, ALU.mult, ALU.add,
                            [("QaT", q // 2), "selv", "Qown"], ["Qown"])
                    mkb = [RS[0][:, :].bitcast(BF16), RS[1][:, :].bitcast(BF16)]
                    DMA("pool", mkb[0], mown_d[:, 0:1024], "const3", (), ["mkb"])
                    DMA("pool", mkb[1], mown_d[:, 1024:2048], "const3", (), ["mkb"])
                    PTl = XIN[0][:, :].bitcast(BF16).rearrange("p (j x) -> p j x", j=8)
                    PTo = XIN[1][:, :].bitcast(BF16)[:, 0:1024].rearrange("p (j x) -> p j x", j=4)
                    for hg in range(2):
                        for hh in range(4):
                            h = hg * 4 + hh
                            g, c = h // 4, h // 2
                            for pr_ in range(2):
                                bS = S.ps_from(ROT)
                                for kk in range(2):
                                    kb = 2 * pr_ + kk
                                    MM(PS[bS][:, kk * 256:(kk + 1) * 256], KcTz[h % 2][:, g, kb * 128:(kb + 1) * 128], Qown[:, c, :],
                                       True, True, [kck, "Qown"], [pk(bS)])
                                ACT(PTo[:, 2 * pr_:2 * pr_ + 2, :], PS[bS][:].rearrange("p (k x) -> p k x", k=2), AF.Exp,
                                    [pk(bS)], ["PTo"], scale=ATT_SCALE)
                            for pr_ in range(4):
                                bS = S.ps_from(ROT)
                                for kk in range(2):
                                    j = 2 * pr_ + kk
                                    MM(PS[bS][:, kk * 256:(kk + 1) * 256], KaTz[h % 2][:, g, j * 128:(j + 1) * 128], Qown[:, c, :],
                                       True, True, [kzk, "Qown"], [pk(bS)])
                                ACT(PTl[:, 2 * pr_:2 * pr_ + 2, :], PS[bS][:].rearrange("p (k x) -> p k x", k=2), AF.Exp,
                                    [pk(bS)], [("PTl", pr_ // 2)], scale=ATT_SCALE)
                            for m_ in range(2):
                                pv = PTl[:, 4 * m_:4 * m_ + 4, :].rearrange("p j x -> p (j x)")
                                TT("dve", pv, pv, mkb[m_], ALU.mult, [("PTl", m_), "mkb"], [("PTl", m_)])
                            for a_ in range(2):
                                mms = []
                                for kb in range(4):
                                    mms.append((PTo[:, kb, a_ * 128:(a_ + 1) * 128], Vc1[:, kb, g, 0:65], ["PTo", "Vc1"]))
                                for j in range(8):
                                    mms.append((PTl[:, j, a_ * 128:(a_ + 1) * 128], Va1[:, j, g, 0:65], [("PTl", j // 4), "Va1"]))
                                for n_, (l_, r_, k_) in enumerate(mms):
                                    MM(PS[ACC[a_]][:, hh * 65:(hh + 1) * 65], l_, r_, n_ == 0, n_ == len(mms) - 1, k_, [pk(ACC[a_])])
                        for a_ in range(2):
                            acc = PS[ACC[a_]][:, 0:260].rearrange("p (h e) -> p h e", h=4)
                            den = sml[:, 4 + a_, 0:4]
                            TT("dve", den, acc[:, :, 64], esink[:, hg * 4:(hg + 1) * 4], ALU.add, [pk(ACC[a_]), "rowb"], [("aden", a_)])
                            RCP(den, den, [("aden", a_)], [("aden", a_)])
                            dst = concat[:, a_, hg * 256:(hg + 1) * 256].rearrange("p (h e) -> p h e", h=4)
                            TT("dve", dst, acc[:, :, 0:64], den.unsqueeze(2).to_broadcast([128, 4, 64]), ALU.mult,
                               [pk(ACC[a_]), ("aden", a_)], [("att", a_)])
                        yield
            yield

        for _ in mlstm_gen():
            pass
        S.barrier()
        for _ in attn_gen():
            pass
        _ck(7)
        S.barrier()

        S.mark('o_attn')
        ntl = 2 if sample else 8
        if sample:
            for buf, lo, kf in ((concat, 512, lambda t_: [("hm", t_)]), (SG, 0, lambda t_: ["SG"])):
                d0 = buf[:, 0:2, lo:lo + 512]
                TS("dve", d0, d0, selv[:, 0:1], None, ALU.mult, None, kf(0) + kf(1) + ["selv"], kf(0) + kf(1))
                for q in range(1, 4):
                    STT("dve", d0, buf[:, 2 * q:2 * q + 2, lo:lo + 512], selv[:, q:q + 1], d0, ALU.mult, ALU.add,
                        kf(2 * q) + kf(2 * q + 1) + kf(0) + kf(1) + ["selv"], kf(0) + kf(1))
        sqs = BIG[:, 7200:15392].bitcast(F32)
        ssq = MXB[:, :, :].rearrange("p a b -> p (a b)")[:, 0:64]
        for t in range(ntl):
            ACT(sqs[:, t * 512:(t + 1) * 512], concat[:, t, 512:1024], AF.Square, [("hm", t)], ["sqs"])
        ssq = ssq[:, 0:ntl * 8]
        sqv = sqs[:, 0:ntl * 512].rearrange("p (a b) -> p a b", b=64)
        S.add("dve", lambda e: e.tensor_reduce(out=ssq, in_=sqv, axis=AX.X, op=ALU.add), ["sqs"], ["ssq"])
        ACT(ssq, ssq, AF.Sqrt, ["ssq", "epsc"], ["ssq"], bias=epsc[:, 0:1], scale=1.0 / 64)
        RCP(ssq, ssq, ["ssq"], ["ssq"])
        for t in range(ntl):
            hmv = concat[:, t, 512:1024].rearrange("p (h e) -> p h e", h=8)
            TT("dve", hmv, hmv, ssq[:, t * 8:(t + 1) * 8].unsqueeze(2).to_broadcast([128, 8, 64]), ALU.mult,
               [("hm", t), "ssq"], [("hm", t)])
            TT("dve", concat[:, t, 512:1024], concat[:, t, 512:1024], SG[:, t, :], ALU.mult, [("hm", t), "SG"], [("hm", t)])
        for t in range(ntl):
            for hf in range(2):
                b = S.ps()
                for j in range(4):
                    kc = hf * 4 + j
                    TR(PS[b][:, j * 128:(j + 1) * 128], concat[:, t, kc * 128:(kc + 1) * 128], ident,
                       ["cmat", ("hm", t), ("att", t)], [pk(b)])
                CP(ev_eng(), H[:, hf * 4:hf * 4 + 4, t * 128:(t + 1) * 128], PS[b][:].rearrange("p (a b) -> p a b", b=128),
                   [pk(b)], kH(t // 4))
        S.barrier()

        if sample:
            for buf, keyf in ((X, xkeys),):
                TS("dve", buf[:, :, QS[0]], buf[:, :, QS[0]], selv[:, 0:1], None, ALU.mult, None, keyf(0) + ["selv"], keyf(0))
                for q in range(1, 4):
                    STT("dve", buf[:, :, QS[0]], buf[:, :, QS[q]], selv[:, q:q + 1], buf[:, :, QS[0]], ALU.mult, ALU.add,
                        keyf(q) + keyf(0) + ["selv"], keyf(0))
            WIN[0], WIN[1] = 1, 256

        def epiO(ci, tb, b):
            CP(ev_eng(), Y[:, ci, wsl(tb)], PS[b][:, 0:WIN[1]], [pk(b)], kY(tb))

        proj_fm(oout_d, 8, [[(c * 128, 128)] for c in range(8)], srcH, kH, epiO)
        S.mark('o_outproj')

    for half in range(2):
        cond = half
        nseq = 4 if half == 0 else 1
        load_x(xp_d if half == 0 else xs_d)
        S.mark('x_loaded')
        if half == 0:
            g0 = mod_steps(0)
            for _ in range(4):
                next(g0)
            bg[0] = g0
            bgdone[0] = 4
            bg_every[0] = 5
        junction(None, (0, 0), cond)
        even_layer(cond, nseq)
        if half == 0:
            bg_flush()
            half_is0[0] = False
            bg[0] = mod_steps(1)
            bgdone[0] = 0
            bg_every[0] = 2
        junction((0, 0, True), (0, 1), cond)
        mlp(0, cond)
        bg_flush()
        junction((0, 1, True), (1, 0), cond)
        odd_layer(half, cond)
        own = (half == 1)
        junction((1, 0, False), (1, 1), cond, qs=(0,) if own else (0, 1, 2, 3))
        mlp(1, cond)
        junction((1, 1, True), None, cond, qs=(0,) if own else (0, 1, 2, 3))
        store_x(yp_d if half == 0 else ys_d, 2 if own else 8)
        WIN[0], WIN[1] = 2, 512
        S.barrier()

    S.emit(nc, ["out"])
    es.close()
    return nc


_CACHE = {}


def _consts():
    idn = np.eye(128, dtype=np.float32)
    u = np.arange(128)
    triU = (u[:, None] <= u[None, :]).astype(np.float32)
    triL = (u[:, None] >= u[None, :]).astype(np.float32)
    perm = np.zeros((128, 128), dtype=np.float32)
    for d in range(128):
        if d % 64 < 32:
            perm[d + 32, d] = -1.0
        else:
            perm[d - 32, d] = 1.0
    cmat = np.stack([idn, triU, triL, perm], axis=1).astype(np.float32)
    t = np.arange(1024)
    row = (t // 64).astype(np.float32)
    col = (t % 64).astype(np.float32)
    inv = (10000.0 ** (-np.arange(16, dtype=np.float32) / 16)).astype(np.float32)
    ang = np.concatenate([row[:, None] * inv, col[:, None] * inv], axis=-1).astype(np.float32)
    cos = np.cos(ang).astype(np.float32).T
    sin = np.sin(ang).astype(np.float32).T
    cosT = np.concatenate([cos, cos, cos, cos], axis=0)
    sinT = np.concatenate([sin, sin, sin, sin], axis=0)
    return cmat, np.ascontiguousarray(cosT), np.ascontiguousarray(sinT)


def _maskown(q):
    k = np.arange(128)[:, None]
    qq = np.arange(128)[None, :]
    m = np.zeros((128, 8, 2, 128), dtype=np.float32)
    for a in range(2):
        i = 2 * q + a
        for j in range(8):
            if j == i:
                m[:, j, a, :] = 1.0
            elif j == i - 1:
                m[:, j, a, :] = (k >= qq)
            elif j == i + 1:
                m[:, j, a, :] = (k <= qq)
    return np.ascontiguousarray(m.reshape(128, 2048))


def fm(v):
    v = np.asarray(v, dtype=np.float32)
    lead = v.shape[:-1]
    n = v.shape[-1] // 128
    r = v.reshape(lead + (n, 128))
    r = np.moveaxis(r, -1, 0)
    return np.ascontiguousarray(r)


def kernel(**inp):
    f = lambda k: np.ascontiguousarray(np.asarray(inp[k], dtype=np.float32))
    if "nc" not in _CACHE:
        _CACHE["nc"] = build_program()
    nc = _CACHE["nc"]
    cmat, cosT, sinT = _consts()
    x_prompt, x_sample = f("x_prompt"), f("x_sample")
    c, c_ctx = f("c"), f("c_ctx")
    mod_w, mod_b, norm_g = f("mod_w"), f("mod_b"), f("norm_g")
    modbT = np.ascontiguousarray(fm(mod_b))
    normgT = np.ascontiguousarray(fm(norm_g))
    convaT = np.ascontiguousarray(f("conv_a_w")[0].T.reshape(4, 128, 31).transpose(1, 0, 2))
    convbT = np.ascontiguousarray(f("conv_b_w")[0].T.reshape(4, 128, 3).transpose(1, 0, 2))
    evec = np.ascontiguousarray(np.stack([fm(f("conv_a_b")[0]), fm(f("ln_a_g")[0]), fm(f("ln_a_b")[0])], axis=1))
    rowv = np.concatenate([f("attn_sink")[0], f("gate_b")[0].reshape(-1), f("hnorm_g")[0]])[None, :].astype(np.float32)
    in_maps = []
    for core in range(8):
        b = core // 4
        condT = np.ascontiguousarray(np.stack([fm(c_ctx), fm(c[b])], axis=-1))

        def exp4(v):
            return np.repeat(v.reshape(4, 2).T, 64, axis=0)

        def n4(v):
            return v.reshape(4, 2, 64).transpose(1, 2, 0).reshape(128, 4)

        snm = np.stack([n4(f("state_n_fwd")[b, 0]), n4(f("state_n_bwd")[b, 0]),
                        exp4(f("state_m_fwd")[b, 0]), exp4(f("state_m_bwd")[b, 0])], axis=-1)
        in_maps.append({
            "xp": np.ascontiguousarray(x_prompt[core * 4:(core + 1) * 4].reshape(1024, 1024)),
            "xs": x_sample[b],
            "condT": condT, "mod_w": mod_w, "modbT": modbT, "normgT": normgT,
            "mlp_w1": f("mlp_w1"), "mlp_w2": f("mlp_w2"),
            "even_in_w": f("even_in_w")[0], "even_out_w": f("even_out_w")[0],
            "odd_in_w": f("odd_in_w")[0], "odd_out_w": f("odd_out_w")[0],
            "convaT": convaT, "convbT": convbT, "evec": evec, "rowv": rowv,
            "cache_k": f("cache_k")[b, 0], "cache_v": f("cache_v")[b, 0],
            "st_c_f": f("state_c_fwd")[b, 0], "st_c_b": f("state_c_bwd")[b, 0],
            "st_nm": np.ascontiguousarray(snm.astype(np.float32)),
            "cosT": cosT, "sinT": sinT, "cmat": cmat,
            "selv": np.ascontiguousarray(np.tile(np.eye(4, dtype=np.float32)[core % 4][None, :], (128, 1))),
            "maskown": _maskown(core % 4),
        })
    res = run_bass_kernel_spmd(nc, in_maps[:NCORES], core_ids=list(range(NCORES)))
    R = list(res.results) + [res.results[0]] * (8 - NCORES)
    yp = np.concatenate([R[i]["yp"].reshape(4, 256, 1024) for i in range(8)], axis=0)
    ys = np.stack([np.concatenate([R[4 * b_ + q_]["ys"] for q_ in range(4)], axis=0) for b_ in range(2)], axis=0)
    cat = lambda k: np.concatenate([R[i][k] for i in range(8)], axis=0)
    nk = cat("nk")[:, None]
    nv = cat("nv")[:, None]
    return (yp, ys, nk, nv, cat("ncf")[:, None], cat("nnf")[:, None], cat("nmf")[:, None],
            cat("ncb")[:, None], cat("nnb")[:, None], cat("nmb")[:, None])
```

```python
import os
import numpy as np
import concourse.bass as bass
import concourse.mybir as mybir
from concourse.bass_utils import run_bass_kernel_spmd

F32 = mybir.dt.float32
BF16 = mybir.dt.bfloat16
AF = mybir.ActivationFunctionType
ALU = mybir.AluOpType
AX = mybir.AxisListType

D = 1024
T = 1024
EPS = 1e-6
SAME_ENG_SYNC = True
STAGE = 99
SUB = 99
NCORES = 8


class _Stop(Exception):
    pass


def _ck(k):
    if SUB <= k:
        raise _Stop()


class Op:
    __slots__ = ("eng", "fn", "deps", "dma", "dcount", "dwaits", "tick", "inc", "idx")


class Sched:
    ENGS = ("pe", "act", "dve", "pool", "sp")

    def __init__(self):
        self.ops = []
        self.lastw = {}
        self.readers = {}
        self.dcount = {}
        self.pool_dmas = []
        self.psi = 0
        self.bar_deps = set()
        self.bar_dw = {}
        self.rot = {}
        self.marks = []
        self.multiw = {}

    def add(self, eng, fn, r=(), w=(), dma=None):
        op = Op()
        op.eng, op.fn, op.dma = eng, fn, dma
        op.idx = len(self.ops)
        deps = set()
        for k in r:
            if k in self.lastw:
                deps.add(self.lastw[k])
            if k in self.multiw:
                deps.update(self.multiw[k])
            if isinstance(k, tuple) and k[0] == "ps":
                for ridx in self.readers.get(k, ()):
                    if self.ops[ridx].eng != eng:
                        deps.add(ridx)
        for k in w:
            if k in self.lastw:
                deps.add(self.lastw[k])
            if k in self.multiw:
                deps.update(self.multiw.pop(k))
            deps.update(self.readers.get(k, ()))
        if dma is not None and eng == "pool":
            if len(self.pool_dmas) >= 4:
                deps.add(self.pool_dmas[-4])
            self.pool_dmas.append(op.idx)
        deps.update(self.bar_deps)
        deps.discard(op.idx)
        op.deps = deps
        op.dwaits = dict(self.bar_dw)
        for d in deps:
            P = self.ops[d]
            if P.dma is not None:
                op.dwaits[P.dma] = self.dcount[P.dma]
        if dma is not None:
            self.dcount[dma] = self.dcount.get(dma, 0) + 16
            op.dcount = self.dcount[dma]
        op.tick = 0
        op.inc = False
        for k in r:
            self.readers.setdefault(k, []).append(op.idx)
        for k in w:
            self.lastw[k] = op.idx
            self.readers[k] = []
        self.ops.append(op)
        return op

    def mark(self, name):
        self.marks.append((name, sum(1 for o in self.ops if o.eng == 'pe')))

    def ps(self):
        i = self.psi
        self.psi = (self.psi + 1) % 7
        return i

    def ps_from(self, lst):
        k = tuple(lst)
        j = self.rot.get(k, 0)
        self.rot[k] = (j + 1) % len(lst)
        return lst[j]

    def alias(self, fine_keys, coarse):
        idxs = [self.lastw[k] for k in fine_keys if k in self.lastw]
        for k in fine_keys:
            idxs.extend(self.readers.get(k, ()))
        self.multiw.setdefault(coarse, set()).update(idxs)
        self.readers.setdefault(coarse, [])

    def barrier(self):
        last = {}
        for op in self.ops:
            if op.dma is None:
                last[op.eng] = op.idx
        self.bar_deps = set(last.values())
        self.bar_dw = dict(self.dcount)

    def emit(self, nc, final_waits):
        ops = self.ops
        for op in ops:
            for d in op.deps:
                P = ops[d]
                if P.dma is not None:
                    continue
                if P.eng == op.eng and (P.eng == "pe" or not SAME_ENG_SYNC):
                    continue
                P.inc = True
        cnt = {e: 0 for e in self.ENGS}
        for op in ops:
            if op.dma is None and op.inc:
                cnt[op.eng] += 1
                op.tick = cnt[op.eng]
        nops = {e: sum(1 for o in ops if o.eng == e) for e in self.ENGS}
        import contextlib
        with contextlib.ExitStack() as es:
            esem = {e: es.enter_context(nc.semaphore("e_" + e)) for e in self.ENGS}
            dsem = {k: es.enter_context(nc.semaphore("d_" + str(k))) for k in self.dcount}
            block = es.enter_context(nc.Block())

            def run(eng_name):
                def body(e):
                    waited = {}
                    for op in ops:
                        if op.eng != eng_name:
                            continue
                        waits = {}
                        for d in op.deps:
                            P = ops[d]
                            if P.dma is not None:
                                continue
                            if P.eng == op.eng and (P.eng == "pe" or not SAME_ENG_SYNC):
                                continue
                            key = ("e", P.eng)
                            waits[key] = max(waits.get(key, 0), P.tick)
                        for k, v in op.dwaits.items():
                            waits[("d", k)] = max(waits.get(("d", k), 0), v)
                        for key, v in waits.items():
                            if waited.get(key, 0) < v:
                                sem = esem[key[1]] if key[0] == "e" else dsem[key[1]]
                                e.wait_ge(sem, v)
                                waited[key] = v
                        inst = op.fn(e)
                        if op.dma is not None:
                            inst.then_inc(dsem[op.dma], 16)
                        elif op.inc:
                            inst.then_inc(esem[op.eng], 1)
                    if eng_name == "sp":
                        for k in final_waits:
                            if k in self.dcount:
                                e.wait_ge(dsem[k], self.dcount[k])
                return body

            block.tensor(run("pe"))
            block.scalar(run("act"))
            block.vector(run("dve"))
            block.gpsimd(run("pool"))
            block.sync(run("sp"))


def build_program():
    nc = bass.Bass("TRN2", target_bir_lowering=False)
    S = Sched()

    def din(name, shape, dt=F32):
        return nc.dram_tensor(name, list(shape), dt, kind="ExternalInput").ap()

    def dout(name, shape):
        return nc.dram_tensor(name, list(shape), F32, kind="ExternalOutput").ap()

    xp_d = din("xp", [1024, 1024])
    xs_d = din("xs", [1024, 1024])
    condT_d = din("condT", [128, 8, 2])
    modw_d = din("mod_w", [2, 1024, 6144])
    modbT_d = din("modbT", [128, 2, 48])
    normgT_d = din("normgT", [128, 2, 4, 8])
    w1_d = din("mlp_w1", [2, 1024, 4096])
    w2_d = din("mlp_w2", [2, 4096, 1024])
    ein_d = din("even_in_w", [1024, 2560])
    eout_d = din("even_out_w", [1024, 1024])
    oin_d = din("odd_in_w", [1024, 2848])
    oout_d = din("odd_out_w", [1024, 1024])
    convaT_d = din("convaT", [128, 4, 31])
    convbT_d = din("convbT", [128, 4, 3])
    evec_d = din("evec", [128, 3, 4])
    rowv_d = din("rowv", [1, 552])
    ck_d = din("cache_k", [2, 512, 64])
    cv_d = din("cache_v", [2, 512, 64])
    scf_d = din("st_c_f", [8, 64, 64])
    scb_d = din("st_c_b", [8, 64, 64])
    snm_d = din("st_nm", [128, 4, 4])
    cosT_d = din("cosT", [128, 1024])
    sinT_d = din("sinT", [128, 1024])
    sel_d = din("selv", [128, 4])
    mown_d = din("maskown", [128, 2048])
    cmat_d = din("cmat", [128, 4, 128])

    yp_d = dout("yp", [1024, 1024])
    ys_d = dout("ys", [256, 1024])
    nk_d = dout("nk", [4, 2, 256, 64])
    nv_d = dout("nv", [4, 2, 256, 64])
    ncf_d = dout("ncf", [4, 8, 64, 64])
    nnf_d = dout("nnf", [4, 8, 64])
    nmf_d = dout("nmf", [4, 8])
    ncb_d = dout("ncb", [4, 8, 64, 64])
    nnb_d = dout("nnb", [4, 8, 64])
    nmb_d = dout("nmb", [4, 8])

    import contextlib
    es = contextlib.ExitStack()

    def sb(name, shape, dt=F32):
        return es.enter_context(nc.sbuf_tensor(name, list(shape), dt))

    X = sb("X", [128, 8, T])
    H = sb("H", [128, 8, T], BF16)
    Y = sb("Y", [128, 8, T])
    BIG = sb("BIG", [128, 32768], BF16)
    WR = [sb("WR%d" % i, [128, 4096], BF16) for i in range(3)]
    XIN = [sb("XIN%d" % i, [128, 1024]) for i in range(2)]
    MR = [XIN[i][:, :].rearrange("p (k c) -> p k c", c=128) for i in range(2)]
    RS = [sb("RS%d" % i, [128, 512]) for i in range(2)]
    RS2 = sb("RS2", [128, 512])
    cmat = sb("cmat_sb", [128, 4, 128])
    ident = cmat[:, 0, :]
    triU = cmat[:, 1, :]
    triL = cmat[:, 2, :]
    cbf = sb("cbf", [128, 4, 128], BF16)
    ones_f = sb("ones_f", [128, 128])
    mask4 = sb("mask4", [128, 2, 4, 128], BF16)
    epsc = sb("epsc", [128, 1])
    condT = sb("condT_sb", [128, 8, 2])
    modb = sb("modb", [128, 2, 48])
    normg = sb("normg", [128, 2, 4, 8])
    modsb = sb("modsb", [128, 2, 48, 2])
    dvec = sb("dvec", [128, 2, 6, 8, 2])
    convaT = sb("convaT_sb", [128, 4, 31])
    convbT = sb("convbT_sb", [128, 4, 3])
    evec = sb("evec_sb", [128, 3, 4])
    rowb = sb("rowb", [128, 552])
    snm = sb("snm_sb", [128, 4, 4])
    selv = sb("selv_sb", [128, 4])
    SM = sb("SM", [128, 8, 8, 16])
    Cst = sb("Cst", [128, 2, 4, 65])
    Cz = sb("Cz", [128, 2, 8, 66], BF16)
    Ctmp = sb("Ctmp", [128, 4, 65])
    MXB = sb("MXB", [128, 8, 16])
    mrec = sb("mrec", [128, 4, 16])
    CoutS = sb("CoutS", [128, 3, 4, 65])
    DgT = sb("DgT", [128, 128])
    sml = sb("sml", [128, 8, 8])
    d3 = BIG[:, 30240:31776].rearrange("p (k m) -> p k m", m=128)

    PS = [es.enter_context(nc.psum_tensor("ps%d" % i, [128, 512], F32)) for i in range(8)]

    def pk(i):
        return ("ps", i)

    def MM(out, lhsT, rhs, start, stop, r, w, tp=None):
        if False:
            return S.add("pe", lambda e: e.matmul(out, lhsT=lhsT, rhs=rhs, start=start, stop=stop, tile_position=tp), r, w)
        return S.add("pe", lambda e: e.matmul(out, lhsT=lhsT, rhs=rhs, start=start, stop=stop), r, w)

    def TR(out, in_, idn, r, w):
        return S.add("pe", lambda e: e.transpose(out=out, in_=in_, identity=idn), r, w)

    def ACT(out, in_, func, r, w, bias=None, scale=None, eng="act"):
        kw = {}
        if bias is not None:
            kw["bias"] = bias
        if scale is not None:
            kw["scale"] = scale
        return S.add("act", lambda e: e.activation(out=out, in_=in_, func=func, **kw), r, w)

    def TT(eng, out, in0, in1, op, r, w):
        return S.add(eng, lambda e: e.tensor_tensor(out=out, in0=in0, in1=in1, op=op), r, w)

    def TS(eng, out, in0, s1, s2, op0, op1, r, w):
        if s2 is None:
            return S.add(eng, lambda e: e.tensor_scalar(out=out, in0=in0, scalar1=s1, scalar2=None, op0=op0), r, w)
        return S.add(eng, lambda e: e.tensor_scalar(out=out, in0=in0, scalar1=s1, scalar2=s2, op0=op0, op1=op1), r, w)

    def STT(eng, out, in0, sc, in1, op0, op1, r, w):
        return S.add(eng, lambda e: e.scalar_tensor_tensor(out=out, in0=in0, scalar=sc, in1=in1, op0=op0, op1=op1), r, w)

    def CP(eng, out, in_, r, w):
        if eng == "act":
            return S.add("act", lambda e: e.activation(out=out, in_=in_, func=AF.Copy), r, w)
        return S.add(eng, lambda e: e.tensor_copy(out=out, in_=in_), r, w)

    def MS(eng, ap, val, w):
        return S.add(eng, lambda e: e.memset(ap, val), (), w)

    def RCP(out, in_, r, w):
        return S.add("dve", lambda e: e.reciprocal(out=out, in_=in_), r, w)

    def DMA(eng, out, in_, dkey, r, w, slow=False):
        if slow:
            return S.add(eng, lambda e: e.dma_start(out=out, in_=in_, allow_slow_non_contiguous=True), r, w, dma=dkey)
        return S.add(eng, lambda e: e.dma_start(out=out, in_=in_), r, w, dma=dkey)

    alt = [0]

    def ev_eng():
        alt[0] ^= 1
        return "act" if alt[0] else "dve"

    DMA("sp", cmat[:], cmat_d, "const", (), ["cmat"])
    DMA("sp", condT[:], condT_d, "const", (), ["condT"])
    DMA("sp", modb[:], modbT_d, "const", (), ["modb"])
    DMA("sp", normg[:], normgT_d, "const", (), ["normg"])
    DMA("sp", convaT[:], convaT_d, "const", (), ["convaT"])
    DMA("sp", convbT[:], convbT_d, "const", (), ["convbT"])
    DMA("sp", evec[:], evec_d, "const", (), ["evec"])
    DMA("sp", rowb[0:1, :], rowv_d, "const", (), ["rowv"])
    DMA("sp", snm[:], snm_d, "const", (), ["snm"])
    DMA("sp", selv[:], sel_d, "const", (), ["selv"])
    MS("dve", ones_f[:], 1.0, ["ones_f"])
    MS("dve", Cz[:], 0.0, [("Cz", 0), ("Cz", 1)])
    MS("dve", epsc[:], EPS, ["epsc"])
    MS("dve", cbf[:, 3, :], 1.0, ["cbf"])
    CP("dve", cbf[:, 0:3, :], cmat[:, 0:3, :], ["cmat"], ["cbf"])
    for k in range(4):
        CP("dve", mask4[:, 0, k, :], cmat[:, 1, :], ["cmat"], ["mask4"])
        CP("dve", mask4[:, 1, k, :], cmat[:, 2, :], ["cmat"], ["mask4"])
    ones_b = cbf[:, 3, :]
    ident_b = cbf[:, 0, :]
    scTb = sb("scTb", [128, 8, 2], BF16)
    ACT(scTb[:], condT[:], AF.Silu, ["condT"], ["scT"])
    for c0 in (0, 276):
        MM(PS[7][:, 0:276], ones_f[0:1, :], rowb[0:1, c0:c0 + 276], True, True, ["ones_f", "rowv"], [pk(7)])
        CP("dve", rowb[:, c0:c0 + 276], PS[7][:, 0:276], [pk(7)], ["rowb", "rowv"])
    ACT(rowb[:, 0:8], rowb[:, 0:8], AF.Exp, ["rowb"], ["rowb"])
    esink = rowb[:, 0:8]
    gateb = rowb[:, 8:40]
    hng = rowb[:, 40:552]

    def mod_steps(l):
        for cb in range(12):
            view, wkey = load_slab(modw_d[l], 8, [(0, cb * 512, 512)])
            b = S.ps()
            for kc in range(8):
                MM(PS[b][0:2, :], scTb[:, kc, :], view[:, kc, :], kc == 0, kc == 7, [wkey, "scT"], [pk(b)])
            mrow = RS2
            CP("dve", mrow[0:2, :], PS[b][0:2, :], [pk(b)], ["RS2"])
            for j in range(4):
                oc = cb * 4 + j
                MM(PS[7][:, oc * 2:oc * 2 + 2], mrow[0:2, j * 128:(j + 1) * 128], ident[0:2, 0:2], True, True,
                   ["RS2", "cmat"], [pk(7)])
            if cb % 2 == 1:
                grp = cb // 2
                TT("dve", modsb[:, l, grp * 8:(grp + 1) * 8, :],
                   PS[7][:, grp * 16:(grp + 1) * 16].rearrange("p (a b) -> p a b", b=2),
                   modb[:, l, grp * 8:(grp + 1) * 8].unsqueeze(2).to_broadcast([128, 8, 2]), ALU.add,
                   [pk(7), "modb"], [("modsb", l, grp)])
                src = modsb[:, l, grp * 8:(grp + 1) * 8, :]
                if grp in (0, 3):
                    j = 1 if grp == 0 else 4
                    CP("dve", dvec[:, l, j, :, :], src, [("modsb", l, grp)], [("dvec", l, j)])
                elif grp in (1, 4):
                    j = 0 if grp == 1 else 3
                    gi = 0 if grp == 1 else 2
                    TS("dve", dvec[:, l, j, :, :], src, 1.0, None, ALU.add, None, [("modsb", l, grp)], [("dvec", l, j)])
                    TT("dve", dvec[:, l, j, :, :], dvec[:, l, j, :, :],
                       normg[:, l, gi, :].unsqueeze(2).to_broadcast([128, 8, 2]), ALU.mult, [("dvec", l, j), "normg"],
                       [("dvec", l, j)])
                else:
                    j = 2 if grp == 2 else 5
                    gi = 1 if grp == 2 else 3
                    TT("dve", dvec[:, l, j, :, :], src, normg[:, l, gi, :].unsqueeze(2).to_broadcast([128, 8, 2]), ALU.mult,
                       [("modsb", l, grp), "normg"], [("dvec", l, j)])
            yield

    def run_all(gen):
        for _ in gen:
            pass

    bg = [None]
    half_is0 = [True]
    bgcnt = [0]
    bg_every = [1]

    def bg_step(n=1):
        if bg[0] is None:
            return
        bgcnt[0] += 1
        if bgcnt[0] % bg_every[0] != 0:
            return
        bg_force(n)

    bgdone = [0]

    def bg_force(n=1):
        for _ in range(n):
            if bg[0] is None:
                return
            try:
                next(bg[0])
                bgdone[0] += 1
            except StopIteration:
                bg[0] = None
                return

    def bg_ensure(n_done):
        while bg[0] is not None and bgdone[0] < n_done:
            bg_force(1)

    def bg_flush():
        if bg[0] is not None:
            run_all(bg[0])
            bg[0] = None

    wri = [0]

    def load_slab(Wd, KC, pieces):
        ncol = max(p[0] + p[2] for p in pieces)
        i = wri[0]
        wri[0] = (wri[0] + 1) % 3
        view = WR[i][:, 0:KC * ncol].rearrange("p (k c) -> p k c", c=ncol)
        for (off, c0, wd) in pieces:
            kstep = max(1, min(KC, 2048 // wd))
            for k0 in range(0, KC, kstep):
                DMA("pool", view[:, k0:k0 + kstep, off:off + wd],
                    Wd[k0 * 128:(k0 + kstep) * 128, c0:c0 + wd].rearrange("(kc p) c -> p kc c", p=128),
                    "wr%d" % i, (), [("WR", i)])
        return view, ("WR", i)

    WIN = [2, 512]

    def wsl(tb):
        return slice(tb * WIN[1], (tb + 1) * WIN[1])

    def kX(tb):
        return [("X", t) for t in range(4 * tb, 4 * tb + 4)]

    def kH(tb):
        if WIN[1] == 256:
            return [("H", tb)]
        return [("H", 2 * tb), ("H", 2 * tb + 1)]

    def kY(tb):
        if WIN[1] == 256:
            return [("Y", tb)]
        return [("Y", 2 * tb), ("Y", 2 * tb + 1)]

    def load_x(xd):
        for t in range(8):
            xin = XIN[t % 2]
            DMA("sp", xin[:], xd[t * 128:(t + 1) * 128, :], "xin%d" % (t % 2), (), [("XIN", t % 2)])
            for hf in range(2):
                b = S.ps()
                for j in range(4):
                    kc = hf * 4 + j
                    TR(PS[b][:, j * 128:(j + 1) * 128], xin[:, kc * 128:(kc + 1) * 128], ident,
                       [("XIN", t % 2), "cmat"], [pk(b)])
                CP(ev_eng(), X[:, hf * 4:hf * 4 + 4, t * 128:(t + 1) * 128],
                   PS[b][:].rearrange("p (a b) -> p a b", b=128), [pk(b)], [("X", t)])

    def store_x(yd, ntiles=8):
        for t in range(ntiles):
            xin = XIN[t % 2]
            for hf in range(2):
                b = S.ps()
                for j in range(4):
                    kc = hf * 4 + j
                    TR(PS[b][:, j * 128:(j + 1) * 128], X[:, kc, t * 128:(t + 1) * 128], ident,
                       [("X", t), "cmat"], [pk(b)])
                CP(ev_eng(), xin[:, hf * 512:(hf + 1) * 512], PS[b][:], [pk(b)], [("XIN", t % 2)])
            DMA("sp", yd[t * 128:(t + 1) * 128, :], xin[:], "out", [("XIN", t % 2)], [])

    def rstd_from_ps(b, rs, scale, rkey):
        ACT(rs[:], PS[b][:], AF.Sqrt, [pk(b), "epsc"], [rkey], bias=epsc[:, 0:1], scale=scale)
        RCP(rs[:], rs[:], [rkey], [rkey])

    def rsq(q):
        return RS[q // 2][:, (q % 2) * 256:(q % 2 + 1) * 256], ("RSq", q)

    QS = [slice(q * 256, (q + 1) * 256) for q in range(4)]

    def stats_all(srcbuf, srckeys_fn, presq=False):
        if not presq:
            for q in range(4):
                ACT(H[:, :, QS[q]], srcbuf[:, :, QS[q]], AF.Square, srckeys_fn(q), [("H", q)])
        banks = []
        for q in range(4):
            b = S.ps()
            banks.append(b)
            for kc in range(8):
                MM(PS[b][:, 0:256], ones_b, H[:, kc, QS[q]], kc == 0, kc == 7, [("H", q), "cbf"], [pk(b)])
        for q in range(4):
            rs, rk = rsq(q)
            ACT(rs, PS[banks[q]][:, 0:256], AF.Ln, [pk(banks[q]), "epsc"], [rk], bias=epsc[:, 0:1], scale=1.0 / D)
        for q in range(4):
            rs, rk = rsq(q)
            ACT(rs, rs, AF.Exp, [rk], [rk], scale=-0.5)

    def xkeys(q):
        return [("X", 2 * q), ("X", 2 * q + 1)]

    def modnorm(l, j, cond):
        gs = dvec[:, l, 3 * j, :, cond:cond + 1]
        sh = dvec[:, l, 3 * j + 1, :, cond:cond + 1]
        dk = [("dvec", l, 3 * j), ("dvec", l, 3 * j + 1)]
        stats_all(X, xkeys)
        for q in range(4):
            rs, rk = rsq(q)
            TT("dve", Y[:, :, QS[q]], X[:, :, QS[q]], rs.unsqueeze(1).to_broadcast([128, 8, 256]), ALU.mult,
               xkeys(q) + [rk], [("Y", q)])
        for q in range(4):
            for kc in range(8):
                ACT(H[:, kc, QS[q]], Y[:, kc, QS[q]], AF.Identity, [("Y", q)] + dk, [("H", q)],
                    bias=sh[:, kc, :], scale=gs[:, kc, :])

    def gated_out(l, j, cond, prescaled=False):
        gg = dvec[:, l, 3 * j + 2, :, cond:cond + 1]
        stats_all(Y, lambda q: [("Y", q)], presq=prescaled)
        for q in range(4):
            rs, rk = rsq(q)
            TT("dve", Y[:, :, QS[q]], Y[:, :, QS[q]], rs.unsqueeze(1).to_broadcast([128, 8, 256]), ALU.mult,
               [("Y", q), rk], [("Y", q)])
            if not prescaled:
                TT("dve", Y[:, :, QS[q]], Y[:, :, QS[q]], gg.to_broadcast([128, 8, 256]), ALU.mult,
                   [("Y", q), ("dvec", l, 3 * j + 2)], [("Y", q)])
        for q in range(4):
            TT("dve", X[:, 0:6, QS[q]], X[:, 0:6, QS[q]], Y[:, 0:6, QS[q]], ALU.add, [("Y", q)] + xkeys(q), [("Xa", q)])
            TT("pool", X[:, 6:8, QS[q]], X[:, 6:8, QS[q]], Y[:, 6:8, QS[q]], ALU.add, [("Y", q)] + xkeys(q), [("Xb", q)])
        for q in range(4):
            for t_ in (2 * q, 2 * q + 1):
                S.alias([("Xa", q), ("Xb", q)], ("X", t_))

    def junction(go, mn, cond, qs=(0, 1, 2, 3)):
        def st(q):
            b = S.ps()
            for kc in range(8):
                MM(PS[b][:, 0:256], ones_b, H[:, kc, QS[q]], kc == 0, kc == 7, [("H", q), "cbf"], [pk(b)])
            rs, rk = rsq(q)
            ACT(rs, PS[b][:, 0:256], AF.Ln, [pk(b), "epsc"], [rk], bias=epsc[:, 0:1], scale=1.0 / D)
            ACT(rs, rs, AF.Exp, [rk], [rk], scale=-0.5)

        def A(q):
            l, j, pres = go
            if not pres:
                ACT(H[:, :, QS[q]], Y[:, :, QS[q]], AF.Square, [("Y", q)], [("H", q)])
            st(q)

        def B(q):
            l, j, pres = go
            gg = dvec[:, l, 3 * j + 2, :, cond:cond + 1]
            rs, rk = rsq(q)
            TT("dve", Y[:, :, QS[q]], Y[:, :, QS[q]], rs.unsqueeze(1).to_broadcast([128, 8, 256]), ALU.mult,
               [("Y", q), rk], [("Y", q)])
            if not pres:
                TT("dve", Y[:, :, QS[q]], Y[:, :, QS[q]], gg.to_broadcast([128, 8, 256]), ALU.mult,
                   [("Y", q), ("dvec", l, 3 * j + 2)], [("Y", q)])
            TT("dve", X[:, :, QS[q]], X[:, :, QS[q]], Y[:, :, QS[q]], ALU.add, [("Y", q)] + xkeys(q), xkeys(q))

        def C(q):
            ACT(H[:, :, QS[q]], X[:, :, QS[q]], AF.Square, xkeys(q), [("H", q)])
            st(q)

        def Dq(q):
            rs, rk = rsq(q)
            TT("dve", Y[:, :, QS[q]], X[:, :, QS[q]], rs.unsqueeze(1).to_broadcast([128, 8, 256]), ALU.mult,
               xkeys(q) + [rk], [("Y", q)])

        def E(q):
            l, j = mn
            gs = dvec[:, l, 3 * j, :, cond:cond + 1]
            sh = dvec[:, l, 3 * j + 1, :, cond:cond + 1]
            dk = [("dvec", l, 3 * j), ("dvec", l, 3 * j + 1)]
            for kc in range(8):
                ACT(H[:, kc, QS[q]], Y[:, kc, QS[q]], AF.Identity, [("Y", q)] + dk, [("H", q)],
                    bias=sh[:, kc, :], scale=gs[:, kc, :])

        order = [(A, 0), (A, 1), (A, 2), (A, 3), (B, 0), (C, 0), (B, 1), (C, 1), (B, 2), (C, 2), (B, 3), (C, 3),
                 (Dq, 0), (E, 0), (Dq, 1), (E, 1), (Dq, 2), (E, 2), (Dq, 3), (E, 3)]
        for fn, q in order:
            if q not in qs:
                continue
            if fn in (A, B) and go is None:
                continue
            if fn in (C, Dq, E) and mn is None:
                continue
            fn(q)

    def epi_scaled(l, j, cond):
        gg = dvec[:, l, 3 * j + 2, :, cond:cond + 1]

        def epi_(ci, tb, b):
            sl = wsl(tb)
            W_ = WIN[1]
            ACT(Y[:, ci, sl], PS[b][:, 0:W_], AF.Copy, [pk(b), ("dvec", l, 3 * j + 2)], kY(tb), scale=gg[:, ci, :])
            ACT(H[:, ci, sl], PS[b][:, 0:W_], AF.Square, [pk(b)], kH(tb))
        return epi_

    def proj_fm(Wd, KC, chunks, src, src_keys, epi, group=2, lead=0):
        def load_group(g0):
            grp = chunks[g0:g0 + group]
            pieces = []
            for gi, ch in enumerate(grp):
                off = gi * 128
                for (c0, wd) in ch:
                    pieces.append((off, c0, wd))
                    off += wd
            merged = []
            for p in pieces:
                if merged and merged[-1][0] + merged[-1][2] == p[0] and merged[-1][1] + merged[-1][2] == p[1]:
                    merged[-1] = (merged[-1][0], merged[-1][1], merged[-1][2] + p[2])
                else:
                    merged.append(p)
            view, wkey = load_slab(Wd, KC, merged)
            return grp, view, wkey

        def run(g0, grp, view, wkey, tb):
            for gi in range(len(grp)):
                b = S.ps()
                for kc in range(KC):
                    MM(PS[b][:, 0:WIN[1]], view[:, kc, gi * 128:(gi + 1) * 128], src(kc, tb), kc == 0, kc == KC - 1,
                       [wkey] + src_keys(tb), [pk(b)])
                epi(g0 + gi, tb, b)

        starts = list(range(0, len(chunks), group))
        nlead = lead if (WIN[0] == 2 and len(starts) >= lead) else 0
        if nlead:
            loaded = [(g0,) + load_group(g0) for g0 in starts[:nlead]]
            for tb in range(2):
                for (g0, grp, view, wkey) in loaded:
                    run(g0, grp, view, wkey, tb)
            for _ in range(nlead):
                bg_step()
        for g0 in starts[nlead:]:
            grp, view, wkey = load_group(g0)
            for tb in range(WIN[0]):
                run(g0, grp, view, wkey, tb)
            bg_step()

    def srcH(kc, tb):
        return H[:, kc, wsl(tb)]

    hid = BIG[:, :].rearrange("p (a b) -> p a b", b=T)

    def mlp(l, cond):

        def epiA(ci, tb, b):
            sl = wsl(tb)
            W_ = WIN[1]
            if W_ == 256:
                ti = ci % 2
                tmp = (RS[1], RS2)[ti]
                tk = (("RS", 1), "RS2")[ti]
            else:
                ti = (ci * 2 + tb) % 3
                tmp = (RS[0], RS[1], RS2)[ti]
                tk = (("RS", 0), ("RS", 1), "RS2")[ti]
            ACT(tmp[:, 0:W_], PS[b][:, 0:W_], AF.Relu, [pk(b)], [tk])
            TT("dve", hid[:, ci, sl], tmp[:, 0:W_], tmp[:, 0:W_], ALU.mult, [tk], [("hid", ci, tb)])

        proj_fm(w1_d[l], 8, [[(c * 128, 128)] for c in range(32)], srcH, kH, epiA, group=4, lead=3)
        S.mark('mlpA')

        def srcHid(kc, tb):
            return hid[:, kc, wsl(tb)]

        def keysHid(tb):
            return [("hid", c, tb) for c in range(32)]

        if WIN[1] == 256:
            epi2 = epi_scaled(l, 1, cond)
            banks = list(range(8))
            for sidx in range(8):
                view, wkey = load_slab(w2_d[l][sidx * 512:(sidx + 1) * 512, :], 4, [(0, 0, 1024)])
                for oc in range(8):
                    for kc in range(4):
                        hc = sidx * 4 + kc
                        MM(PS[banks[oc]][:, 0:256], view[:, kc, oc * 128:(oc + 1) * 128],
                           hid[:, hc, 0:256], sidx == 0 and kc == 0, sidx == 7 and kc == 3,
                           [wkey, ("hid", hc, 0)], [pk(banks[oc])])
            W_ = 256
            gg_ = dvec[:, l, 5, :, cond:cond + 1]
            for oc in range(8):
                src_ = PS[banks[oc]][:, 0:256]
                ACT(Y[:, oc, 0:256], src_, AF.Copy, [pk(banks[oc]), ("dvec", l, 5)], kY(0), scale=gg_[:, oc, :])
                ACT(H[:, oc, 0:256], src_, AF.Square, [pk(banks[oc])], kH(0))
        else:
            proj_fm(w2_d[l], 32, [[(c * 128, 128)] for c in range(8)], srcHid, keysHid, epi_scaled(l, 1, cond), group=1)
        S.mark('mlpB')
        pass

    def even_layer(cond, nseq):
        L = T // nseq
        LP = L + 30
        LC = L + 2
        apad = BIG[:, 0:4576].rearrange("p (c x) -> p c x", c=4)
        cxpad = BIG[:, 4576:8736].rearrange("p (c x) -> p c x", c=4)
        bgt = BIG[:, 8736:12832].rearrange("p (c x) -> p c x", c=4)
        ac = BIG[:, 12832:21024].bitcast(F32).rearrange("p (c x) -> p c x", c=4)
        U = BIG[:, 21024:29216].rearrange("p (c x) -> p c x", c=8)
        sgt = BIG[:, 29216:30240].bitcast(F32)
        diag = Y[:, :, :].rearrange("p a b -> p (a b)").bitcast(BF16)[:, 0:15872].rearrange("p (k m) -> p k m", m=128)

        MS("dve", diag[:, 0, 0:2], 0.0, ["Yclaim"] + [("Y", q) for q in range(4)])

        def diag_gen():
            for c in range(4):
                for k in range(31):
                    TS("dve", diag[:, c * 31 + k, :], ident_b, convaT[:, c, k:k + 1], None, ALU.mult, None,
                       ["cbf", "convaT", "Yclaim"], [("diag", c, k)])
                    yield
        dgen = [diag_gen()]

        def diag_step(n):
            for _ in range(n):
                if dgen[0] is None:
                    return
                try:
                    next(dgen[0])
                except StopIteration:
                    dgen[0] = None
        MS("dve", apad[:], 0.0, ["apad"])
        MS("dve", cxpad[:], 0.0, ["cxpad"])

        def padview(buf, c, tb, LPx, padl):
            if nseq == 1:
                return buf[:, c, padl + tb * 512: padl + (tb + 1) * 512]
            s0 = tb * 2
            return buf[:, c, s0 * LPx:(s0 + 2) * LPx].rearrange("p (s x) -> p s x", s=2)[:, :, padl:padl + L]

        def psview(b):
            if nseq == 1:
                return PS[b][:]
            return PS[b][:].rearrange("p (s x) -> p s x", s=2)

        hold = {}

        def epi(ci, tb, b):
            diag_step(4)
            grp, c = ci // 2 // 4, None
            if ci < 8:
                c = ci // 2
                if ci % 2 == 0:
                    hold[(tb, "v")] = b
                else:
                    bv = hold[(tb, "v")]
                    ACT(sgt[:], PS[b][:], AF.Sigmoid, [pk(b)], ["sgt"])
                    dst = padview(apad, c, tb, LP, 15)
                    sv = sgt[:] if nseq == 1 else sgt[:].rearrange("p (s x) -> p s x", s=2)
                    TT("dve", dst, psview(bv), sv, ALU.mult, [pk(bv), "sgt"], ["apad"])
            elif ci < 16:
                c = (ci - 8) // 2
                if ci % 2 == 0:
                    hold[(tb, "v")] = b
                else:
                    bv = hold[(tb, "v")]
                    ACT(sgt[:], PS[b][:], AF.Copy, [pk(b)], ["sgt"])
                    dst = padview(cxpad, c, tb, LC, 1)
                    sv = sgt[:] if nseq == 1 else sgt[:].rearrange("p (s x) -> p s x", s=2)
                    TT("dve", dst, psview(bv), sv, ALU.mult, [pk(bv), "sgt"], ["cxpad"])
            else:
                c = ci - 16
                CP(ev_eng(), bgt[:, c, tb * 512:(tb + 1) * 512], PS[b][:], [pk(b)], ["bgt"])

        chunks = []
        for c in range(4):
            chunks += [[(c * 128, 128)], [(512 + c * 128, 128)]]
        for c in range(4):
            chunks += [[(1536 + c * 128, 128)], [(2048 + c * 128, 128)]]
        for c in range(4):
            chunks += [[(1024 + c * 128, 128)]]
        proj_fm(ein_d, 8, chunks, srcH, kH, epi, lead=3)
        diag_step(200)
        for c in range(4):
            S.alias([("diag", c, k) for k in range(31)], ("diag", c))
        S.mark('e_inproj')

        for c in range(4):
            for k in range(3):
                TS("dve", d3[:, c * 3 + k, :], ident_b, convbT[:, c, k:k + 1], None, ALU.mult, None,
                   ["cbf", "convbT"], ["d3"])

        def win(buf, c, tb, LPx, k):
            if nseq == 1:
                return buf[:, c, k + tb * 512: k + (tb + 1) * 512]
            s0 = tb * 2
            return buf[:, c, s0 * LPx:(s0 + 2) * LPx].rearrange("p (s x) -> p s x", s=2)[:, :, k:k + L]

        for cp in range(2):
            for cc in range(2):
                c = cp * 2 + cc
                for tb in range(2):
                    sl = slice(tb * 512, (tb + 1) * 512)
                    b = S.ps()
                    for k in range(31):
                        MM(psview(b), diag[:, c * 31 + k, :], win(apad, c, tb, LP, k), k == 0, k == 30,
                           [("diag", c), "apad"], [pk(b)])
                    ACT(ac[:, c, sl], PS[b][:], AF.Identity, [pk(b), "evec"], [("ac", tb)], bias=evec[:, 0, c:c + 1])
                    b2 = S.ps()
                    for k in range(3):
                        MM(psview(b2), d3[:, c * 3 + k, :], win(cxpad, c, tb, LC, k), k == 0, k == 2,
                           ["d3", "cxpad"], [pk(b2)])
                    TT("dve", U[:, 4 + c, sl], PS[b2][:], bgt[:, c, sl], ALU.mult, [pk(b2), "bgt"], [("U", tb)])
            bg_force(2)
        S.mark('e_conv')
        lnb = {}
        for tb in range(2):
            sl = slice(tb * 512, (tb + 1) * 512)
            ACT(H[:, 0:4, sl], ac[:, :, sl], AF.Square, [("ac", tb)], kH(tb))
        for tb in range(2):
            sl = slice(tb * 512, (tb + 1) * 512)
            b1 = S.ps()
            for c in range(4):
                MM(PS[b1][:], ones_f[:], ac[:, c, sl], c == 0, c == 3, [("ac", tb), "ones_f"], [pk(b1)])
            b2 = S.ps()
            for c in range(4):
                MM(PS[b2][:], ones_b, H[:, c, sl], c == 0, c == 3, kH(tb) + ["cbf"], [pk(b2)])
            lnb[tb] = (b1, b2)
        for tb in range(2):
            sl = slice(tb * 512, (tb + 1) * 512)
            b1, b2 = lnb[tb]
            mean = RS[tb]
            TS("dve", mean[:], PS[b1][:], 1.0 / 512, None, ALU.mult, None, [pk(b1)], [("RS", tb)])
            TT("dve", RS2[:], mean[:], mean[:], ALU.mult, [("RS", tb)], ["RS2"])
            STT("dve", RS2[:], PS[b2][:], 1.0 / 512, RS2[:], ALU.mult, ALU.subtract, [pk(b2), "RS2"], ["RS2"])
            ACT(RS2[:], RS2[:], AF.Sqrt, ["RS2", "epsc"], ["RS2"], bias=epsc[:, 0:1], scale=1.0)
            RCP(RS2[:], RS2[:], ["RS2"], ["RS2"])
            TT("dve", ac[:, :, sl], ac[:, :, sl], mean[:].unsqueeze(1).to_broadcast([128, 4, 512]), ALU.subtract,
               [("ac", tb), ("RS", tb)], [("ac", tb)])
            TT("dve", ac[:, :, sl], ac[:, :, sl], RS2[:].unsqueeze(1).to_broadcast([128, 4, 512]), ALU.mult,
               [("ac", tb), "RS2"], [("ac", tb)])
            for c in range(4):
                ACT(U[:, c, sl], ac[:, c, sl], AF.Silu, [("ac", tb), "evec"], [("U", tb)],
                    bias=evec[:, 2, c:c + 1], scale=evec[:, 1, c:c + 1])

        def srcU(kc, tb):
            return U[:, kc, tb * 512:(tb + 1) * 512]

        for q in range(4):
            S.alias([("diag", c) for c in range(4)], ("Y", q))
        if half_is0[0]:
            bg_ensure(6)
            bg_every[0] = 2
        proj_fm(eout_d, 8, [[(c * 128, 128)] for c in range(8)], srcU, lambda tb: [("U", tb)], epi_scaled(0, 0, cond), lead=3)
        S.mark('e_outproj')
        pass


    ATT_SCALE = 64 ** -0.5

    def odd_layer(half, cond):
        nseq = 4 if half == 0 else 1
        tps = 8 // nseq
        sample = (half == 1)
        S.barrier()
        QaT = BIG[:, 0:4096].rearrange("p (c x) -> p c x", c=4)
        KaT2 = BIG[:, 4096:6144].rearrange("p (c x) -> p c x", c=2)
        Va1 = BIG[:, 6144:7200].rearrange("p (t g e) -> p t g e", t=8, g=2)
        QmT = BIG[:, 7200:11296].rearrange("p (c x) -> p c x", c=4)
        KmT = BIG[:, 11296:15392].rearrange("p (c x) -> p c x", c=4)
        Kmtok = BIG[:, 15392:19488].rearrange("p (t x) -> p t x", t=8)
        Vm1 = BIG[:, 19488:23712].rearrange("p (t h e) -> p t h e", t=8, h=8)
        SG = BIG[:, 23712:27808].rearrange("p (t x) -> p t x", t=8)
        G = BIG[:, 27808:28320].bitcast(F32).rearrange("p (t x) -> p t x", t=8)
        KcT = BIG[:, 28320:29344].rearrange("p (g x) -> p g x", g=2)
        Vc1 = BIG[:, 29344:29872].rearrange("p (k g e) -> p k g e", k=4, g=2)
        Vpp = [BIG[:, 29872 + i * 528:29872 + (i + 1) * 528].rearrange("p (h e) -> p h e", h=8) for i in range(2)]
        PmT = [BIG[:, 30928 + i * 512:30928 + (i + 1) * 512].rearrange("p (h x) -> p h x", h=4) for i in range(2)]
        Yf = Y[:, :, :].rearrange("p a b -> p (a b)")
        ropeA = Yf[:, 0:512]
        ropeB = Yf[:, 512:1024]
        KVst = Yf[:, 1024:3072].rearrange("p (t x) -> p t x", t=8)
        cosT = Yf[:, 3072:4096]
        sinT = Yf[:, 4096:5120]
        concat = Y
        Hf = H[:, :, :].rearrange("p a b -> p (a b)")
        PTc = [XIN[i][:, :].bitcast(BF16).rearrange("p (k x) -> p k x", k=4) for i in range(2)]
        _ptb = [RS[0][:, :].bitcast(BF16), RS[1][:, :].bitcast(BF16), RS2[:, :].bitcast(BF16), BIG[:, 31952:32720]]
        PTbj = [_ptb[j // 2][:, (j % 2) * 384:(j % 2 + 1) * 384].rearrange("p (r x) -> p r x", r=3) for j in range(8)]

        if sample:
            DMA("sp", cosT, cosT_d, "const2", (), ["cosT"])
            DMA("sp", sinT, sinT_d, "const2", (), ["sinT"])
        MS("dve", Va1[:, :, :, 64:65], 1.0, ["Va1"])
        MS("dve", Vm1[:, :, :, 64:65], 1.0, ["Vm1"])

        chunks = []
        kinds = []
        for c in range(4):
            chunks.append([(c * 128, 128)]); kinds.append(("qa", c))
        for g in range(2):
            chunks.append([(512 + g * 64, 64), (512 + g * 64, 64)]); kinds.append(("ka", g))
        if not sample:
            pass
        for c in range(4):
            chunks.append([(768 + c * 128, 128)]); kinds.append(("qm", c))
        for c in range(4):
            chunks.append([(1280 + c * 128, 128)]); kinds.append(("km", c))
        hold = {}

        def epi(ci, tb, b):
            kind, c = kinds[ci]
            sl = slice(tb * 512, (tb + 1) * 512)
            if kind in ("qa", "ka"):
                dst = QaT[:, c, sl] if kind == "qa" else KaT2[:, c, sl]
                dkey = ("QaT", tb) if kind == "qa" else ("KaT2", tb)
                if sample:
                    CP("act", ropeA, PS[b][:], [pk(b)], ["ropeA"])
                    bsw = S.ps()
                    MM(PS[bsw][:], cmat[:, 3, :], ropeA, True, True, ["ropeA", "cmat"], [pk(bsw)])
                    TT("dve", ropeB, PS[bsw][:], sinT[:, sl], ALU.mult, [pk(bsw), "sinT"], ["ropeB"])
                    TT("dve", ropeA, ropeA, cosT[:, sl], ALU.mult, ["ropeA", "cosT"], ["ropeA"])
                    TT("dve", dst, ropeA, ropeB, ALU.add, ["ropeA", "ropeB"], [dkey])
                else:
                    CP(ev_eng(), dst, PS[b][:], [pk(b)], [dkey])
            elif kind == "qm":
                CP(ev_eng(), QmT[:, c, sl], PS[b][:], [pk(b)], [("QmT", tb)])
            else:
                ACT(KmT[:, c, sl], PS[b][:], AF.Copy, [pk(b)], [("KmT", tb)], scale=0.125)

        proj_fm(oin_d, 8, chunks, srcH, kH, epi, lead=3)
        S.mark('o_inproj_fm')
        _ck(1)

        def proj_tm(c0, ncols, epi_t):
            view, wkey = load_slab(oin_d, 8, [(0, c0, ncols)])
            for t in range(8):
                b = S.ps()
                for kc in range(8):
                    MM(PS[b][:, 0:ncols], H[:, kc, t * 128:(t + 1) * 128], view[:, kc, :], kc == 0, kc == 7,
                       [wkey] + kH(t // 4), [pk(b)])
                epi_t(t, b)

        def epi_kv(t, b):
            import os
            if not sample:
                kvm = '0'
                if kvm != '2':
                    CP("act", KVst[:, t, :], PS[b][:, 0:256], [pk(b)], [("KVst", t)])
                s_, qt = t // 2, t % 2
                for g_ in (range(2) if kvm != '1' else ()):
                    DMA("sp", nk_d[s_, g_, qt * 128:(qt + 1) * 128, :], KVst[:, t, g_ * 64:(g_ + 1) * 64], "out", [("KVst", t)], [])
                    DMA("sp", nv_d[s_, g_, qt * 128:(qt + 1) * 128, :], KVst[:, t, 128 + g_ * 64:128 + (g_ + 1) * 64], "out", [("KVst", t)], [])
            CP("dve", Va1[:, t, :, 0:64], PS[b][:, 128:256].rearrange("p (g d) -> p g d", g=2), [pk(b)], ["Va1"])

        proj_tm(512, 256, epi_kv)
        _ck(1.1)

        def epi_km(t, b):
            ACT(Kmtok[:, t, :], PS[b][:], AF.Copy, [pk(b)], ["Kmtok"], scale=0.125)

        proj_tm(1280, 512, epi_km)
        _ck(1.2)

        def epi_vm(t, b):
            CP("dve", Vm1[:, t, :, 0:64], PS[b][:].rearrange("p (h d) -> p h d", h=8), [pk(b)], ["Vm1"])

        proj_tm(1792, 512, epi_vm)
        _ck(1.3)

        def epi_om(t, b):
            ACT(SG[:, t, :], PS[b][:], AF.Sigmoid, [pk(b)], ["SG"])
            TT("dve", SG[:, t, :], SG[:, t, :], hng, ALU.mult, ["SG", "rowb"], ["SG"])

        proj_tm(2304, 512, epi_om)
        _ck(1.4)

        def epi_g(t, b):
            TT("dve", G[:, t, :], PS[b][:, 0:32], gateb, ALU.add, [pk(b), "rowb"], ["G"])

        proj_tm(2816, 32, epi_g)
        S.mark('o_inproj_tm')
        _ck(2)

        if sample:
            kst = XIN[0][:, :].rearrange("p (k g r d) -> p k g r d", k=4, g=2, r=2)
            vst = XIN[1][:, 0:512].rearrange("p (k g d) -> p k g d", k=4, g=2)
            for r_ in range(2):
                for g in range(2):
                    DMA("sp", kst[:, :, g, r_, :], ck_d[g].rearrange("(k p) d -> p k d", p=128), "xin0", (), [("XIN", 0)])
            for g in range(2):
                DMA("sp", vst[:, :, g, :], cv_d[g].rearrange("(k p) d -> p k d", p=128), "xin1", (), [("XIN", 1)])
            for g in range(2):
                b = S.ps()
                for kb in range(4):
                    TR(PS[b][:, kb * 128:(kb + 1) * 128], kst[:, kb, g].rearrange("p r d -> p (r d)"), ident,
                       [("XIN", 0), "cmat"], [pk(b)])
                CP("dve", KcT[:, g, :], PS[b][:], [pk(b)], ["KcT"])
            MS("dve", Vc1[:, :, :, 64:65], 1.0, ["Vc1"])
            CP("dve", Vc1[:, :, :, 0:64], vst, [("XIN", 1)], ["Vc1"])

        _ck(3)
        LFn = SM[:, :, 0, :]
        BCn = SM[:, :, 1, :]
        LA = SM[:, :, 2, :]
        Aex = SM[:, :, 3, :]
        Bex = SM[:, :, 4, :]
        EBT = SM[:, :, 5, :]
        BTn = SM[:, :, 6, :]
        ACT(SM[:, :, 0, 0:8], G[:, :, 8:16], AF.Exp, ["G"], ["SM0"], scale=-1.0)
        ACT(SM[:, :, 0, 8:16], G[:, :, 24:32], AF.Exp, ["G"], ["SM0"], scale=-1.0)
        ACT(LFn, LFn, AF.Ln, ["SM0", "ones_f"], ["SM0"], bias=ones_f[:, 0:1])
        _ck(3.1)
        bg_ = S.ps()
        gv = PS[bg_][:, 0:256].rearrange("p (t x) -> p t x", t=8)
        for t in range(8):
            MM(gv[:, t, 0:8], triU, SM[:, t, 0, 0:8], True, True, ["cmat", "SM0"], [pk(bg_)])
            MM(gv[:, t, 8:16], triL, SM[:, t, 0, 8:16], True, True, ["cmat", "SM0"], [pk(bg_)])
            MM(gv[:, t, 16:32], ones_f[:], SM[:, t, 0, 0:16], True, True, ["ones_f", "SM0"], [pk(bg_)])
        _ck(3.2)
        CP("dve", BCn, gv[:, :, 0:16], [pk(bg_)], ["SM1"])
        CP("dve", BTn, gv[:, :, 16:32], [pk(bg_)], ["SM6"])
        ACT(EBT, gv[:, :, 16:32], AF.Exp, [pk(bg_)], ["SM5"], scale=-1.0)
        _ck(3.3)
        TT("dve", SM[:, :, 2, 0:8], G[:, :, 0:8], SM[:, :, 1, 0:8], ALU.add, ["G", "SM1"], ["SM2"])
        TT("dve", SM[:, :, 2, 8:16], G[:, :, 16:24], SM[:, :, 1, 8:16], ALU.add, ["G", "SM1"], ["SM2"])
        _ck(3.4)
        ACT(Aex, LA, AF.Exp, ["SM2"], ["SM3"])
        _ck(3.5)
        ACT(Bex, BCn, AF.Exp, ["SM1"], ["SM4"], scale=-1.0)
        ACT(SM[:, :, 7, :], BCn, AF.Exp, ["SM1"], ["SM7"])

        _ck(4)
        if not sample:
            bts = [S.ps(), S.ps()]
            for t in range(8):
                TR(PS[bts[t // 4]][0:16, (t % 4) * 128:(t % 4 + 1) * 128], SM[:, t, 2, :], ident, ["SM2", "cmat"], [pk(bts[t // 4])])
            mxT = sml[0:16, 0, :]

            def red(hf):
                src = PS[bts[hf]][0:16, :].rearrange("p (t x) -> p t x", t=4)
                dst = sml[0:16, 0, hf * 4:(hf + 1) * 4]
                S.add("dve", lambda e: e.tensor_reduce(out=dst, in_=src, axis=AX.X, op=ALU.max), [pk(bts[hf])], ["sml"])
            red(0)
            red(1)
            Dg = DgT[0:16, :].rearrange("p (t j) -> p t j", t=8)
            TT("dve", Dg, mxT.unsqueeze(2).to_broadcast([16, 8, 16]),
               cmat[0:16, 0, 0:16].unsqueeze(1).to_broadcast([16, 8, 16]), ALU.mult, ["sml", "cmat"], ["Dg"])
            bm_ = S.ps()
            MM(PS[bm_][:, 0:128], ones_f[0:16, :], Dg.rearrange("p t j -> p (t j)"), True, True, ["Dg", "ones_f"], [pk(bm_)])
            CP("dve", MXB[:, :, :].rearrange("p a b -> p (a b)"), PS[bm_][:, 0:128], [pk(bm_)], ["MXB"])

        _ck(5)
        S.barrier()

        def mlstm_gen():
            ACCd = [[0, 1], [4, 5]]
            ROTd = [[2, 3], [6, 7]]
            PmTd = [[XIN[d_][:, :].bitcast(BF16)[:, hg_ * 512:(hg_ + 1) * 512].rearrange("p (h x) -> p h x", h=4)
                     for hg_ in range(2)] for d_ in range(2)]
            Ctmp_d = [Ctmp, RS2[:, 0:260].rearrange("p (c e) -> p c e", c=4)]
            Ctmp2_d = [CoutS[:, 0], RS[0][:, 0:260].rearrange("p (c e) -> p c e", c=4)]
            KmTz = [Hf[:, 0:4096].rearrange("p (c x) -> p c x", c=4), Hf[:, 4096:8192].rearrange("p (c x) -> p c x", c=4)]
            MS("dve", Hf[64:128, 0:4096], 0.0, ["KmTz"])
            MS("dve", Hf[0:64, 4096:8192], 0.0, ["KmTz"])
            CP("dve", KmTz[0][0:64], KmT[0:64], [("KmT", 0), ("KmT", 1)], ["KmTz"])
            CP("act", KmTz[1][64:128], KmT[64:128], [("KmT", 0), ("KmT", 1)], ["KmTz"])
            written = set()
            bufi = [0]
            mfin = mrec[:, 0, :]
            em = mrec[:, 1, :]
            mt = mrec[:, 2, :]
            Cout = CoutS

            for s_ in range(nseq):
                t0 = s_ * tps
                if sample:
                    DMA("sp", Cst[:, 0, :, 0:64], scf_d.rearrange("(c p) d e -> (p d) c e", p=2), "const2", (), [("Cst", 0)])
                    DMA("sp", Cst[:, 1, :, 0:64], scb_d.rearrange("(c p) d e -> (p d) c e", p=2), "const2", (), [("Cst", 1)])
                    ACT(sml[:, 1, :].rearrange("p (d c) -> p d c", d=2), snm[:, :, 2:4].rearrange("p c d -> p d c"), AF.Exp,
                        ["snm"], ["sml1"])
                    emi = sml[:, 1, :].rearrange("p (d c) -> p d c", d=2)
                    for d_ in range(2):
                        TT("dve", Cst[:, d_, :, 0:64], Cst[:, d_, :, 0:64], emi[:, d_, :].unsqueeze(2).to_broadcast([128, 4, 64]),
                           ALU.mult, [("Cst", d_), "sml1"], [("Cst", d_)])
                        TT("dve", Cst[:, d_, :, 64], snm[:, :, d_], emi[:, d_, :], ALU.mult, ["snm", "sml1"], [("Cst", d_)])
                else:
                    for d_ in range(2):
                        MS("dve", Cst[:, d_], 0.0, [("Cst", d_)])
                for d_ in range(2):
                    CP("act", Cz[0:64, d_, 0:8:2, 0:65], Cst[0:64, d_], [("Cst", d_)], [("Cz", d_)])
                    CP("act", Cz[64:128, d_, 1:8:2, 0:65], Cst[64:128, d_], [("Cst", d_)], [("Cz", d_)])

                def unit(i, d_, t0=t0, s_=s_):
                    if True:
                        t = t0 + (i if d_ == 0 else tps - 1 - i)
                        tsl = slice(t * 128, (t + 1) * 128)
                        bi = d_
                        vpp = Vpp[bi]
                        ACC = ACCd[d_]
                        ROT = ROTd[d_]
                        TT("pool", vpp[:, :, 0:65], Vm1[:, t, :, 0:65], SM[:, t, 3, d_ * 8:(d_ + 1) * 8].unsqueeze(2).to_broadcast([128, 8, 65]),
                           ALU.mult, ["Vm1", "SM3"], [("Vpp", bi)])
                        yield
                        for hg in range(2):
                            bS = S.ps_from(ROT)
                            for hh in range(4):
                                h = hg * 4 + hh
                                c, p0 = h // 2, (h % 2) * 64
                                MM(PS[bS][:, hh * 128:(hh + 1) * 128], KmTz[h % 2][:, c, tsl], QmT[:, c, tsl],
                                   True, True, ["KmTz", ("QmT", t // 4)], [pk(bS)])
                            yield
                            pm = PmTd[d_][hg]
                            TT("dve", pm[:, :, :].rearrange("p h x -> p (h x)"), PS[bS][:],
                               mask4[:, d_].rearrange("p h x -> p (h x)"), ALU.mult, [pk(bS), "mask4"], [("PmT", d_, hg)])
                            yield
                            bA = ACC[hg]
                            for hh in range(4):
                                h = hg * 4 + hh
                                c, p0 = h // 2, (h % 2) * 64
                                MM(PS[bA][:, hh * 65:(hh + 1) * 65], pm[:, hh, :], vpp[:, h, 0:65], True, False,
                                   [("PmT", d_, hg), ("Vpp", bi)], [pk(bA)])
                                MM(PS[bA][:, hh * 65:(hh + 1) * 65], QmT[:, c, tsl], Cz[:, d_, h, 0:65],
                                   False, True, [("QmT", t // 4), ("Cz", d_)], [pk(bA)])
                            yield
                            acc = PS[bA][:, 0:260].rearrange("p (h e) -> p h e", h=4)
                            bsl = SM[:, t, 4, d_ * 8 + hg * 4:d_ * 8 + hg * 4 + 4]
                            den = sml[:, 2 + 2 * d_ + hg, 0:4]
                            rr = sml[:, 2 + 2 * d_ + hg, 4:8]
                            binv = SM[:, t, 7, d_ * 8 + hg * 4:d_ * 8 + hg * 4 + 4]
                            ACT(den, acc[:, :, 64], AF.Abs, [pk(bA)], [("den", d_, hg)])
                            TT("dve", den, den, binv, ALU.max, [("den", d_, hg), "SM7"], [("den", d_, hg)])
                            RCP(rr, den, [("den", d_, hg)], [("den", d_, hg)])
                            dst = concat[:, t, 512 + hg * 256:512 + (hg + 1) * 256].rearrange("p (h e) -> p h e", h=4)
                            if (t, hg) not in written:
                                written.add((t, hg))
                                TT("dve", dst, acc[:, :, 0:64], rr.unsqueeze(2).to_broadcast([128, 4, 64]), ALU.mult,
                                   [pk(bA), ("den", d_, hg)], [("hm", t)])
                            else:
                                tmp = Ctmp_d[d_][:, :, 0:64]
                                TT("dve", tmp, acc[:, :, 0:64], rr.unsqueeze(2).to_broadcast([128, 4, 64]), ALU.mult,
                                   [pk(bA), ("den", d_, hg)], [("Ctmp", d_)])
                                TT("pool", dst, dst, tmp, ALU.add, [("Ctmp", d_), ("hm", t)], [("hm", t)])
                            yield
                        last = (i == tps - 1)
                        if not (last and sample):
                            bE = S.ps_from(ROT)
                            bO = S.ps_from(ROT)
                            for c in range(4):
                                MM(PS[bE][:, c * 65:(c + 1) * 65], Kmtok[:, t, c * 128:(c + 1) * 128], vpp[:, 2 * c, 0:65], True, True,
                                   ["Kmtok", ("Vpp", bi)], [pk(bE)])
                                MM(PS[bO][:, c * 65:(c + 1) * 65], Kmtok[:, t, c * 128:(c + 1) * 128], vpp[:, 2 * c + 1, 0:65], True, True,
                                   ["Kmtok", ("Vpp", bi)], [pk(bO)])
                            yield
                            for par, bb in ((0, bE), (1, bO)):
                                pr = slice(par * 64, (par + 1) * 64)
                                dC = PS[bb][pr, 0:260].rearrange("p (c e) -> p c e", c=4)
                                ctk = ("Ctmp2", d_, par)
                                TT("dve", Ctmp2_d[d_][pr], dC, Cst[pr, d_], ALU.add, [pk(bb), ("Cst", d_)], [ctk])
                                ebt = SM[pr, t, 5, d_ * 8 + par:d_ * 8 + 8:2]
                                TT("pool", Cst[pr, d_], Ctmp2_d[d_][pr], ebt.unsqueeze(2).to_broadcast([64, 4, 65]), ALU.mult,
                                   [ctk, "SM5"], [("Cst", d_)])
                            CP("act", Cz[0:64, d_, 0:8:2, 0:65], Cst[0:64, d_], [("Cst", d_)], [("Cz", d_)])
                            CP("act", Cz[64:128, d_, 1:8:2, 0:65], Cst[64:128, d_], [("Cst", d_)], [("Cz", d_)])
                        yield
                        if last and not sample:
                            order = list(range(t0, t0 + tps)) if d_ == 0 else list(range(t0 + tps - 1, t0 - 1, -1))
                            cs = slice(d_ * 8, (d_ + 1) * 8)
                            mk = ("mfin", d_)
                            for n_, tt in enumerate(order):
                                if n_ == 0:
                                    TS("dve", mt[:, cs], MXB[:, tt, cs], 0.0, None, ALU.max, None, ["MXB"], [mk])
                                else:
                                    TT("dve", mt[:, cs], mfin[:, cs], MXB[:, tt, cs], ALU.max, ["MXB", mk], [mk])
                                TT("dve", mfin[:, cs], mt[:, cs], SM[:, tt, 6, cs], ALU.subtract, [mk, "SM6"], [mk])
                            ACT(em[:, cs], mfin[:, cs], AF.Exp, [mk], [mk], scale=-1.0)
                            md = nmf_d if d_ == 0 else nmb_d
                            DMA("sp", md[s_:s_ + 1, :], mfin[0:1, cs], "out", [mk], [])
                            for par in range(2):
                                pr = slice(par * 64, (par + 1) * 64)
                                emb = em[pr, d_ * 8 + par:d_ * 8 + 8:2]
                                TT("dve", Cout[pr, 1 + d_], Cst[pr, d_], emb.unsqueeze(2).to_broadcast([64, 4, 65]), ALU.mult,
                                   [("Cst", d_), mk], [("Cout", d_)])
                            cd = ncf_d if d_ == 0 else ncb_d
                            nd = nnf_d if d_ == 0 else nnb_d
                            DMA("sp", cd[s_].rearrange("(c p) d e -> (p d) c e", p=2), Cout[:, 1 + d_, :, 0:64], "out",
                                [("Cout", d_)], [])
                            DMA("sp", nd[s_].rearrange("(c p) d -> (p d) c", p=2), Cout[:, 1 + d_, :, 64], "out",
                                [("Cout", d_)], [], slow=True)

                def chain(d_):
                    for i_ in range(tps):
                        yield from unit(i_, d_)
                cs_ = [chain(0), chain(1)]
                while cs_:
                    for c_ in list(cs_):
                        try:
                            next(c_)
                        except StopIteration:
                            cs_.remove(c_)
            yield

        def attn_gen():
            ACC = [0, 1]
            ROT = [2, 3, 4, 5, 6, 7]
            S.mark('o_mlstm')
            i1 = wri[0]
            wri[0] = (wri[0] + 1) % 3
            kzk = ("WR", i1)
            KaTz = [WR[i1][:, p_ * 2048:(p_ + 1) * 2048].rearrange("p (g x) -> p g x", g=2) for p_ in range(2)]
            MS("dve", KaTz[0][64:128], 0.0, [kzk])
            MS("dve", KaTz[1][0:64], 0.0, [kzk])
            CP("dve", KaTz[0][0:64], KaT2[0:64], [("KaT2", 0), ("KaT2", 1)], [kzk])
            CP("act", KaTz[1][64:128], KaT2[64:128], [("KaT2", 0), ("KaT2", 1)], [kzk])
            if sample:
                i2 = wri[0]
                wri[0] = (wri[0] + 1) % 3
                kck = ("WR", i2)
                KcTz = [WR[i2][:, p_ * 1024:(p_ + 1) * 1024].rearrange("p (g x) -> p g x", g=2) for p_ in range(2)]
                MS("dve", KcTz[0][64:128], 0.0, [kck])
                MS("dve", KcTz[1][0:64], 0.0, [kck])
                CP("dve", KcTz[0][0:64], KcT[0:64], ["KcT"], [kck])
                CP("act", KcTz[1][64:128], KcT[64:128], ["KcT"], [kck])
            for s_ in range(nseq):
                if not sample:
                    for hg in range(2):
                        for hh in range(4):
                            h = hg * 4 + hh
                            g, c, p0 = h // 4, h // 2, (h % 2) * 64
                            bS = S.ps_from(ROT)
                            for kb in range(2):
                                MM(PS[bS][:, kb * 256:(kb + 1) * 256], KaTz[h % 2][:, g, (2 * s_ + kb) * 128:(2 * s_ + kb + 1) * 128],
                                   QaT[:, c, s_ * 256:(s_ + 1) * 256], True, True,
                                   [kzk, ("QaT", s_ // 2)], [pk(bS)])
                            pt = PTc[h % 2][:, 0:2, 0:256]
                            ACT(pt, PS[bS][:].rearrange("p (k x) -> p k x", k=2), AF.Exp, [pk(bS)], [("PTc", h % 2)], scale=ATT_SCALE)
                            for qt in range(2):
                                for kb in range(2):
                                    MM(PS[ACC[qt]][:, hh * 65:(hh + 1) * 65], pt[:, kb, qt * 128:(qt + 1) * 128],
                                       Va1[:, 2 * s_ + kb, g, 0:65], kb == 0, kb == 1, [("PTc", h % 2), "Va1"], [pk(ACC[qt])])
                        for qt in range(2):
                            acc = PS[ACC[qt]][:, 0:260].rearrange("p (h e) -> p h e", h=4)
                            den = sml[:, 4 + qt, 0:4]
                            TT("dve", den, acc[:, :, 64], esink[:, hg * 4:(hg + 1) * 4], ALU.add, [pk(ACC[qt]), "rowb"], [("aden", qt)])
                            RCP(den, den, [("aden", qt)], [("aden", qt)])
                            dst = concat[:, 2 * s_ + qt, hg * 256:(hg + 1) * 256].rearrange("p (h e) -> p h e", h=4)
                            TT("dve", dst, acc[:, :, 0:64], den.unsqueeze(2).to_broadcast([128, 4, 64]), ALU.mult,
                               [pk(ACC[qt]), ("aden", qt)], [("att", 2 * s_ + qt)])
                        yield
                else:
                    Qown = RS2[:, :].bitcast(BF16).rearrange("p (c x) -> p c x", c=4)
                    TS("dve", Qown, QaT[:, :, QS[0]], selv[:, 0:1], None, ALU.mult, None, [("QaT", 0), "selv"], ["Qown"])
                    for q in range(1, 4):
                        STT("dve", Qown, QaT[:, :, QS[q]], selv[:, q:q + 1], Qown, ALU.mult, ALU.add,
                            [("QaT", q // 2), "selv", "Qown"], ["Qown"])
                    mkb = [RS[0][:, :].bitcast(BF16), RS[1][:, :].bitcast(BF16)]
                    DMA("pool", mkb[0], mown_d[:, 0:1024], "const3", (), ["mkb"])
                    DMA("pool", mkb[1], mown_d[:, 1024:2048], "const3", (), ["mkb"])
                    PTl = XIN[0][:, :].bitcast(BF16).rearrange("p (j x) -> p j x", j=8)
                    PTo = XIN[1][:, :].bitcast(BF16)[:, 0:1024].rearrange("p (j x) -> p j x", j=4)
                    for hg in range(2):
                        for hh in range(4):
                            h = hg * 4 + hh
                            g, c = h // 4, h // 2
                            for pr_ in range(2):
                                bS = S.ps_from(ROT)
                                for kk in range(2):
                                    kb = 2 * pr_ + kk
                                    MM(PS[bS][:, kk * 256:(kk + 1) * 256], KcTz[h % 2][:, g, kb * 128:(kb + 1) * 128], Qown[:, c, :],
                                       True, True, [kck, "Qown"], [pk(bS)])
                                ACT(PTo[:, 2 * pr_:2 * pr_ + 2, :], PS[bS][:].rearrange("p (k x) -> p k x", k=2), AF.Exp,
                                    [pk(bS)], ["PTo"], scale=ATT_SCALE)
                            for pr_ in range(4):
                                bS = S.ps_from(ROT)
                                for kk in range(2):
                                    j = 2 * pr_ + kk
                                    MM(PS[bS][:, kk * 256:(kk + 1) * 256], KaTz[h % 2][:, g, j * 128:(j + 1) * 128], Qown[:, c, :],
                                       True, True, [kzk, "Qown"], [pk(bS)])
                                ACT(PTl[:, 2 * pr_:2 * pr_ + 2, :], PS[bS][:].rearrange("p (k x) -> p k x", k=2), AF.Exp,
                                    [pk(bS)], [("PTl", pr_ // 2)], scale=ATT_SCALE)
                            for m_ in range(2):
                                pv = PTl[:, 4 * m_:4 * m_ + 4, :].rearrange("p j x -> p (j x)")
                                TT("dve", pv, pv, mkb[m_], ALU.mult, [("PTl", m_), "mkb"], [("PTl", m_)])
                            for a_ in range(2):
                                mms = []
                                for kb in range(4):
                                    mms.append((PTo[:, kb, a_ * 128:(a_ + 1) * 128], Vc1[:, kb, g, 0:65], ["PTo", "Vc1"]))
                                for j in range(8):
                                    mms.append((PTl[:, j, a_ * 128:(a_ + 1) * 128], Va1[:, j, g, 0:65], [("PTl", j // 4), "Va1"]))
                                for n_, (l_, r_, k_) in enumerate(mms):
                                    MM(PS[ACC[a_]][:, hh * 65:(hh + 1) * 65], l_, r_, n_ == 0, n_ == len(mms) - 1, k_, [pk(ACC[a_])])
                        for a_ in range(2):
                            acc = PS[ACC[a_]][:, 0:260].rearrange("p (h e) -> p h e", h=4)
                            den = sml[:, 4 + a_, 0:4]
                            TT("dve", den, acc[:, :, 64], esink[:, hg * 4:(hg + 1) * 4], ALU.add, [pk(ACC[a_]), "rowb"], [("aden", a_)])
                            RCP(den, den, [("aden", a_)], [("aden", a_)])
                            dst = concat[:, a_, hg * 256:(hg + 1) * 256].rearrange("p (h e) -> p h e", h=4)
                            TT("dve", dst, acc[:, :, 0:64], den.unsqueeze(2).to_broadcast([128, 4, 64]), ALU.mult,
                               [pk(ACC[a_]), ("aden", a_)], [("att", a_)])
                        yield
            yield

        for _ in mlstm_gen():
            pass
        S.barrier()
        for _ in attn_gen():
            pass
        _ck(7)
        S.barrier()

        S.mark('o_attn')
        ntl = 2 if sample else 8
        if sample:
            for buf, lo, kf in ((concat, 512, lambda t_: [("hm", t_)]), (SG, 0, lambda t_: ["SG"])):
                d0 = buf[:, 0:2, lo:lo + 512]
                TS("dve", d0, d0, selv[:, 0:1], None, ALU.mult, None, kf(0) + kf(1) + ["selv"], kf(0) + kf(1))
                for q in range(1, 4):
                    STT("dve", d0, buf[:, 2 * q:2 * q + 2, lo:lo + 512], selv[:, q:q + 1], d0, ALU.mult, ALU.add,
                        kf(2 * q) + kf(2 * q + 1) + kf(0) + kf(1) + ["selv"], kf(0) + kf(1))
        sqs = BIG[:, 7200:15392].bitcast(F32)
        ssq = MXB[:, :, :].rearrange("p a b -> p (a b)")[:, 0:64]
        for t in range(ntl):
            ACT(sqs[:, t * 512:(t + 1) * 512], concat[:, t, 512:1024], AF.Square, [("hm", t)], ["sqs"])
        ssq = ssq[:, 0:ntl * 8]
        sqv = sqs[:, 0:ntl * 512].rearrange("p (a b) -> p a b", b=64)
        S.add("dve", lambda e: e.tensor_reduce(out=ssq, in_=sqv, axis=AX.X, op=ALU.add), ["sqs"], ["ssq"])
        ACT(ssq, ssq, AF.Sqrt, ["ssq", "epsc"], ["ssq"], bias=epsc[:, 0:1], scale=1.0 / 64)
        RCP(ssq, ssq, ["ssq"], ["ssq"])
        for t in range(ntl):
            hmv = concat[:, t, 512:1024].rearrange("p (h e) -> p h e", h=8)
            TT("dve", hmv, hmv, ssq[:, t * 8:(t + 1) * 8].unsqueeze(2).to_broadcast([128, 8, 64]), ALU.mult,
               [("hm", t), "ssq"], [("hm", t)])
            TT("dve", concat[:, t, 512:1024], concat[:, t, 512:1024], SG[:, t, :], ALU.mult, [("hm", t), "SG"], [("hm", t)])
        for t in range(ntl):
            for hf in range(2):
                b = S.ps()
                for j in range(4):
                    kc = hf * 4 + j
                    TR(PS[b][:, j * 128:(j + 1) * 128], concat[:, t, kc * 128:(kc + 1) * 128], ident,
                       ["cmat", ("hm", t), ("att", t)], [pk(b)])
                CP(ev_eng(), H[:, hf * 4:hf * 4 + 4, t * 128:(t + 1) * 128], PS[b][:].rearrange("p (a b) -> p a b", b=128),
                   [pk(b)], kH(t // 4))
        S.barrier()

        if sample:
            for buf, keyf in ((X, xkeys),):
                TS("dve", buf[:, :, QS[0]], buf[:, :, QS[0]], selv[:, 0:1], None, ALU.mult, None, keyf(0) + ["selv"], keyf(0))
                for q in range(1, 4):
                    STT("dve", buf[:, :, QS[0]], buf[:, :, QS[q]], selv[:, q:q + 1], buf[:, :, QS[0]], ALU.mult, ALU.add,
                        keyf(q) + keyf(0) + ["selv"], keyf(0))
            WIN[0], WIN[1] = 1, 256

        def epiO(ci, tb, b):
            CP(ev_eng(), Y[:, ci, wsl(tb)], PS[b][:, 0:WIN[1]], [pk(b)], kY(tb))

        proj_fm(oout_d, 8, [[(c * 128, 128)] for c in range(8)], srcH, kH, epiO)
        S.mark('o_outproj')

    for half in range(2):
        cond = half
        nseq = 4 if half == 0 else 1
        load_x(xp_d if half == 0 else xs_d)
        S.mark('x_loaded')
        if half == 0:
            g0 = mod_steps(0)
            for _ in range(4):
                next(g0)
            bg[0] = g0
            bgdone[0] = 4
            bg_every[0] = 5
        S.barrier()
        junction(None, (0, 0), cond)
        even_layer(cond, nseq)
        if half == 0:
            bg_flush()
            half_is0[0] = False
            bg[0] = mod_steps(1)
            bgdone[0] = 0
            bg_every[0] = 2
        junction((0, 0, True), (0, 1), cond)
        mlp(0, cond)
        bg_flush()
        junction((0, 1, True), (1, 0), cond)
        odd_layer(half, cond)
        own = (half == 1)
        junction((1, 0, False), (1, 1), cond, qs=(0,) if own else (0, 1, 2, 3))
        mlp(1, cond)
        junction((1, 1, True), None, cond, qs=(0,) if own else (0, 1, 2, 3))
        store_x(yp_d if half == 0 else ys_d, 2 if own else 8)
        WIN[0], WIN[1] = 2, 512
        S.barrier()

    S.emit(nc, ["out"])
    es.close()
    return nc


_CACHE = {}


def _consts():
    idn = np.eye(128, dtype=np.float32)
    u = np.arange(128)
    triU = (u[:, None] <= u[None, :]).astype(np.float32)
    triL = (u[:, None] >= u[None, :]).astype(np.float32)
    perm = np.zeros((128, 128), dtype=np.float32)
    for d in range(128):
        if d % 64 < 32:
            perm[d + 32, d] = -1.0
        else:
            perm[d - 32, d] = 1.0
    cmat = np.stack([idn, triU, triL, perm], axis=1).astype(np.float32)
    t = np.arange(1024)
    row = (t // 64).astype(np.float32)
    col = (t % 64).astype(np.float32)
    inv = (10000.0 ** (-np.arange(16, dtype=np.float32) / 16)).astype(np.float32)
    ang = np.concatenate([row[:, None] * inv, col[:, None] * inv], axis=-1).astype(np.float32)
    cos = np.cos(ang).astype(np.float32).T
    sin = np.sin(ang).astype(np.float32).T
    cosT = np.concatenate([cos, cos, cos, cos], axis=0)
    sinT = np.concatenate([sin, sin, sin, sin], axis=0)
    return cmat, np.ascontiguousarray(cosT), np.ascontiguousarray(sinT)


def _maskown(q):
    k = np.arange(128)[:, None]
    qq = np.arange(128)[None, :]
    m = np.zeros((128, 8, 2, 128), dtype=np.float32)
    for a in range(2):
        i = 2 * q + a
        for j in range(8):
            if j == i:
                m[:, j, a, :] = 1.0
            elif j == i - 1:
                m[:, j, a, :] = (k >= qq)
            elif j == i + 1:
                m[:, j, a, :] = (k <= qq)
    return np.ascontiguousarray(m.reshape(128, 2048))


def fm(v):
    v = np.asarray(v, dtype=np.float32)
    lead = v.shape[:-1]
    n = v.shape[-1] // 128
    r = v.reshape(lead + (n, 128))
    r = np.moveaxis(r, -1, 0)
    return np.ascontiguousarray(r)


def kernel(**inp):
    f = lambda k: np.ascontiguousarray(np.asarray(inp[k], dtype=np.float32))
    if "nc" not in _CACHE:
        _CACHE["nc"] = build_program()
    nc = _CACHE["nc"]
    cmat, cosT, sinT = _consts()
    x_prompt, x_sample = f("x_prompt"), f("x_sample")
    c, c_ctx = f("c"), f("c_ctx")
    mod_w, mod_b, norm_g = f("mod_w"), f("mod_b"), f("norm_g")
    modbT = np.ascontiguousarray(fm(mod_b))
    normgT = np.ascontiguousarray(fm(norm_g))
    convaT = np.ascontiguousarray(f("conv_a_w")[0].T.reshape(4, 128, 31).transpose(1, 0, 2))
    convbT = np.ascontiguousarray(f("conv_b_w")[0].T.reshape(4, 128, 3).transpose(1, 0, 2))
    evec = np.ascontiguousarray(np.stack([fm(f("conv_a_b")[0]), fm(f("ln_a_g")[0]), fm(f("ln_a_b")[0])], axis=1))
    rowv = np.concatenate([f("attn_sink")[0], f("gate_b")[0].reshape(-1), f("hnorm_g")[0]])[None, :].astype(np.float32)
    in_maps = []
    for core in range(8):
        b = core // 4
        condT = np.ascontiguousarray(np.stack([fm(c_ctx), fm(c[b])], axis=-1))

        def exp4(v):
            return np.repeat(v.reshape(4, 2).T, 64, axis=0)

        def n4(v):
            return v.reshape(4, 2, 64).transpose(1, 2, 0).reshape(128, 4)

        snm = np.stack([n4(f("state_n_fwd")[b, 0]), n4(f("state_n_bwd")[b, 0]),
                        exp4(f("state_m_fwd")[b, 0]), exp4(f("state_m_bwd")[b, 0])], axis=-1)
        in_maps.append({
            "xp": np.ascontiguousarray(x_prompt[core * 4:(core + 1) * 4].reshape(1024, 1024)),
            "xs": x_sample[b],
            "condT": condT, "mod_w": mod_w, "modbT": modbT, "normgT": normgT,
            "mlp_w1": f("mlp_w1"), "mlp_w2": f("mlp_w2"),
            "even_in_w": f("even_in_w")[0], "even_out_w": f("even_out_w")[0],
            "odd_in_w": f("odd_in_w")[0], "odd_out_w": f("odd_out_w")[0],
            "convaT": convaT, "convbT": convbT, "evec": evec, "rowv": rowv,
            "cache_k": f("cache_k")[b, 0], "cache_v": f("cache_v")[b, 0],
            "st_c_f": f("state_c_fwd")[b, 0], "st_c_b": f("state_c_bwd")[b, 0],
            "st_nm": np.ascontiguousarray(snm.astype(np.float32)),
            "cosT": cosT, "sinT": sinT, "cmat": cmat,
            "selv": np.ascontiguousarray(np.tile(np.eye(4, dtype=np.float32)[core % 4][None, :], (128, 1))),
            "maskown": _maskown(core % 4),
        })
    res = run_bass_kernel_spmd(nc, in_maps[:NCORES], core_ids=list(range(NCORES)))
    R = list(res.results) + [res.results[0]] * (8 - NCORES)
    yp = np.concatenate([R[i]["yp"].reshape(4, 256, 1024) for i in range(8)], axis=0)
    ys = np.stack([np.concatenate([R[4 * b_ + q_]["ys"] for q_ in range(4)], axis=0) for b_ in range(2)], axis=0)
    cat = lambda k: np.concatenate([R[i][k] for i in range(8)], axis=0)
    nk = cat("nk")[:, None]
    nv = cat("nv")[:, None]
    return (yp, ys, nk, nv, cat("ncf")[:, None], cat("nnf")[:, None], cat("nmf")[:, None],
            cat("ncb")[:, None], cat("nnb")[:, None], cat("nmb")[:, None])
```

```python
import os
import numpy as np
import concourse.bass as bass
import concourse.mybir as mybir
from concourse.bass_utils import run_bass_kernel_spmd

F32 = mybir.dt.float32
BF16 = mybir.dt.bfloat16
AF = mybir.ActivationFunctionType
ALU = mybir.AluOpType
AX = mybir.AxisListType

D = 1024
T = 1024
EPS = 1e-6
SAME_ENG_SYNC = True
STAGE = 99
SUB = 99
NCORES = 8


class _Stop(Exception):
    pass


def _ck(k):
    if SUB <= k:
        raise _Stop()


class Op:
    __slots__ = ("eng", "fn", "deps", "dma", "dcount", "dwaits", "tick", "inc", "idx")


class Sched:
    ENGS = ("pe", "act", "dve", "pool", "sp")

    def __init__(self):
        self.ops = []
        self.lastw = {}
        self.readers = {}
        self.dcount = {}
        self.pool_dmas = []
        self.psi = 0
        self.nps = 7
        self.bar_deps = set()
        self.bar_dw = {}
        self.rot = {}
        self.marks = []
        self.multiw = {}

    def add(self, eng, fn, r=(), w=(), dma=None):
        op = Op()
        op.eng, op.fn, op.dma = eng, fn, dma
        op.idx = len(self.ops)
        deps = set()
        for k in r:
            if k in self.lastw:
                deps.add(self.lastw[k])
            if k in self.multiw:
                deps.update(self.multiw[k])
            if isinstance(k, tuple) and k[0] == "ps":
                for ridx in self.readers.get(k, ()):
                    if self.ops[ridx].eng != eng:
                        deps.add(ridx)
        for k in w:
            if k in self.lastw:
                deps.add(self.lastw[k])
            if k in self.multiw:
                deps.update(self.multiw.pop(k))
            deps.update(self.readers.get(k, ()))
        if dma is not None and eng == "pool":
            if len(self.pool_dmas) >= 4:
                deps.add(self.pool_dmas[-4])
            self.pool_dmas.append(op.idx)
        deps.update(self.bar_deps)
        deps.discard(op.idx)
        op.deps = deps
        op.dwaits = dict(self.bar_dw)
        for d in deps:
            P = self.ops[d]
            if P.dma is not None:
                op.dwaits[P.dma] = self.dcount[P.dma]
        if dma is not None:
            self.dcount[dma] = self.dcount.get(dma, 0) + 16
            op.dcount = self.dcount[dma]
        op.tick = 0
        op.inc = False
        for k in r:
            self.readers.setdefault(k, []).append(op.idx)
        for k in w:
            self.lastw[k] = op.idx
            self.readers[k] = []
        self.ops.append(op)
        return op

    def mark(self, name):
        self.marks.append((name, sum(1 for o in self.ops if o.eng == 'pe')))

    def ps(self):
        i = self.psi
        self.psi = (self.psi + 1) % self.nps
        return i

    def ps_from(self, lst):
        k = tuple(lst)
        j = self.rot.get(k, 0)
        self.rot[k] = (j + 1) % len(lst)
        return lst[j]

    def alias(self, fine_keys, coarse):
        idxs = [self.lastw[k] for k in fine_keys if k in self.lastw]
        for k in fine_keys:
            idxs.extend(self.readers.get(k, ()))
        self.multiw.setdefault(coarse, set()).update(idxs)
        self.readers.setdefault(coarse, [])

    def barrier(self):
        last = {}
        for op in self.ops:
            if op.dma is None:
                last[op.eng] = op.idx
        self.bar_deps = set(last.values())
        self.bar_dw = dict(self.dcount)

    def emit(self, nc, final_waits):
        ops = self.ops
        for op in ops:
            for d in op.deps:
                P = ops[d]
                if P.dma is not None:
                    continue
                if P.eng == op.eng and (P.eng == "pe" or not SAME_ENG_SYNC):
                    continue
                P.inc = True
        cnt = {e: 0 for e in self.ENGS}
        for op in ops:
            if op.dma is None and op.inc:
                cnt[op.eng] += 1
                op.tick = cnt[op.eng]
        nops = {e: sum(1 for o in ops if o.eng == e) for e in self.ENGS}
        import contextlib
        with contextlib.ExitStack() as es:
            esem = {e: es.enter_context(nc.semaphore("e_" + e)) for e in self.ENGS}
            dsem = {k: es.enter_context(nc.semaphore("d_" + str(k))) for k in self.dcount}
            block = es.enter_context(nc.Block())

            def run(eng_name):
                def body(e):
                    waited = {}
                    for op in ops:
                        if op.eng != eng_name:
                            continue
                        waits = {}
                        for d in op.deps:
                            P = ops[d]
                            if P.dma is not None:
                                continue
                            if P.eng == op.eng and (P.eng == "pe" or not SAME_ENG_SYNC):
                                continue
                            key = ("e", P.eng)
                            waits[key] = max(waits.get(key, 0), P.tick)
                        for k, v in op.dwaits.items():
                            waits[("d", k)] = max(waits.get(("d", k), 0), v)
                        for key, v in waits.items():
                            if waited.get(key, 0) < v:
                                sem = esem[key[1]] if key[0] == "e" else dsem[key[1]]
                                e.wait_ge(sem, v)
                                waited[key] = v
                        inst = op.fn(e)
                        if op.dma is not None:
                            inst.then_inc(dsem[op.dma], 16)
                        elif op.inc:
                            inst.then_inc(esem[op.eng], 1)
                    if eng_name == "sp":
                        for k in final_waits:
                            if k in self.dcount:
                                e.wait_ge(dsem[k], self.dcount[k])
                return body

            block.tensor(run("pe"))
            block.scalar(run("act"))
            block.vector(run("dve"))
            block.gpsimd(run("pool"))
            block.sync(run("sp"))


def build_program():
    nc = bass.Bass("TRN2", target_bir_lowering=False)
    S = Sched()

    def din(name, shape, dt=F32):
        return nc.dram_tensor(name, list(shape), dt, kind="ExternalInput").ap()

    def dout(name, shape):
        return nc.dram_tensor(name, list(shape), F32, kind="ExternalOutput").ap()

    xp_d = din("xp", [1024, 1024])
    xs_d = din("xs", [1024, 1024])
    condT_d = din("condT", [128, 8, 2])
    modw_d = din("mod_w", [2, 1024, 6144])
    modbT_d = din("modbT", [128, 2, 48])
    normgT_d = din("normgT", [128, 2, 4, 8])
    w1_d = din("mlp_w1", [2, 1024, 4096])
    w2_d = din("mlp_w2", [2, 4096, 1024])
    ein_d = din("even_in_w", [1024, 2560])
    eout_d = din("even_out_w", [1024, 1024])
    oin_d = din("odd_in_w", [1024, 2848])
    oout_d = din("odd_out_w", [1024, 1024])
    convaT_d = din("convaT", [128, 4, 31])
    convbT_d = din("convbT", [128, 4, 3])
    evec_d = din("evec", [128, 3, 4])
    rowv_d = din("rowv", [1, 552])
    ck_d = din("cache_k", [2, 512, 64])
    cv_d = din("cache_v", [2, 512, 64])
    scf_d = din("st_c_f", [8, 64, 64])
    scb_d = din("st_c_b", [8, 64, 64])
    snm_d = din("st_nm", [128, 4, 4])
    cosT_d = din("cosT", [128, 1024])
    sinT_d = din("sinT", [128, 1024])
    sel_d = din("selv", [128, 4])
    mown_d = din("maskown", [128, 2048])
    cmat_d = din("cmat", [128, 4, 128])

    yp_d = dout("yp", [1024, 1024])
    ys_d = dout("ys", [256, 1024])
    nk_d = dout("nk", [4, 2, 256, 64])
    nv_d = dout("nv", [4, 2, 256, 64])
    ncf_d = dout("ncf", [4, 8, 64, 64])
    nnf_d = dout("nnf", [4, 8, 64])
    nmf_d = dout("nmf", [4, 8])
    ncb_d = dout("ncb", [4, 8, 64, 64])
    nnb_d = dout("nnb", [4, 8, 64])
    nmb_d = dout("nmb", [4, 8])

    import contextlib
    es = contextlib.ExitStack()

    def sb(name, shape, dt=F32):
        return es.enter_context(nc.sbuf_tensor(name, list(shape), dt))

    X = sb("X", [128, 8, T])
    H = sb("H", [128, 8, T], BF16)
    Y = sb("Y", [128, 8, T])
    BIG = sb("BIG", [128, 32768], BF16)
    WR = [sb("WR%d" % i, [128, 4096], BF16) for i in range(3)]
    XIN = [sb("XIN%d" % i, [128, 1024]) for i in range(2)]
    MR = [XIN[i][:, :].rearrange("p (k c) -> p k c", c=128) for i in range(2)]
    RS = [sb("RS%d" % i, [128, 512]) for i in range(2)]
    RS2 = sb("RS2", [128, 512])
    cmat = sb("cmat_sb", [128, 4, 128])
    ident = cmat[:, 0, :]
    triU = cmat[:, 1, :]
    triL = cmat[:, 2, :]
    cbf = sb("cbf", [128, 4, 128], BF16)
    ones_f = sb("ones_f", [128, 128])
    mask4 = sb("mask4", [128, 2, 4, 128], BF16)
    epsc = sb("epsc", [128, 1])
    condT = sb("condT_sb", [128, 8, 2])
    modb = sb("modb", [128, 2, 48])
    normg = sb("normg", [128, 2, 4, 8])
    modsb = sb("modsb", [128, 2, 48, 2])
    dvec = sb("dvec", [128, 2, 6, 8, 2])
    convaT = sb("convaT_sb", [128, 4, 31])
    convbT = sb("convbT_sb", [128, 4, 3])
    evec = sb("evec_sb", [128, 3, 4])
    rowb = sb("rowb", [128, 552])
    snm = sb("snm_sb", [128, 4, 4])
    selv = sb("selv_sb", [128, 4])
    SM = sb("SM", [128, 8, 8, 16])
    Cst = sb("Cst", [128, 2, 4, 65])
    Cz = sb("Cz", [128, 2, 8, 66], BF16)
    Ctmp = sb("Ctmp", [128, 4, 65])
    MXB = sb("MXB", [128, 8, 16])
    mrec = sb("mrec", [128, 4, 16])
    CoutS = sb("CoutS", [128, 3, 4, 65])
    DgT = sb("DgT", [128, 128])
    sml = sb("sml", [128, 8, 8])
    d3 = BIG[:, 30240:31776].rearrange("p (k m) -> p k m", m=128)

    PS = [es.enter_context(nc.psum_tensor("ps%d" % i, [128, 512], F32)) for i in range(8)]

    def pk(i):
        return ("ps", i)

    def MM(out, lhsT, rhs, start, stop, r, w, tp=None):
        if False:
            return S.add("pe", lambda e: e.matmul(out, lhsT=lhsT, rhs=rhs, start=start, stop=stop, tile_position=tp), r, w)
        return S.add("pe", lambda e: e.matmul(out, lhsT=lhsT, rhs=rhs, start=start, stop=stop), r, w)

    def TR(out, in_, idn, r, w):
        return S.add("pe", lambda e: e.transpose(out=out, in_=in_, identity=idn), r, w)

    def ACT(out, in_, func, r, w, bias=None, scale=None, eng="act"):
        kw = {}
        if bias is not None:
            kw["bias"] = bias
        if scale is not None:
            kw["scale"] = scale
        return S.add("act", lambda e: e.activation(out=out, in_=in_, func=func, **kw), r, w)

    def TT(eng, out, in0, in1, op, r, w):
        return S.add(eng, lambda e: e.tensor_tensor(out=out, in0=in0, in1=in1, op=op), r, w)

    def TS(eng, out, in0, s1, s2, op0, op1, r, w):
        if s2 is None:
            return S.add(eng, lambda e: e.tensor_scalar(out=out, in0=in0, scalar1=s1, scalar2=None, op0=op0), r, w)
        return S.add(eng, lambda e: e.tensor_scalar(out=out, in0=in0, scalar1=s1, scalar2=s2, op0=op0, op1=op1), r, w)

    def STT(eng, out, in0, sc, in1, op0, op1, r, w):
        return S.add(eng, lambda e: e.scalar_tensor_tensor(out=out, in0=in0, scalar=sc, in1=in1, op0=op0, op1=op1), r, w)

    def CP(eng, out, in_, r, w):
        if eng == "act":
            return S.add("act", lambda e: e.activation(out=out, in_=in_, func=AF.Copy), r, w)
        return S.add(eng, lambda e: e.tensor_copy(out=out, in_=in_), r, w)

    def MS(eng, ap, val, w):
        return S.add(eng, lambda e: e.memset(ap, val), (), w)

    def RCP(out, in_, r, w):
        return S.add("dve", lambda e: e.reciprocal(out=out, in_=in_), r, w)

    def DMA(eng, out, in_, dkey, r, w, slow=False):
        if slow:
            return S.add(eng, lambda e: e.dma_start(out=out, in_=in_, allow_slow_non_contiguous=True), r, w, dma=dkey)
        return S.add(eng, lambda e: e.dma_start(out=out, in_=in_), r, w, dma=dkey)

    alt = [0]

    def ev_eng():
        alt[0] ^= 1
        return "act" if alt[0] else "dve"

    DMA("sp", cmat[:], cmat_d, "const", (), ["cmat"])
    DMA("sp", condT[:], condT_d, "const", (), ["condT"])
    DMA("sp", modb[:], modbT_d, "const", (), ["modb"])
    DMA("sp", normg[:], normgT_d, "const", (), ["normg"])
    DMA("sp", convaT[:], convaT_d, "const", (), ["convaT"])
    DMA("sp", convbT[:], convbT_d, "const", (), ["convbT"])
    DMA("sp", evec[:], evec_d, "const", (), ["evec"])
    DMA("sp", rowb[0:1, :], rowv_d, "const", (), ["rowv"])
    DMA("sp", snm[:], snm_d, "const", (), ["snm"])
    DMA("sp", selv[:], sel_d, "const", (), ["selv"])
    MS("dve", ones_f[:], 1.0, ["ones_f"])
    MS("dve", Cz[:], 0.0, [("Cz", 0), ("Cz", 1)])
    MS("dve", epsc[:], EPS, ["epsc"])
    MS("dve", cbf[:, 3, :], 1.0, ["cbf"])
    CP("dve", cbf[:, 0:3, :], cmat[:, 0:3, :], ["cmat"], ["cbf"])
    for k in range(4):
        CP("dve", mask4[:, 0, k, :], cmat[:, 1, :], ["cmat"], ["mask4"])
        CP("dve", mask4[:, 1, k, :], cmat[:, 2, :], ["cmat"], ["mask4"])
    ones_b = cbf[:, 3, :]
    ident_b = cbf[:, 0, :]
    scTb = sb("scTb", [128, 8, 2], BF16)
    ACT(scTb[:], condT[:], AF.Silu, ["condT"], ["scT"])
    for c0 in (0, 276):
        MM(PS[7][:, 0:276], ones_f[0:1, :], rowb[0:1, c0:c0 + 276], True, True, ["ones_f", "rowv"], [pk(7)])
        CP("dve", rowb[:, c0:c0 + 276], PS[7][:, 0:276], [pk(7)], ["rowb", "rowv"])
    ACT(rowb[:, 0:8], rowb[:, 0:8], AF.Exp, ["rowb"], ["rowb"])
    esink = rowb[:, 0:8]
    gateb = rowb[:, 8:40]
    hng = rowb[:, 40:552]

    def mod_steps(l):
        for cb in range(12):
            view, wkey = load_slab(modw_d[l], 8, [(0, cb * 512, 512)])
            b = S.ps()
            for kc in range(8):
                MM(PS[b][0:2, :], scTb[:, kc, :], view[:, kc, :], kc == 0, kc == 7, [wkey, "scT"], [pk(b)])
            mrow = RS2
            CP("dve", mrow[0:2, :], PS[b][0:2, :], [pk(b)], ["RS2"])
            for j in range(4):
                oc = cb * 4 + j
                MM(PS[7][:, oc * 2:oc * 2 + 2], mrow[0:2, j * 128:(j + 1) * 128], ident[0:2, 0:2], True, True,
                   ["RS2", "cmat"], [pk(7)])
            if cb % 2 == 1:
                grp = cb // 2
                TT("dve", modsb[:, l, grp * 8:(grp + 1) * 8, :],
                   PS[7][:, grp * 16:(grp + 1) * 16].rearrange("p (a b) -> p a b", b=2),
                   modb[:, l, grp * 8:(grp + 1) * 8].unsqueeze(2).to_broadcast([128, 8, 2]), ALU.add,
                   [pk(7), "modb"], [("modsb", l, grp)])
                src = modsb[:, l, grp * 8:(grp + 1) * 8, :]
                if grp in (0, 3):
                    j = 1 if grp == 0 else 4
                    CP("dve", dvec[:, l, j, :, :], src, [("modsb", l, grp)], [("dvec", l, j)])
                elif grp in (1, 4):
                    j = 0 if grp == 1 else 3
                    gi = 0 if grp == 1 else 2
                    TS("dve", dvec[:, l, j, :, :], src, 1.0, None, ALU.add, None, [("modsb", l, grp)], [("dvec", l, j)])
                    TT("dve", dvec[:, l, j, :, :], dvec[:, l, j, :, :],
                       normg[:, l, gi, :].unsqueeze(2).to_broadcast([128, 8, 2]), ALU.mult, [("dvec", l, j), "normg"],
                       [("dvec", l, j)])
                else:
                    j = 2 if grp == 2 else 5
                    gi = 1 if grp == 2 else 3
                    TT("dve", dvec[:, l, j, :, :], src, normg[:, l, gi, :].unsqueeze(2).to_broadcast([128, 8, 2]), ALU.mult,
                       [("modsb", l, grp), "normg"], [("dvec", l, j)])
            yield

    def run_all(gen):
        for _ in gen:
            pass

    bg = [None]
    half_is0 = [True]
    bgcnt = [0]
    bg_every = [1]

    def bg_step(n=1):
        if bg[0] is None:
            return
        bgcnt[0] += 1
        if bgcnt[0] % bg_every[0] != 0:
            return
        bg_force(n)

    bgdone = [0]

    def bg_force(n=1):
        for _ in range(n):
            if bg[0] is None:
                return
            try:
                next(bg[0])
                bgdone[0] += 1
            except StopIteration:
                bg[0] = None
                return

    def bg_ensure(n_done):
        while bg[0] is not None and bgdone[0] < n_done:
            bg_force(1)

    def bg_flush():
        if bg[0] is not None:
            run_all(bg[0])
            bg[0] = None

    wri = [0]

    def load_slab(Wd, KC, pieces):
        ncol = max(p[0] + p[2] for p in pieces)
        i = wri[0]
        wri[0] = (wri[0] + 1) % 3
        view = WR[i][:, 0:KC * ncol].rearrange("p (k c) -> p k c", c=ncol)
        for (off, c0, wd) in pieces:
            kstep = max(1, min(KC, 2048 // wd))
            for k0 in range(0, KC, kstep):
                DMA("pool", view[:, k0:k0 + kstep, off:off + wd],
                    Wd[k0 * 128:(k0 + kstep) * 128, c0:c0 + wd].rearrange("(kc p) c -> p kc c", p=128),
                    "wr%d" % i, (), [("WR", i)])
        return view, ("WR", i)

    WIN = [2, 512]

    def wsl(tb):
        return slice(tb * WIN[1], (tb + 1) * WIN[1])

    def kX(tb):
        return [("X", t) for t in range(4 * tb, 4 * tb + 4)]

    def kH(tb):
        if WIN[1] == 256:
            return [("H", tb)]
        return [("H", 2 * tb), ("H", 2 * tb + 1)]

    def kY(tb):
        if WIN[1] == 256:
            return [("Y", tb)]
        return [("Y", 2 * tb), ("Y", 2 * tb + 1)]

    def load_x(xd):
        for t in range(8):
            xin = XIN[t % 2]
            DMA("sp", xin[:], xd[t * 128:(t + 1) * 128, :], "xin%d" % (t % 2), (), [("XIN", t % 2)])
            for hf in range(2):
                b = S.ps()
                for j in range(4):
                    kc = hf * 4 + j
                    TR(PS[b][:, j * 128:(j + 1) * 128], xin[:, kc * 128:(kc + 1) * 128], ident,
                       [("XIN", t % 2), "cmat"], [pk(b)])
                CP(ev_eng(), X[:, hf * 4:hf * 4 + 4, t * 128:(t + 1) * 128],
                   PS[b][:].rearrange("p (a b) -> p a b", b=128), [pk(b)], [("X", t)])

    def store_x(yd, ntiles=8):
        for t in range(ntiles):
            xin = XIN[t % 2]
            for hf in range(2):
                b = S.ps()
                for j in range(4):
                    kc = hf * 4 + j
                    TR(PS[b][:, j * 128:(j + 1) * 128], X[:, kc, t * 128:(t + 1) * 128], ident,
                       [("X", t), "cmat"], [pk(b)])
                CP(ev_eng(), xin[:, hf * 512:(hf + 1) * 512], PS[b][:], [pk(b)], [("XIN", t % 2)])
            DMA("sp", yd[t * 128:(t + 1) * 128, :], xin[:], "out", [("XIN", t % 2)], [])

    def rstd_from_ps(b, rs, scale, rkey):
        ACT(rs[:], PS[b][:], AF.Sqrt, [pk(b), "epsc"], [rkey], bias=epsc[:, 0:1], scale=scale)
        RCP(rs[:], rs[:], [rkey], [rkey])

    def rsq(q):
        return RS[q // 2][:, (q % 2) * 256:(q % 2 + 1) * 256], ("RSq", q)

    QS = [slice(q * 256, (q + 1) * 256) for q in range(4)]

    def stats_all(srcbuf, srckeys_fn, presq=False):
        if not presq:
            for q in range(4):
                ACT(H[:, :, QS[q]], srcbuf[:, :, QS[q]], AF.Square, srckeys_fn(q), [("H", q)])
        banks = []
        for q in range(4):
            b = S.ps()
            banks.append(b)
            for kc in range(8):
                MM(PS[b][:, 0:256], ones_b, H[:, kc, QS[q]], kc == 0, kc == 7, [("H", q), "cbf"], [pk(b)])
        for q in range(4):
            rs, rk = rsq(q)
            ACT(rs, PS[banks[q]][:, 0:256], AF.Ln, [pk(banks[q]), "epsc"], [rk], bias=epsc[:, 0:1], scale=1.0 / D)
        for q in range(4):
            rs, rk = rsq(q)
            ACT(rs, rs, AF.Exp, [rk], [rk], scale=-0.5)

    def xkeys(q):
        return [("X", 2 * q), ("X", 2 * q + 1)]

    def modnorm(l, j, cond):
        gs = dvec[:, l, 3 * j, :, cond:cond + 1]
        sh = dvec[:, l, 3 * j + 1, :, cond:cond + 1]
        dk = [("dvec", l, 3 * j), ("dvec", l, 3 * j + 1)]
        stats_all(X, xkeys)
        for q in range(4):
            rs, rk = rsq(q)
            TT("dve", Y[:, :, QS[q]], X[:, :, QS[q]], rs.unsqueeze(1).to_broadcast([128, 8, 256]), ALU.mult,
               xkeys(q) + [rk], [("Y", q)])
        for q in range(4):
            for kc in range(8):
                ACT(H[:, kc, QS[q]], Y[:, kc, QS[q]], AF.Identity, [("Y", q)] + dk, [("H", q)],
                    bias=sh[:, kc, :], scale=gs[:, kc, :])

    def gated_out(l, j, cond, prescaled=False):
        gg = dvec[:, l, 3 * j + 2, :, cond:cond + 1]
        stats_all(Y, lambda q: [("Y", q)], presq=prescaled)
        for q in range(4):
            rs, rk = rsq(q)
            TT("dve", Y[:, :, QS[q]], Y[:, :, QS[q]], rs.unsqueeze(1).to_broadcast([128, 8, 256]), ALU.mult,
               [("Y", q), rk], [("Y", q)])
            if not prescaled:
                TT("dve", Y[:, :, QS[q]], Y[:, :, QS[q]], gg.to_broadcast([128, 8, 256]), ALU.mult,
                   [("Y", q), ("dvec", l, 3 * j + 2)], [("Y", q)])
        for q in range(4):
            TT("dve", X[:, 0:6, QS[q]], X[:, 0:6, QS[q]], Y[:, 0:6, QS[q]], ALU.add, [("Y", q)] + xkeys(q), [("Xa", q)])
            TT("pool", X[:, 6:8, QS[q]], X[:, 6:8, QS[q]], Y[:, 6:8, QS[q]], ALU.add, [("Y", q)] + xkeys(q), [("Xb", q)])
        for q in range(4):
            for t_ in (2 * q, 2 * q + 1):
                S.alias([("Xa", q), ("Xb", q)], ("X", t_))

    def junction(go, mn, cond, qs=(0, 1, 2, 3)):
        def st(q):
            b = S.ps()
            for kc in range(8):
                MM(PS[b][:, 0:256], ones_b, H[:, kc, QS[q]], kc == 0, kc == 7, [("H", q), "cbf"], [pk(b)])
            rs, rk = rsq(q)
            ACT(rs, PS[b][:, 0:256], AF.Ln, [pk(b), "epsc"], [rk], bias=epsc[:, 0:1], scale=1.0 / D)
            ACT(rs, rs, AF.Exp, [rk], [rk], scale=-0.5)

        def A(q):
            l, j, pres = go
            if not pres:
                ACT(H[:, :, QS[q]], Y[:, :, QS[q]], AF.Square, [("Y", q)], [("H", q)])
            st(q)

        def B(q):
            l, j, pres = go
            gg = dvec[:, l, 3 * j + 2, :, cond:cond + 1]
            rs, rk = rsq(q)
            TT("dve", Y[:, :, QS[q]], Y[:, :, QS[q]], rs.unsqueeze(1).to_broadcast([128, 8, 256]), ALU.mult,
               [("Y", q), rk], [("Y", q)])
            if not pres:
                TT("dve", Y[:, :, QS[q]], Y[:, :, QS[q]], gg.to_broadcast([128, 8, 256]), ALU.mult,
                   [("Y", q), ("dvec", l, 3 * j + 2)], [("Y", q)])
            TT("dve", X[:, :, QS[q]], X[:, :, QS[q]], Y[:, :, QS[q]], ALU.add, [("Y", q)] + xkeys(q), xkeys(q))

        def C(q):
            ACT(H[:, :, QS[q]], X[:, :, QS[q]], AF.Square, xkeys(q), [("H", q)])
            st(q)

        def Dq(q):
            rs, rk = rsq(q)
            TT("dve", Y[:, :, QS[q]], X[:, :, QS[q]], rs.unsqueeze(1).to_broadcast([128, 8, 256]), ALU.mult,
               xkeys(q) + [rk], [("Y", q)])

        def E(q):
            l, j = mn
            gs = dvec[:, l, 3 * j, :, cond:cond + 1]
            sh = dvec[:, l, 3 * j + 1, :, cond:cond + 1]
            dk = [("dvec", l, 3 * j), ("dvec", l, 3 * j + 1)]
            for kc in range(8):
                ACT(H[:, kc, QS[q]], Y[:, kc, QS[q]], AF.Identity, [("Y", q)] + dk, [("H", q)],
                    bias=sh[:, kc, :], scale=gs[:, kc, :])

        order = [(A, 0), (A, 1), (A, 2), (A, 3), (B, 0), (C, 0), (B, 1), (C, 1), (B, 2), (C, 2), (B, 3), (C, 3),
                 (Dq, 0), (E, 0), (Dq, 1), (E, 1), (Dq, 2), (E, 2), (Dq, 3), (E, 3)]
        for fn, q in order:
            if q not in qs:
                continue
            if fn in (A, B) and go is None:
                continue
            if fn in (C, Dq, E) and mn is None:
                continue
            fn(q)

    def epi_scaled(l, j, cond):
        gg = dvec[:, l, 3 * j + 2, :, cond:cond + 1]

        def epi_(ci, tb, b):
            sl = wsl(tb)
            W_ = WIN[1]
            ACT(Y[:, ci, sl], PS[b][:, 0:W_], AF.Copy, [pk(b), ("dvec", l, 3 * j + 2)], kY(tb), scale=gg[:, ci, :])
            ACT(H[:, ci, sl], PS[b][:, 0:W_], AF.Square, [pk(b)], kH(tb))
        return epi_

    def proj_fm(Wd, KC, chunks, src, src_keys, epi, group=2, lead=0):
        def load_group(g0):
            grp = chunks[g0:g0 + group]
            pieces = []
            for gi, ch in enumerate(grp):
                off = gi * 128
                for (c0, wd) in ch:
                    pieces.append((off, c0, wd))
                    off += wd
            merged = []
            for p in pieces:
                if merged and merged[-1][0] + merged[-1][2] == p[0] and merged[-1][1] + merged[-1][2] == p[1]:
                    merged[-1] = (merged[-1][0], merged[-1][1], merged[-1][2] + p[2])
                else:
                    merged.append(p)
            view, wkey = load_slab(Wd, KC, merged)
            return grp, view, wkey

        def run(g0, grp, view, wkey, tb):
            for gi in range(len(grp)):
                b = S.ps()
                for kc in range(KC):
                    MM(PS[b][:, 0:WIN[1]], view[:, kc, gi * 128:(gi + 1) * 128], src(kc, tb), kc == 0, kc == KC - 1,
                       [wkey] + src_keys(tb), [pk(b)])
                epi(g0 + gi, tb, b)

        starts = list(range(0, len(chunks), group))
        nlead = lead if (WIN[0] == 2 and len(starts) >= lead) else 0
        if nlead:
            loaded = [(g0,) + load_group(g0) for g0 in starts[:nlead]]
            for tb in range(2):
                for (g0, grp, view, wkey) in loaded:
                    run(g0, grp, view, wkey, tb)
            for _ in range(nlead):
                bg_step()
        for g0 in starts[nlead:]:
            grp, view, wkey = load_group(g0)
            for tb in range(WIN[0]):
                run(g0, grp, view, wkey, tb)
            bg_step()

    def srcH(kc, tb):
        return H[:, kc, wsl(tb)]

    hid = BIG[:, :].rearrange("p (a b) -> p a b", b=T)

    def mlp(l, cond):

        def epiA(ci, tb, b):
            sl = wsl(tb)
            W_ = WIN[1]
            if W_ == 256:
                ti = ci % 2
                tmp = (RS[1], RS2)[ti]
                tk = (("RS", 1), "RS2")[ti]
            else:
                ti = (ci * 2 + tb) % 3
                tmp = (RS[0], RS[1], RS2)[ti]
                tk = (("RS", 0), ("RS", 1), "RS2")[ti]
            ACT(tmp[:, 0:W_], PS[b][:, 0:W_], AF.Relu, [pk(b)], [tk])
            TT("dve", hid[:, ci, sl], tmp[:, 0:W_], tmp[:, 0:W_], ALU.mult, [tk], [("hid", ci, tb)])

        proj_fm(w1_d[l], 8, [[(c * 128, 128)] for c in range(32)], srcH, kH, epiA, group=4, lead=3)
        S.mark('mlpA')

        def srcHid(kc, tb):
            return hid[:, kc, wsl(tb)]

        def keysHid(tb):
            return [("hid", c, tb) for c in range(32)]

        if WIN[1] == 256:
            epi2 = epi_scaled(l, 1, cond)
            banks = list(range(8))
            for sidx in range(8):
                view, wkey = load_slab(w2_d[l][sidx * 512:(sidx + 1) * 512, :], 4, [(0, 0, 1024)])
                for oc in range(8):
                    for kc in range(4):
                        hc = sidx * 4 + kc
                        MM(PS[banks[oc]][:, 0:256], view[:, kc, oc * 128:(oc + 1) * 128],
                           hid[:, hc, 0:256], sidx == 0 and kc == 0, sidx == 7 and kc == 3,
                           [wkey, ("hid", hc, 0)], [pk(banks[oc])])
            W_ = 256
            gg_ = dvec[:, l, 5, :, cond:cond + 1]
            for oc in range(8):
                src_ = PS[banks[oc]][:, 0:256]
                ACT(Y[:, oc, 0:256], src_, AF.Copy, [pk(banks[oc]), ("dvec", l, 5)], kY(0), scale=gg_[:, oc, :])
                ACT(H[:, oc, 0:256], src_, AF.Square, [pk(banks[oc])], kH(0))
        else:
            proj_fm(w2_d[l], 32, [[(c * 128, 128)] for c in range(8)], srcHid, keysHid, epi_scaled(l, 1, cond), group=1)
        S.mark('mlpB')
        pass

    def even_layer(cond, nseq):
        L = T // nseq
        LP = L + 30
        LC = L + 2
        apad = BIG[:, 0:4576].rearrange("p (c x) -> p c x", c=4)
        cxpad = BIG[:, 4576:8736].rearrange("p (c x) -> p c x", c=4)
        bgt = BIG[:, 8736:12832].rearrange("p (c x) -> p c x", c=4)
        ac = BIG[:, 12832:21024].bitcast(F32).rearrange("p (c x) -> p c x", c=4)
        U = BIG[:, 21024:29216].rearrange("p (c x) -> p c x", c=8)
        sgt = BIG[:, 29216:30240].bitcast(F32)
        diag = Y[:, :, :].rearrange("p a b -> p (a b)").bitcast(BF16)[:, 0:15872].rearrange("p (k m) -> p k m", m=128)

        MS("dve", diag[:, 0, 0:2], 0.0, ["Yclaim"] + [("Y", q) for q in range(4)])

        def diag_gen():
            for c in range(4):
                for k in range(31):
                    TS("dve", diag[:, c * 31 + k, :], ident_b, convaT[:, c, k:k + 1], None, ALU.mult, None,
                       ["cbf", "convaT", "Yclaim"], [("diag", c, k)])
                    yield
        dgen = [diag_gen()]

        def diag_step(n):
            for _ in range(n):
                if dgen[0] is None:
                    return
                try:
                    next(dgen[0])
                except StopIteration:
                    dgen[0] = None
        MS("dve", apad[:], 0.0, ["apad"])
        MS("dve", cxpad[:], 0.0, ["cxpad"])

        def padview(buf, c, tb, LPx, padl):
            if nseq == 1:
                return buf[:, c, padl + tb * 512: padl + (tb + 1) * 512]
            s0 = tb * 2
            return buf[:, c, s0 * LPx:(s0 + 2) * LPx].rearrange("p (s x) -> p s x", s=2)[:, :, padl:padl + L]

        def psview(b):
            if nseq == 1:
                return PS[b][:]
            return PS[b][:].rearrange("p (s x) -> p s x", s=2)

        hold = {}

        def epi(ci, tb, b):
            diag_step(4)
            grp, c = ci // 2 // 4, None
            if ci < 8:
                c = ci // 2
                if ci % 2 == 0:
                    hold[(tb, "v")] = b
                else:
                    bv = hold[(tb, "v")]
                    ACT(sgt[:], PS[b][:], AF.Sigmoid, [pk(b)], ["sgt"])
                    dst = padview(apad, c, tb, LP, 15)
                    sv = sgt[:] if nseq == 1 else sgt[:].rearrange("p (s x) -> p s x", s=2)
                    TT("dve", dst, psview(bv), sv, ALU.mult, [pk(bv), "sgt"], ["apad"])
            elif ci < 16:
                c = (ci - 8) // 2
                if ci % 2 == 0:
                    hold[(tb, "v")] = b
                else:
                    bv = hold[(tb, "v")]
                    ACT(sgt[:], PS[b][:], AF.Copy, [pk(b)], ["sgt"])
                    dst = padview(cxpad, c, tb, LC, 1)
                    sv = sgt[:] if nseq == 1 else sgt[:].rearrange("p (s x) -> p s x", s=2)
                    TT("dve", dst, psview(bv), sv, ALU.mult, [pk(bv), "sgt"], ["cxpad"])
            else:
                c = ci - 16
                CP(ev_eng(), bgt[:, c, tb * 512:(tb + 1) * 512], PS[b][:], [pk(b)], ["bgt"])

        chunks = []
        for c in range(4):
            chunks += [[(c * 128, 128)], [(512 + c * 128, 128)]]
        for c in range(4):
            chunks += [[(1536 + c * 128, 128)], [(2048 + c * 128, 128)]]
        for c in range(4):
            chunks += [[(1024 + c * 128, 128)]]
        proj_fm(ein_d, 8, chunks, srcH, kH, epi, lead=3)
        diag_step(200)
        for c in range(4):
            S.alias([("diag", c, k) for k in range(31)], ("diag", c))
        S.mark('e_inproj')

        for c in range(4):
            for k in range(3):
                TS("dve", d3[:, c * 3 + k, :], ident_b, convbT[:, c, k:k + 1], None, ALU.mult, None,
                   ["cbf", "convbT"], ["d3"])

        def win(buf, c, tb, LPx, k):
            if nseq == 1:
                return buf[:, c, k + tb * 512: k + (tb + 1) * 512]
            s0 = tb * 2
            return buf[:, c, s0 * LPx:(s0 + 2) * LPx].rearrange("p (s x) -> p s x", s=2)[:, :, k:k + L]

        for cp in range(2):
            for cc in range(2):
                c = cp * 2 + cc
                for tb in range(2):
                    sl = slice(tb * 512, (tb + 1) * 512)
                    b = S.ps()
                    for k in range(31):
                        MM(psview(b), diag[:, c * 31 + k, :], win(apad, c, tb, LP, k), k == 0, k == 30,
                           [("diag", c), "apad"], [pk(b)])
                    ACT(ac[:, c, sl], PS[b][:], AF.Identity, [pk(b), "evec"], [("ac", tb)], bias=evec[:, 0, c:c + 1])
                    b2 = S.ps()
                    for k in range(3):
                        MM(psview(b2), d3[:, c * 3 + k, :], win(cxpad, c, tb, LC, k), k == 0, k == 2,
                           ["d3", "cxpad"], [pk(b2)])
                    TT("dve", U[:, 4 + c, sl], PS[b2][:], bgt[:, c, sl], ALU.mult, [pk(b2), "bgt"], [("U", tb)])
            bg_force(2)
        S.mark('e_conv')
        for tb in range(2):
            sl = slice(tb * 512, (tb + 1) * 512)
            sq = H[:, 0:4, sl]
            ACT(sq, ac[:, :, sl], AF.Square, [("ac", tb)], kH(tb))
            b1 = S.ps()
            for c in range(4):
                MM(PS[b1][:], ones_f[:], ac[:, c, sl], c == 0, c == 3, [("ac", tb), "ones_f"], [pk(b1)])
            b2 = S.ps()
            for c in range(4):
                MM(PS[b2][:], ones_b, H[:, c, sl], c == 0, c == 3, kH(tb) + ["cbf"], [pk(b2)])
            mean = RS[tb]
            TS("dve", mean[:], PS[b1][:], 1.0 / 512, None, ALU.mult, None, [pk(b1)], [("RS", tb)])
            TT("dve", RS2[:], mean[:], mean[:], ALU.mult, [("RS", tb)], ["RS2"])
            STT("dve", RS2[:], PS[b2][:], 1.0 / 512, RS2[:], ALU.mult, ALU.subtract, [pk(b2), "RS2"], ["RS2"])
            ACT(RS2[:], RS2[:], AF.Sqrt, ["RS2", "epsc"], ["RS2"], bias=epsc[:, 0:1], scale=1.0)
            RCP(RS2[:], RS2[:], ["RS2"], ["RS2"])
            TT("dve", ac[:, :, sl], ac[:, :, sl], mean[:].unsqueeze(1).to_broadcast([128, 4, 512]), ALU.subtract,
               [("ac", tb), ("RS", tb)], [("ac", tb)])
            TT("dve", ac[:, :, sl], ac[:, :, sl], RS2[:].unsqueeze(1).to_broadcast([128, 4, 512]), ALU.mult,
               [("ac", tb), "RS2"], [("ac", tb)])
            for c in range(4):
                ACT(U[:, c, sl], ac[:, c, sl], AF.Silu, [("ac", tb), "evec"], [("U", tb)],
                    bias=evec[:, 2, c:c + 1], scale=evec[:, 1, c:c + 1])

        def srcU(kc, tb):
            return U[:, kc, tb * 512:(tb + 1) * 512]

        for q in range(4):
            S.alias([("diag", c) for c in range(4)], ("Y", q))
        if half_is0[0]:
            bg_ensure(6)
            bg_every[0] = 2
        proj_fm(eout_d, 8, [[(c * 128, 128)] for c in range(8)], srcU, lambda tb: [("U", tb)], epi_scaled(0, 0, cond))
        S.mark('e_outproj')
        pass


    ATT_SCALE = 64 ** -0.5

    def odd_layer(half, cond):
        nseq = 4 if half == 0 else 1
        tps = 8 // nseq
        sample = (half == 1)
        S.barrier()
        QaT = BIG[:, 0:4096].rearrange("p (c x) -> p c x", c=4)
        KaT2 = BIG[:, 4096:6144].rearrange("p (c x) -> p c x", c=2)
        Va1 = BIG[:, 6144:7200].rearrange("p (t g e) -> p t g e", t=8, g=2)
        QmT = BIG[:, 7200:11296].rearrange("p (c x) -> p c x", c=4)
        KmT = BIG[:, 11296:15392].rearrange("p (c x) -> p c x", c=4)
        Kmtok = BIG[:, 15392:19488].rearrange("p (t x) -> p t x", t=8)
        Vm1 = BIG[:, 19488:23712].rearrange("p (t h e) -> p t h e", t=8, h=8)
        SG = BIG[:, 23712:27808].rearrange("p (t x) -> p t x", t=8)
        G = BIG[:, 27808:28320].bitcast(F32).rearrange("p (t x) -> p t x", t=8)
        KcT = BIG[:, 28320:29344].rearrange("p (g x) -> p g x", g=2)
        Vc1 = BIG[:, 29344:29872].rearrange("p (k g e) -> p k g e", k=4, g=2)
        Vpp = [BIG[:, 29872 + i * 528:29872 + (i + 1) * 528].rearrange("p (h e) -> p h e", h=8) for i in range(2)]
        PmT = [BIG[:, 30928 + i * 512:30928 + (i + 1) * 512].rearrange("p (h x) -> p h x", h=4) for i in range(2)]
        Yf = Y[:, :, :].rearrange("p a b -> p (a b)")
        ropeA = Yf[:, 0:512]
        ropeB = Yf[:, 512:1024]
        KVst = Yf[:, 1024:3072].rearrange("p (t x) -> p t x", t=8)
        cosT = Yf[:, 3072:4096]
        sinT = Yf[:, 4096:5120]
        concat = Y
        Hf = H[:, :, :].rearrange("p a b -> p (a b)")
        PTc = [XIN[i][:, :].bitcast(BF16).rearrange("p (k x) -> p k x", k=4) for i in range(2)]
        _ptb = [RS[0][:, :].bitcast(BF16), RS[1][:, :].bitcast(BF16), RS2[:, :].bitcast(BF16), BIG[:, 31952:32720]]
        PTbj = [_ptb[j // 2][:, (j % 2) * 384:(j % 2 + 1) * 384].rearrange("p (r x) -> p r x", r=3) for j in range(8)]

        if sample:
            DMA("sp", cosT, cosT_d, "const2", (), ["cosT"])
            DMA("sp", sinT, sinT_d, "const2", (), ["sinT"])
        MS("dve", Va1[:, :, :, 64:65], 1.0, ["Va1"])
        MS("dve", Vm1[:, :, :, 64:65], 1.0, ["Vm1"])

        chunks = []
        kinds = []
        for c in range(4):
            chunks.append([(c * 128, 128)]); kinds.append(("qa", c))
        for g in range(2):
            chunks.append([(512 + g * 64, 64), (512 + g * 64, 64)]); kinds.append(("ka", g))
        if not sample:
            pass
        for c in range(4):
            chunks.append([(768 + c * 128, 128)]); kinds.append(("qm", c))
        for c in range(4):
            chunks.append([(1280 + c * 128, 128)]); kinds.append(("km", c))
        hold = {}

        def epi(ci, tb, b):
            kind, c = kinds[ci]
            sl = slice(tb * 512, (tb + 1) * 512)
            if kind in ("qa", "ka"):
                dst = QaT[:, c, sl] if kind == "qa" else KaT2[:, c, sl]
                dkey = ("QaT", tb) if kind == "qa" else ("KaT2", tb)
                if sample:
                    CP("act", ropeA, PS[b][:], [pk(b)], ["ropeA"])
                    bsw = S.ps()
                    MM(PS[bsw][:], cmat[:, 3, :], ropeA, True, True, ["ropeA", "cmat"], [pk(bsw)])
                    TT("dve", ropeB, PS[bsw][:], sinT[:, sl], ALU.mult, [pk(bsw), "sinT"], ["ropeB"])
                    TT("dve", ropeA, ropeA, cosT[:, sl], ALU.mult, ["ropeA", "cosT"], ["ropeA"])
                    TT("dve", dst, ropeA, ropeB, ALU.add, ["ropeA", "ropeB"], [dkey])
                else:
                    CP(ev_eng(), dst, PS[b][:], [pk(b)], [dkey])
            elif kind == "qm":
                CP(ev_eng(), QmT[:, c, sl], PS[b][:], [pk(b)], [("QmT", tb)])
            else:
                ACT(KmT[:, c, sl], PS[b][:], AF.Copy, [pk(b)], [("KmT", tb)], scale=0.125)

        proj_fm(oin_d, 8, chunks, srcH, kH, epi, lead=3)
        S.mark('o_inproj_fm')
        _ck(1)

        def proj_tm(c0, ncols, epi_t):
            view, wkey = load_slab(oin_d, 8, [(0, c0, ncols)])
            for t in range(8):
                b = S.ps()
                for kc in range(8):
                    MM(PS[b][:, 0:ncols], H[:, kc, t * 128:(t + 1) * 128], view[:, kc, :], kc == 0, kc == 7,
                       [wkey] + kH(t // 4), [pk(b)])
                epi_t(t, b)

        def epi_kv(t, b):
            import os
            if not sample:
                kvm = '0'
                if kvm != '2':
                    CP("act", KVst[:, t, :], PS[b][:, 0:256], [pk(b)], [("KVst", t)])
                s_, qt = t // 2, t % 2
                for g_ in (range(2) if kvm != '1' else ()):
                    DMA("sp", nk_d[s_, g_, qt * 128:(qt + 1) * 128, :], KVst[:, t, g_ * 64:(g_ + 1) * 64], "out", [("KVst", t)], [])
                    DMA("sp", nv_d[s_, g_, qt * 128:(qt + 1) * 128, :], KVst[:, t, 128 + g_ * 64:128 + (g_ + 1) * 64], "out", [("KVst", t)], [])
            CP("dve", Va1[:, t, :, 0:64], PS[b][:, 128:256].rearrange("p (g d) -> p g d", g=2), [pk(b)], ["Va1"])

        proj_tm(512, 256, epi_kv)
        _ck(1.1)

        def epi_km(t, b):
            ACT(Kmtok[:, t, :], PS[b][:], AF.Copy, [pk(b)], ["Kmtok"], scale=0.125)

        proj_tm(1280, 512, epi_km)
        _ck(1.2)

        def epi_vm(t, b):
            CP("dve", Vm1[:, t, :, 0:64], PS[b][:].rearrange("p (h d) -> p h d", h=8), [pk(b)], ["Vm1"])

        proj_tm(1792, 512, epi_vm)
        _ck(1.3)

        def epi_om(t, b):
            ACT(SG[:, t, :], PS[b][:], AF.Sigmoid, [pk(b)], ["SG"])
            TT("dve", SG[:, t, :], SG[:, t, :], hng, ALU.mult, ["SG", "rowb"], ["SG"])

        proj_tm(2304, 512, epi_om)
        _ck(1.4)

        def epi_g(t, b):
            TT("dve", G[:, t, :], PS[b][:, 0:32], gateb, ALU.add, [pk(b), "rowb"], ["G"])

        proj_tm(2816, 32, epi_g)
        S.mark('o_inproj_tm')
        _ck(2)

        if sample:
            kst = XIN[0][:, :].rearrange("p (k g r d) -> p k g r d", k=4, g=2, r=2)
            vst = XIN[1][:, 0:512].rearrange("p (k g d) -> p k g d", k=4, g=2)
            for r_ in range(2):
                for g in range(2):
                    DMA("sp", kst[:, :, g, r_, :], ck_d[g].rearrange("(k p) d -> p k d", p=128), "xin0", (), [("XIN", 0)])
            for g in range(2):
                DMA("sp", vst[:, :, g, :], cv_d[g].rearrange("(k p) d -> p k d", p=128), "xin1", (), [("XIN", 1)])
            for g in range(2):
                b = S.ps()
                for kb in range(4):
                    TR(PS[b][:, kb * 128:(kb + 1) * 128], kst[:, kb, g].rearrange("p r d -> p (r d)"), ident,
                       [("XIN", 0), "cmat"], [pk(b)])
                CP("dve", KcT[:, g, :], PS[b][:], [pk(b)], ["KcT"])
            MS("dve", Vc1[:, :, :, 64:65], 1.0, ["Vc1"])
            CP("dve", Vc1[:, :, :, 0:64], vst, [("XIN", 1)], ["Vc1"])

        _ck(3)
        LFn = SM[:, :, 0, :]
        BCn = SM[:, :, 1, :]
        LA = SM[:, :, 2, :]
        Aex = SM[:, :, 3, :]
        Bex = SM[:, :, 4, :]
        EBT = SM[:, :, 5, :]
        BTn = SM[:, :, 6, :]
        ACT(SM[:, :, 0, 0:8], G[:, :, 8:16], AF.Exp, ["G"], ["SM0"], scale=-1.0)
        ACT(SM[:, :, 0, 8:16], G[:, :, 24:32], AF.Exp, ["G"], ["SM0"], scale=-1.0)
        ACT(LFn, LFn, AF.Ln, ["SM0", "ones_f"], ["SM0"], bias=ones_f[:, 0:1])
        _ck(3.1)
        bg_ = S.ps()
        gv = PS[bg_][:, 0:256].rearrange("p (t x) -> p t x", t=8)
        for t in range(8):
            MM(gv[:, t, 0:8], triU, SM[:, t, 0, 0:8], True, True, ["cmat", "SM0"], [pk(bg_)])
            MM(gv[:, t, 8:16], triL, SM[:, t, 0, 8:16], True, True, ["cmat", "SM0"], [pk(bg_)])
            MM(gv[:, t, 16:32], ones_f[:], SM[:, t, 0, 0:16], True, True, ["ones_f", "SM0"], [pk(bg_)])
        _ck(3.2)
        CP("dve", BCn, gv[:, :, 0:16], [pk(bg_)], ["SM1"])
        CP("dve", BTn, gv[:, :, 16:32], [pk(bg_)], ["SM6"])
        ACT(EBT, gv[:, :, 16:32], AF.Exp, [pk(bg_)], ["SM5"], scale=-1.0)
        _ck(3.3)
        TT("dve", SM[:, :, 2, 0:8], G[:, :, 0:8], SM[:, :, 1, 0:8], ALU.add, ["G", "SM1"], ["SM2"])
        TT("dve", SM[:, :, 2, 8:16], G[:, :, 16:24], SM[:, :, 1, 8:16], ALU.add, ["G", "SM1"], ["SM2"])
        _ck(3.4)
        ACT(Aex, LA, AF.Exp, ["SM2"], ["SM3"])
        _ck(3.5)
        ACT(Bex, BCn, AF.Exp, ["SM1"], ["SM4"], scale=-1.0)
        ACT(SM[:, :, 7, :], BCn, AF.Exp, ["SM1"], ["SM7"])

        _ck(4)
        if not sample:
            bts = [S.ps(), S.ps()]
            for t in range(8):
                TR(PS[bts[t // 4]][0:16, (t % 4) * 128:(t % 4 + 1) * 128], SM[:, t, 2, :], ident, ["SM2", "cmat"], [pk(bts[t // 4])])
            mxT = sml[0:16, 0, :]

            def red(hf):
                src = PS[bts[hf]][0:16, :].rearrange("p (t x) -> p t x", t=4)
                dst = sml[0:16, 0, hf * 4:(hf + 1) * 4]
                S.add("dve", lambda e: e.tensor_reduce(out=dst, in_=src, axis=AX.X, op=ALU.max), [pk(bts[hf])], ["sml"])
            red(0)
            red(1)
            Dg = DgT[0:16, :].rearrange("p (t j) -> p t j", t=8)
            TT("dve", Dg, mxT.unsqueeze(2).to_broadcast([16, 8, 16]),
               cmat[0:16, 0, 0:16].unsqueeze(1).to_broadcast([16, 8, 16]), ALU.mult, ["sml", "cmat"], ["Dg"])
            bm_ = S.ps()
            MM(PS[bm_][:, 0:128], ones_f[0:16, :], Dg.rearrange("p t j -> p (t j)"), True, True, ["Dg", "ones_f"], [pk(bm_)])
            CP("dve", MXB[:, :, :].rearrange("p a b -> p (a b)"), PS[bm_][:, 0:128], [pk(bm_)], ["MXB"])

        _ck(5)
        S.barrier()

        def mlstm_gen():
            ACCd = [[0, 1], [4, 5]]
            ROTd = [[2, 3], [6, 7]]
            PmTd = [[XIN[d_][:, :].bitcast(BF16)[:, hg_ * 512:(hg_ + 1) * 512].rearrange("p (h x) -> p h x", h=4)
                     for hg_ in range(2)] for d_ in range(2)]
            Ctmp_d = [Ctmp, RS2[:, 0:260].rearrange("p (c e) -> p c e", c=4)]
            Ctmp2_d = [CoutS[:, 0], RS[0][:, 0:260].rearrange("p (c e) -> p c e", c=4)]
            KmTz = [Hf[:, 0:4096].rearrange("p (c x) -> p c x", c=4), Hf[:, 4096:8192].rearrange("p (c x) -> p c x", c=4)]
            MS("dve", Hf[64:128, 0:4096], 0.0, ["KmTz"])
            MS("dve", Hf[0:64, 4096:8192], 0.0, ["KmTz"])
            CP("dve", KmTz[0][0:64], KmT[0:64], [("KmT", 0), ("KmT", 1)], ["KmTz"])
            CP("act", KmTz[1][64:128], KmT[64:128], [("KmT", 0), ("KmT", 1)], ["KmTz"])
            written = set()
            bufi = [0]
            mfin = mrec[:, 0, :]
            em = mrec[:, 1, :]
            mt = mrec[:, 2, :]
            Cout = CoutS

            for s_ in range(nseq):
                t0 = s_ * tps
                if sample:
                    DMA("sp", Cst[:, 0, :, 0:64], scf_d.rearrange("(c p) d e -> (p d) c e", p=2), "const2", (), [("Cst", 0)])
                    DMA("sp", Cst[:, 1, :, 0:64], scb_d.rearrange("(c p) d e -> (p d) c e", p=2), "const2", (), [("Cst", 1)])
                    ACT(sml[:, 1, :].rearrange("p (d c) -> p d c", d=2), snm[:, :, 2:4].rearrange("p c d -> p d c"), AF.Exp,
                        ["snm"], ["sml1"])
                    emi = sml[:, 1, :].rearrange("p (d c) -> p d c", d=2)
                    for d_ in range(2):
                        TT("dve", Cst[:, d_, :, 0:64], Cst[:, d_, :, 0:64], emi[:, d_, :].unsqueeze(2).to_broadcast([128, 4, 64]),
                           ALU.mult, [("Cst", d_), "sml1"], [("Cst", d_)])
                        TT("dve", Cst[:, d_, :, 64], snm[:, :, d_], emi[:, d_, :], ALU.mult, ["snm", "sml1"], [("Cst", d_)])
                else:
                    for d_ in range(2):
                        MS("dve", Cst[:, d_], 0.0, [("Cst", d_)])
                for d_ in range(2):
                    CP("act", Cz[0:64, d_, 0:8:2, 0:65], Cst[0:64, d_], [("Cst", d_)], [("Cz", d_)])
                    CP("act", Cz[64:128, d_, 1:8:2, 0:65], Cst[64:128, d_], [("Cst", d_)], [("Cz", d_)])

                def unit(i, d_, t0=t0, s_=s_):
                    if True:
                        t = t0 + (i if d_ == 0 else tps - 1 - i)
                        tsl = slice(t * 128, (t + 1) * 128)
                        bi = d_
                        vpp = Vpp[bi]
                        ACC = ACCd[d_]
                        ROT = ROTd[d_]
                        TT("pool", vpp[:, :, 0:65], Vm1[:, t, :, 0:65], SM[:, t, 3, d_ * 8:(d_ + 1) * 8].unsqueeze(2).to_broadcast([128, 8, 65]),
                           ALU.mult, ["Vm1", "SM3"], [("Vpp", bi)])
                        yield
                        for hg in range(2):
                            bS = S.ps_from(ROT)
                            for hh in range(4):
                                h = hg * 4 + hh
                                c, p0 = h // 2, (h % 2) * 64
                                MM(PS[bS][:, hh * 128:(hh + 1) * 128], KmTz[h % 2][:, c, tsl], QmT[:, c, tsl],
                                   True, True, ["KmTz", ("QmT", t // 4)], [pk(bS)])
                            yield
                            pm = PmTd[d_][hg]
                            TT("dve", pm[:, :, :].rearrange("p h x -> p (h x)"), PS[bS][:],
                               mask4[:, d_].rearrange("p h x -> p (h x)"), ALU.mult, [pk(bS), "mask4"], [("PmT", d_, hg)])
                            yield
                            bA = ACC[hg]
                            for hh in range(4):
                                h = hg * 4 + hh
                                c, p0 = h // 2, (h % 2) * 64
                                MM(PS[bA][:, hh * 65:(hh + 1) * 65], pm[:, hh, :], vpp[:, h, 0:65], True, False,
                                   [("PmT", d_, hg), ("Vpp", bi)], [pk(bA)])
                                MM(PS[bA][:, hh * 65:(hh + 1) * 65], QmT[:, c, tsl], Cz[:, d_, h, 0:65],
                                   False, True, [("QmT", t // 4), ("Cz", d_)], [pk(bA)])
                            yield
                            acc = PS[bA][:, 0:260].rearrange("p (h e) -> p h e", h=4)
                            bsl = SM[:, t, 4, d_ * 8 + hg * 4:d_ * 8 + hg * 4 + 4]
                            den = sml[:, 2 + 2 * d_ + hg, 0:4]
                            rr = sml[:, 2 + 2 * d_ + hg, 4:8]
                            binv = SM[:, t, 7, d_ * 8 + hg * 4:d_ * 8 + hg * 4 + 4]
                            ACT(den, acc[:, :, 64], AF.Abs, [pk(bA)], [("den", d_, hg)])
                            TT("dve", den, den, binv, ALU.max, [("den", d_, hg), "SM7"], [("den", d_, hg)])
                            RCP(rr, den, [("den", d_, hg)], [("den", d_, hg)])
                            dst = concat[:, t, 512 + hg * 256:512 + (hg + 1) * 256].rearrange("p (h e) -> p h e", h=4)
                            if (t, hg) not in written:
                                written.add((t, hg))
                                TT("dve", dst, acc[:, :, 0:64], rr.unsqueeze(2).to_broadcast([128, 4, 64]), ALU.mult,
                                   [pk(bA), ("den", d_, hg)], [("hm", t)])
                            else:
                                tmp = Ctmp_d[d_][:, :, 0:64]
                                TT("dve", tmp, acc[:, :, 0:64], rr.unsqueeze(2).to_broadcast([128, 4, 64]), ALU.mult,
                                   [pk(bA), ("den", d_, hg)], [("Ctmp", d_)])
                                TT("pool", dst, dst, tmp, ALU.add, [("Ctmp", d_), ("hm", t)], [("hm", t)])
                            yield
                        last = (i == tps - 1)
                        if not (last and sample):
                            bE = S.ps_from(ROT)
                            bO = S.ps_from(ROT)
                            for c in range(4):
                                MM(PS[bE][:, c * 65:(c + 1) * 65], Kmtok[:, t, c * 128:(c + 1) * 128], vpp[:, 2 * c, 0:65], True, True,
                                   ["Kmtok", ("Vpp", bi)], [pk(bE)])
                                MM(PS[bO][:, c * 65:(c + 1) * 65], Kmtok[:, t, c * 128:(c + 1) * 128], vpp[:, 2 * c + 1, 0:65], True, True,
                                   ["Kmtok", ("Vpp", bi)], [pk(bO)])
                            yield
                            for par, bb in ((0, bE), (1, bO)):
                                pr = slice(par * 64, (par + 1) * 64)
                                dC = PS[bb][pr, 0:260].rearrange("p (c e) -> p c e", c=4)
                                ctk = ("Ctmp2", d_, par)
                                TT("dve", Ctmp2_d[d_][pr], dC, Cst[pr, d_], ALU.add, [pk(bb), ("Cst", d_)], [ctk])
                                ebt = SM[pr, t, 5, d_ * 8 + par:d_ * 8 + 8:2]
                                TT("pool", Cst[pr, d_], Ctmp2_d[d_][pr], ebt.unsqueeze(2).to_broadcast([64, 4, 65]), ALU.mult,
                                   [ctk, "SM5"], [("Cst", d_)])
                            CP("act", Cz[0:64, d_, 0:8:2, 0:65], Cst[0:64, d_], [("Cst", d_)], [("Cz", d_)])
                            CP("act", Cz[64:128, d_, 1:8:2, 0:65], Cst[64:128, d_], [("Cst", d_)], [("Cz", d_)])
                        yield
                        if last and not sample:
                            order = list(range(t0, t0 + tps)) if d_ == 0 else list(range(t0 + tps - 1, t0 - 1, -1))
                            cs = slice(d_ * 8, (d_ + 1) * 8)
                            mk = ("mfin", d_)
                            for n_, tt in enumerate(order):
                                if n_ == 0:
                                    TS("dve", mt[:, cs], MXB[:, tt, cs], 0.0, None, ALU.max, None, ["MXB"], [mk])
                                else:
                                    TT("dve", mt[:, cs], mfin[:, cs], MXB[:, tt, cs], ALU.max, ["MXB", mk], [mk])
                                TT("dve", mfin[:, cs], mt[:, cs], SM[:, tt, 6, cs], ALU.subtract, [mk, "SM6"], [mk])
                            ACT(em[:, cs], mfin[:, cs], AF.Exp, [mk], [mk], scale=-1.0)
                            md = nmf_d if d_ == 0 else nmb_d
                            DMA("sp", md[s_:s_ + 1, :], mfin[0:1, cs], "out", [mk], [])
                            for par in range(2):
                                pr = slice(par * 64, (par + 1) * 64)
                                emb = em[pr, d_ * 8 + par:d_ * 8 + 8:2]
                                TT("dve", Cout[pr, 1 + d_], Cst[pr, d_], emb.unsqueeze(2).to_broadcast([64, 4, 65]), ALU.mult,
                                   [("Cst", d_), mk], [("Cout", d_)])
                            cd = ncf_d if d_ == 0 else ncb_d
                            nd = nnf_d if d_ == 0 else nnb_d
                            DMA("sp", cd[s_].rearrange("(c p) d e -> (p d) c e", p=2), Cout[:, 1 + d_, :, 0:64], "out",
                                [("Cout", d_)], [])
                            DMA("sp", nd[s_].rearrange("(c p) d -> (p d) c", p=2), Cout[:, 1 + d_, :, 64], "out",
                                [("Cout", d_)], [], slow=True)

                def chain(d_):
                    for i_ in range(tps):
                        yield from unit(i_, d_)
                cs_ = [chain(0), chain(1)]
                while cs_:
                    for c_ in list(cs_):
                        try:
                            next(c_)
                        except StopIteration:
                            cs_.remove(c_)
            yield

        def attn_gen():
            ACC = [0, 1]
            ROT = [2, 3, 4, 5, 6, 7]
            S.mark('o_mlstm')
            i1 = wri[0]
            wri[0] = (wri[0] + 1) % 3
            kzk = ("WR", i1)
            KaTz = [WR[i1][:, p_ * 2048:(p_ + 1) * 2048].rearrange("p (g x) -> p g x", g=2) for p_ in range(2)]
            MS("dve", KaTz[0][64:128], 0.0, [kzk])
            MS("dve", KaTz[1][0:64], 0.0, [kzk])
            CP("dve", KaTz[0][0:64], KaT2[0:64], [("KaT2", 0), ("KaT2", 1)], [kzk])
            CP("act", KaTz[1][64:128], KaT2[64:128], [("KaT2", 0), ("KaT2", 1)], [kzk])
            if sample:
                i2 = wri[0]
                wri[0] = (wri[0] + 1) % 3
                kck = ("WR", i2)
                KcTz = [WR[i2][:, p_ * 1024:(p_ + 1) * 1024].rearrange("p (g x) -> p g x", g=2) for p_ in range(2)]
                MS("dve", KcTz[0][64:128], 0.0, [kck])
                MS("dve", KcTz[1][0:64], 0.0, [kck])
                CP("dve", KcTz[0][0:64], KcT[0:64], ["KcT"], [kck])
                CP("act", KcTz[1][64:128], KcT[64:128], ["KcT"], [kck])
            for s_ in range(nseq):
                if not sample:
                    for hg in range(2):
                        for hh in range(4):
                            h = hg * 4 + hh
                            g, c, p0 = h // 4, h // 2, (h % 2) * 64
                            bS = S.ps_from(ROT)
                            for kb in range(2):
                                MM(PS[bS][:, kb * 256:(kb + 1) * 256], KaTz[h % 2][:, g, (2 * s_ + kb) * 128:(2 * s_ + kb + 1) * 128],
                                   QaT[:, c, s_ * 256:(s_ + 1) * 256], True, True,
                                   [kzk, ("QaT", s_ // 2)], [pk(bS)])
                            pt = PTc[h % 2][:, 0:2, 0:256]
                            ACT(pt, PS[bS][:].rearrange("p (k x) -> p k x", k=2), AF.Exp, [pk(bS)], [("PTc", h % 2)], scale=ATT_SCALE)
                            for qt in range(2):
                                for kb in range(2):
                                    MM(PS[ACC[qt]][:, hh * 65:(hh + 1) * 65], pt[:, kb, qt * 128:(qt + 1) * 128],
                                       Va1[:, 2 * s_ + kb, g, 0:65], kb == 0, kb == 1, [("PTc", h % 2), "Va1"], [pk(ACC[qt])])
                        for qt in range(2):
                            acc = PS[ACC[qt]][:, 0:260].rearrange("p (h e) -> p h e", h=4)
                            den = sml[:, 4 + qt, 0:4]
                            TT("dve", den, acc[:, :, 64], esink[:, hg * 4:(hg + 1) * 4], ALU.add, [pk(ACC[qt]), "rowb"], [("aden", qt)])
                            RCP(den, den, [("aden", qt)], [("aden", qt)])
                            dst = concat[:, 2 * s_ + qt, hg * 256:(hg + 1) * 256].rearrange("p (h e) -> p h e", h=4)
                            TT("dve", dst, acc[:, :, 0:64], den.unsqueeze(2).to_broadcast([128, 4, 64]), ALU.mult,
                               [pk(ACC[qt]), ("aden", qt)], [("att", 2 * s_ + qt)])
                        yield
                else:
                    Qown = RS2[:, :].bitcast(BF16).rearrange("p (c x) -> p c x", c=4)
                    TS("dve", Qown, QaT[:, :, QS[0]], selv[:, 0:1], None, ALU.mult, None, [("QaT", 0), "selv"], ["Qown"])
                    for q in range(1, 4):
                        STT("dve", Qown, QaT[:, :, QS[q]], selv[:, q:q + 1], Qown, ALU.mult, ALU.add,
                            [("QaT", q // 2), "selv", "Qown"], ["Qown"])
                    mkb = [RS[0][:, :].bitcast(BF16), RS[1][:, :].bitcast(BF16)]
                    DMA("pool", mkb[0], mown_d[:, 0:1024], "const3", (), ["mkb"])
                    DMA("pool", mkb[1], mown_d[:, 1024:2048], "const3", (), ["mkb"])
                    PTl = XIN[0][:, :].bitcast(BF16).rearrange("p (j x) -> p j x", j=8)
                    PTo = XIN[1][:, :].bitcast(BF16)[:, 0:1024].rearrange("p (j x) -> p j x", j=4)
                    for hg in range(2):
                        for hh in range(4):
                            h = hg * 4 + hh
                            g, c = h // 4, h // 2
                            for pr_ in range(2):
                                bS = S.ps_from(ROT)
                                for kk in range(2):
                                    kb = 2 * pr_ + kk
                                    MM(PS[bS][:, kk * 256:(kk + 1) * 256], KcTz[h % 2][:, g, kb * 128:(kb + 1) * 128], Qown[:, c, :],
                                       True, True, [kck, "Qown"], [pk(bS)])
                                ACT(PTo[:, 2 * pr_:2 * pr_ + 2, :], PS[bS][:].rearrange("p (k x) -> p k x", k=2), AF.Exp,
                                    [pk(bS)], ["PTo"], scale=ATT_SCALE)
                            for pr_ in range(4):
                                bS = S.ps_from(ROT)
                                for kk in range(2):
                                    j = 2 * pr_ + kk
                                    MM(PS[bS][:, kk * 256:(kk + 1) * 256], KaTz[h % 2][:, g, j * 128:(j + 1) * 128], Qown[:, c, :],
                                       True, True, [kzk, "Qown"], [pk(bS)])
                                ACT(PTl[:, 2 * pr_:2 * pr_ + 2, :], PS[bS][:].rearrange("p (k x) -> p k x", k=2), AF.Exp,
                                    [pk(bS)], [("PTl", pr_ // 2)], scale=ATT_SCALE)
                            for m_ in range(2):
                                pv = PTl[:, 4 * m_:4 * m_ + 4, :].rearrange("p j x -> p (j x)")
                                TT("dve", pv, pv, mkb[m_], ALU.mult, [("PTl", m_), "mkb"], [("PTl", m_)])
                            for a_ in range(2):
                                mms = []
                                for kb in range(4):
                                    mms.append((PTo[:, kb, a_ * 128:(a_ + 1) * 128], Vc1[:, kb, g, 0:65], ["PTo", "Vc1"]))
                                for j in range(8):
                                    mms.append((PTl[:, j, a_ * 128:(a_ + 1) * 128], Va1[:, j, g, 0:65], [("PTl", j // 4), "Va1"]))
                                for n_, (l_, r_, k_) in enumerate(mms):
                                    MM(PS[ACC[a_]][:, hh * 65:(hh + 1) * 65], l_, r_, n_ == 0, n_ == len(mms) - 1, k_, [pk(ACC[a_])])
                        for a_ in range(2):
                            acc = PS[ACC[a_]][:, 0:260].rearrange("p (h e) -> p h e", h=4)
                            den = sml[:, 4 + a_, 0:4]
                            TT("dve", den, acc[:, :, 64], esink[:, hg * 4:(hg + 1) * 4], ALU.add, [pk(ACC[a_]), "rowb"], [("aden", a_)])
                            RCP(den, den, [("aden", a_)], [("aden", a_)])
                            dst = concat[:, a_, hg * 256:(hg + 1) * 256].rearrange("p (h e) -> p h e", h=4)
                            TT("dve", dst, acc[:, :, 0:64], den.unsqueeze(2).to_broadcast([128, 4, 64]), ALU.mult,
                               [pk(ACC[a_]), ("aden", a_)], [("att", a_)])
                        yield
            yield

        for _ in mlstm_gen():
            pass
        S.barrier()
        for _ in attn_gen():
            pass
        _ck(7)
        S.barrier()

        S.mark('o_attn')
        ntl = 2 if sample else 8
        if sample:
            for buf, lo, kf in ((concat, 512, lambda t_: [("hm", t_)]), (SG, 0, lambda t_: ["SG"])):
                d0 = buf[:, 0:2, lo:lo + 512]
                TS("dve", d0, d0, selv[:, 0:1], None, ALU.mult, None, kf(0) + kf(1) + ["selv"], kf(0) + kf(1))
                for q in range(1, 4):
                    STT("dve", d0, buf[:, 2 * q:2 * q + 2, lo:lo + 512], selv[:, q:q + 1], d0, ALU.mult, ALU.add,
                        kf(2 * q) + kf(2 * q + 1) + kf(0) + kf(1) + ["selv"], kf(0) + kf(1))
        sqs = BIG[:, 7200:15392].bitcast(F32)
        ssq = MXB[:, :, :].rearrange("p a b -> p (a b)")[:, 0:64]
        for t in range(ntl):
            ACT(sqs[:, t * 512:(t + 1) * 512], concat[:, t, 512:1024], AF.Square, [("hm", t)], ["sqs"])
        ssq = ssq[:, 0:ntl * 8]
        sqv = sqs[:, 0:ntl * 512].rearrange("p (a b) -> p a b", b=64)
        S.add("dve", lambda e: e.tensor_reduce(out=ssq, in_=sqv, axis=AX.X, op=ALU.add), ["sqs"], ["ssq"])
        ACT(ssq, ssq, AF.Sqrt, ["ssq", "epsc"], ["ssq"], bias=epsc[:, 0:1], scale=1.0 / 64)
        RCP(ssq, ssq, ["ssq"], ["ssq"])
        for t in range(ntl):
            hmv = concat[:, t, 512:1024].rearrange("p (h e) -> p h e", h=8)
            TT("dve", hmv, hmv, ssq[:, t * 8:(t + 1) * 8].unsqueeze(2).to_broadcast([128, 8, 64]), ALU.mult,
               [("hm", t), "ssq"], [("hm", t)])
            TT("dve", concat[:, t, 512:1024], concat[:, t, 512:1024], SG[:, t, :], ALU.mult, [("hm", t), "SG"], [("hm", t)])
        for t in range(ntl):
            for hf in range(2):
                b = S.ps()
                for j in range(4):
                    kc = hf * 4 + j
                    TR(PS[b][:, j * 128:(j + 1) * 128], concat[:, t, kc * 128:(kc + 1) * 128], ident,
                       ["cmat", ("hm", t), ("att", t)], [pk(b)])
                CP(ev_eng(), H[:, hf * 4:hf * 4 + 4, t * 128:(t + 1) * 128], PS[b][:].rearrange("p (a b) -> p a b", b=128),
                   [pk(b)], kH(t // 4))
        S.barrier()

        if sample:
            for buf, keyf in ((X, xkeys),):
                TS("dve", buf[:, :, QS[0]], buf[:, :, QS[0]], selv[:, 0:1], None, ALU.mult, None, keyf(0) + ["selv"], keyf(0))
                for q in range(1, 4):
                    STT("dve", buf[:, :, QS[0]], buf[:, :, QS[q]], selv[:, q:q + 1], buf[:, :, QS[0]], ALU.mult, ALU.add,
                        keyf(q) + keyf(0) + ["selv"], keyf(0))
            WIN[0], WIN[1] = 1, 256

        def epiO(ci, tb, b):
            CP(ev_eng(), Y[:, ci, wsl(tb)], PS[b][:, 0:WIN[1]], [pk(b)], kY(tb))

        proj_fm(oout_d, 8, [[(c * 128, 128)] for c in range(8)], srcH, kH, epiO)
        S.mark('o_outproj')

    for half in range(2):
        cond = half
        nseq = 4 if half == 0 else 1
        load_x(xp_d if half == 0 else xs_d)
        S.mark('x_loaded')
        if half == 0:
            g0 = mod_steps(0)
            for _ in range(4):
                next(g0)
            bg[0] = g0
            bgdone[0] = 4
            bg_every[0] = 5
        S.barrier()
        junction(None, (0, 0), cond)
        even_layer(cond, nseq)
        if half == 0:
            bg_flush()
            half_is0[0] = False
            bg[0] = mod_steps(1)
            bgdone[0] = 0
            bg_every[0] = 2
        junction((0, 0, True), (0, 1), cond)
        mlp(0, cond)
        bg_flush()
        S.nps = 8
        junction((0, 1, True), (1, 0), cond)
        odd_layer(half, cond)
        own = (half == 1)
        junction((1, 0, False), (1, 1), cond, qs=(0,) if own else (0, 1, 2, 3))
        mlp(1, cond)
        junction((1, 1, True), None, cond, qs=(0,) if own else (0, 1, 2, 3))
        store_x(yp_d if half == 0 else ys_d, 2 if own else 8)
        WIN[0], WIN[1] = 2, 512
        S.barrier()

    S.emit(nc, ["out"])
    es.close()
    return nc


_CACHE = {}


def _consts():
    idn = np.eye(128, dtype=np.float32)
    u = np.arange(128)
    triU = (u[:, None] <= u[None, :]).astype(np.float32)
    triL = (u[:, None] >= u[None, :]).astype(np.float32)
    perm = np.zeros((128, 128), dtype=np.float32)
    for d in range(128):
        if d % 64 < 32:
            perm[d + 32, d] = -1.0
        else:
            perm[d - 32, d] = 1.0
    cmat = np.stack([idn, triU, triL, perm], axis=1).astype(np.float32)
    t = np.arange(1024)
    row = (t // 64).astype(np.float32)
    col = (t % 64).astype(np.float32)
    inv = (10000.0 ** (-np.arange(16, dtype=np.float32) / 16)).astype(np.float32)
    ang = np.concatenate([row[:, None] * inv, col[:, None] * inv], axis=-1).astype(np.float32)
    cos = np.cos(ang).astype(np.float32).T
    sin = np.sin(ang).astype(np.float32).T
    cosT = np.concatenate([cos, cos, cos, cos], axis=0)
    sinT = np.concatenate([sin, sin, sin, sin], axis=0)
    return cmat, np.ascontiguousarray(cosT), np.ascontiguousarray(sinT)


def _maskown(q):
    k = np.arange(128)[:, None]
    qq = np.arange(128)[None, :]
    m = np.zeros((128, 8, 2, 128), dtype=np.float32)
    for a in range(2):
        i = 2 * q + a
        for j in range(8):
            if j == i:
                m[:, j, a, :] = 1.0
            elif j == i - 1:
                m[:, j, a, :] = (k >= qq)
            elif j == i + 1:
                m[:, j, a, :] = (k <= qq)
    return np.ascontiguousarray(m.reshape(128, 2048))


def fm(v):
    v = np.asarray(v, dtype=np.float32)
    lead = v.shape[:-1]
    n = v.shape[-1] // 128
    r = v.reshape(lead + (n, 128))
    r = np.moveaxis(r, -1, 0)
    return np.ascontiguousarray(r)


def kernel(**inp):
    f = lambda k: np.ascontiguousarray(np.asarray(inp[k], dtype=np.float32))
    if "nc" not in _CACHE:
        _CACHE["nc"] = build_program()
    nc = _CACHE["nc"]
    cmat, cosT, sinT = _consts()
    x_prompt, x_sample = f("x_prompt"), f("x_sample")
    c, c_ctx = f("c"), f("c_ctx")
    mod_w, mod_b, norm_g = f("mod_w"), f("mod_b"), f("norm_g")
    modbT = np.ascontiguousarray(fm(mod_b))
    normgT = np.ascontiguousarray(fm(norm_g))
    convaT = np.ascontiguousarray(f("conv_a_w")[0].T.reshape(4, 128, 31).transpose(1, 0, 2))
    convbT = np.ascontiguousarray(f("conv_b_w")[0].T.reshape(4, 128, 3).transpose(1, 0, 2))
    evec = np.ascontiguousarray(np.stack([fm(f("conv_a_b")[0]), fm(f("ln_a_g")[0]), fm(f("ln_a_b")[0])], axis=1))
    rowv = np.concatenate([f("attn_sink")[0], f("gate_b")[0].reshape(-1), f("hnorm_g")[0]])[None, :].astype(np.float32)
    in_maps = []
    for core in range(8):
        b = core // 4
        condT = np.ascontiguousarray(np.stack([fm(c_ctx), fm(c[b])], axis=-1))

        def exp4(v):
            return np.repeat(v.reshape(4, 2).T, 64, axis=0)

        def n4(v):
            return v.reshape(4, 2, 64).transpose(1, 2, 0).reshape(128, 4)

        snm = np.stack([n4(f("state_n_fwd")[b, 0]), n4(f("state_n_bwd")[b, 0]),
                        exp4(f("state_m_fwd")[b, 0]), exp4(f("state_m_bwd")[b, 0])], axis=-1)
        in_maps.append({
            "xp": np.ascontiguousarray(x_prompt[core * 4:(core + 1) * 4].reshape(1024, 1024)),
            "xs": x_sample[b],
            "condT": condT, "mod_w": mod_w, "modbT": modbT, "normgT": normgT,
            "mlp_w1": f("mlp_w1"), "mlp_w2": f("mlp_w2"),
            "even_in_w": f("even_in_w")[0], "even_out_w": f("even_out_w")[0],
            "odd_in_w": f("odd_in_w")[0], "odd_out_w": f("odd_out_w")[0],
            "convaT": convaT, "convbT": convbT, "evec": evec, "rowv": rowv,
            "cache_k": f("cache_k")[b, 0], "cache_v": f("cache_v")[b, 0],
            "st_c_f": f("state_c_fwd")[b, 0], "st_c_b": f("state_c_bwd")[b, 0],
            "st_nm": np.ascontiguousarray(snm.astype(np.float32)),
            "cosT": cosT, "sinT": sinT, "cmat": cmat,
            "selv": np.ascontiguousarray(np.tile(np.eye(4, dtype=np.float32)[core % 4][None, :], (128, 1))),
            "maskown": _maskown(core % 4),
        })
    res = run_bass_kernel_spmd(nc, in_maps[:NCORES], core_ids=list(range(NCORES)))
    R = list(res.results) + [res.results[0]] * (8 - NCORES)
    yp = np.concatenate([R[i]["yp"].reshape(4, 256, 1024) for i in range(8)], axis=0)
    ys = np.stack([np.concatenate([R[4 * b_ + q_]["ys"] for q_ in range(4)], axis=0) for b_ in range(2)], axis=0)
    cat = lambda k: np.concatenate([R[i][k] for i in range(8)], axis=0)
    nk = cat("nk")[:, None]
    nv = cat("nv")[:, None]
    return (yp, ys, nk, nv, cat("ncf")[:, None], cat("nnf")[:, None], cat("nmf")[:, None],
            cat("ncb")[:, None], cat("nnb")[:, None], cat("nmb")[:, None])
```
